# Optimizing a Trainium2 kernel written in Bass

```python
import jax
import jax.numpy as jnp
from jax import lax
import numpy as np

D_MODEL = 2048
BATCH = 8
SEQ = 4096
DEPTH = 1

D_MIX = D_MODEL
V_HEAD_DIM = 128
MLA_WIDTH = D_MIX // 2
MLA_HEADS = MLA_WIDTH // V_HEAD_DIM
QK_NOPE_DIM = 128
QK_ROPE_DIM = 64
QK_HEAD_DIM = QK_NOPE_DIM + QK_ROPE_DIM
Q_LORA_RANK = 512
KV_LORA_RANK = 512
ROPE_THETA = 10000.0
Q_BLOCK = 128
SSD_WIDTH = D_MIX - MLA_WIDTH
SSD_HEAD_DIM = 64
SSD_HEADS = SSD_WIDTH // SSD_HEAD_DIM
SSD_GROUPS = 2
SSD_HEADS_PER_GROUP = SSD_HEADS // SSD_GROUPS
SSD_STATE = 128
SSD_CONV = 4
SSD_CHUNK = 128
SSD_CONV_DIM = SSD_WIDTH + 2 * SSD_GROUPS * SSD_STATE
D_FF = -(-8 * D_MODEL // (3 * 256)) * 256
IN_SIZES = (Q_LORA_RANK, KV_LORA_RANK, QK_ROPE_DIM, SSD_WIDTH, SSD_CONV_DIM, SSD_HEADS)
D_IN = Q_LORA_RANK + KV_LORA_RANK + QK_ROPE_DIM + SSD_WIDTH + SSD_CONV_DIM + SSD_HEADS
EPS = 1e-6

kernel_name = "hymba_mla_ssd_sandwich_layer"


def rms_norm(t, w):
    tf = t.astype(jnp.float32)
    y = tf * lax.rsqrt(jnp.mean(tf * tf, axis=-1, keepdims=True) + EPS)
    return (y * w.astype(jnp.float32)).astype(t.dtype)


def split_cols(t, sizes):
    offsets = np.cumsum(np.array(sizes))[:-1].tolist()
    return jnp.split(t, offsets, axis=-1)


def rope_tables(positions):
    inv_freq = ROPE_THETA ** (-jnp.arange(0, QK_ROPE_DIM, 2, dtype=jnp.float32) / QK_ROPE_DIM)
    ang = positions.astype(jnp.float32)[..., None] * inv_freq
    return jnp.cos(ang), jnp.sin(ang)


def apply_rope(t, cos, sin):
    t1, t2 = jnp.split(t.astype(jnp.float32), 2, axis=-1)
    return jnp.concatenate([t1 * cos - t2 * sin, t2 * cos + t1 * sin], axis=-1).astype(t.dtype)


def mla_group(c_q, c_kv, k_rope, cos, sin, q_norm_w, w_uq, kv_norm_w, w_ukv):
    b, s, _ = c_q.shape
    q = (rms_norm(c_q, q_norm_w) @ w_uq).reshape(b, s, MLA_HEADS, QK_HEAD_DIM)
    q_nope, q_rope = q[..., :QK_NOPE_DIM], q[..., QK_NOPE_DIM:]
    kv = (rms_norm(c_kv, kv_norm_w) @ w_ukv).reshape(b, s, MLA_HEADS, QK_NOPE_DIM + V_HEAD_DIM)
    k_nope, v = kv[..., :QK_NOPE_DIM], kv[..., QK_NOPE_DIM:]
    q_rope = apply_rope(q_rope, cos[:, :, None, :], sin[:, :, None, :])
    k_rope = apply_rope(k_rope, cos, sin)
    scale = QK_HEAD_DIM ** -0.5
    n_blk = s // Q_BLOCK
    qn_blocks = jnp.moveaxis(q_nope.reshape(b, n_blk, Q_BLOCK, MLA_HEADS, QK_NOPE_DIM), 1, 0)
    qr_blocks = jnp.moveaxis(q_rope.reshape(b, n_blk, Q_BLOCK, MLA_HEADS, QK_ROPE_DIM), 1, 0)
    key_idx = jnp.arange(s)

    def attend(args):
        blk, qn, qr = args
        sc = (jnp.einsum('bqhd,bkhd->bhqk', qn, k_nope, preferred_element_type=jnp.float32)
              + jnp.einsum('bqhr,bkr->bhqk', qr, k_rope, preferred_element_type=jnp.float32)) * scale
        q_idx = blk * Q_BLOCK + jnp.arange(Q_BLOCK)
        causal = key_idx[None, :] <= q_idx[:, None]
        p = jax.nn.softmax(jnp.where(causal, sc, -jnp.inf), axis=-1).astype(v.dtype)
        return jnp.einsum('bhqk,bkhd->bqhd', p, v)

    o = lax.map(attend, (jnp.arange(n_blk), qn_blocks, qr_blocks))
    return jnp.moveaxis(o, 0, 1).reshape(b, s, MLA_HEADS * V_HEAD_DIM)


def causal_depthwise_conv(t, w, bias):
    y = lax.conv_general_dilated(t, w[:, None, :], window_strides=(1,), padding=[(SSD_CONV - 1, 0)],
                                 dimension_numbers=('NWC', 'WIO', 'NWC'), feature_group_count=t.shape[-1])
    return y + bias


def ssd_group(z, xbc, dt_raw, conv_w, conv_b, dt_bias, a_log, d_skip, norm_w):
    b, s, _ = z.shape
    G, E, P, N, T = SSD_GROUPS, SSD_HEADS_PER_GROUP, SSD_HEAD_DIM, SSD_STATE, SSD_CHUNK
    c = s // T
    xbc = jax.nn.silu(causal_depthwise_conv(xbc, conv_w, conv_b))
    xs, bm, cm = split_cols(xbc, (SSD_WIDTH, G * N, G * N))
    dt = jax.nn.softplus(dt_raw.astype(jnp.float32) + dt_bias.astype(jnp.float32))
    a_neg = -jnp.exp(a_log.astype(jnp.float32)).reshape(G, E)
    x = xs.astype(jnp.float32).reshape(b, c, T, G, E, P)
    dt_c = dt.reshape(b, c, T, G, E)
    bc = bm.astype(jnp.float32).reshape(b, c, T, G, N)
    cc = cm.astype(jnp.float32).reshape(b, c, T, G, N)
    xdt = x * dt_c[..., None]
    a_cum = jnp.cumsum(jnp.transpose(dt_c * a_neg, (0, 1, 3, 4, 2)), axis=-1)
    seg = a_cum[..., :, None] - a_cum[..., None, :]
    tri = jnp.tril(jnp.ones((T, T), dtype=bool))
    decay = jnp.exp(jnp.where(tri, seg, -jnp.inf))
    cb = jnp.einsum('bclgn,bcsgn->bcgls', cc, bc)
    y_diag = jnp.einsum('bcgels,bcsgep->bclgep', cb[:, :, :, None] * decay, xdt)
    decay_states = jnp.exp(a_cum[..., -1:] - a_cum)
    states = jnp.einsum('bcsgn,bcsgep->bcgepn', bc, xdt * jnp.transpose(decay_states, (0, 1, 4, 2, 3))[..., None])
    chunk_decay = jnp.exp(a_cum[..., -1])

    def step(h, inp):
        st, dec = inp
        return h * dec[..., None, None] + st, h

    h0 = jnp.zeros((b, G, E, P, N), jnp.float32)
    _, prev = lax.scan(step, h0, (jnp.moveaxis(states, 1, 0), jnp.moveaxis(chunk_decay, 1, 0)))
    prev = jnp.moveaxis(prev, 0, 1)
    y_off = jnp.einsum('bclgn,bcgepn->bclgep', cc, prev) * jnp.transpose(jnp.exp(a_cum), (0, 1, 4, 2, 3))[..., None]
    y = (y_diag + y_off + x * d_skip.astype(jnp.float32).reshape(G, E, 1)).reshape(b, s, SSD_WIDTH)
    g = (y * jax.nn.silu(z.astype(jnp.float32))).reshape(b, s, G, SSD_WIDTH // G)
    g = g * lax.rsqrt(jnp.mean(g * g, axis=-1, keepdims=True) + EPS)
    return (g.reshape(b, s, SSD_WIDTH) * norm_w.astype(jnp.float32)).astype(z.dtype)


def setup_inputs(seed: int = 0) -> dict:
    key = jax.random.key(seed)
    ks = jax.random.split(key, 24)
    f32 = jnp.float32
    L = DEPTH

    def dense(k, fan_in, fan_out):
        return jax.random.normal(k, (L, fan_in, fan_out), f32) * fan_in ** -0.5

    def gain(k, n):
        return 1.0 + 0.02 * jax.random.normal(k, (L, n), f32)

    x = jax.random.normal(ks[0], (BATCH, SEQ, D_MODEL), f32)
    positions = jnp.arange(SEQ, dtype=jnp.int32)[None, :] + jax.random.randint(ks[1], (BATCH, 1), 0, SEQ, dtype=jnp.int32)
    dt0 = jnp.exp(jax.random.uniform(ks[2], (L, SSD_HEADS), f32, float(np.log(1e-3)), float(np.log(1e-1))))
    dt_bias = dt0 + jnp.log(-jnp.expm1(-dt0))
    a_log = jnp.log(jax.random.uniform(ks[3], (L, SSD_HEADS), f32, 1.0, 16.0))
    return {
        "x": x,
        "positions": positions,
        "w_in": dense(ks[4], D_MODEL, D_IN),
        "q_norm_w": gain(ks[5], Q_LORA_RANK),
        "w_uq": dense(ks[6], Q_LORA_RANK, MLA_HEADS * QK_HEAD_DIM),
        "kv_norm_w": gain(ks[7], KV_LORA_RANK),
        "w_ukv": dense(ks[8], KV_LORA_RANK, MLA_HEADS * (QK_NOPE_DIM + V_HEAD_DIM)),
        "conv_w": jax.random.normal(ks[9], (L, SSD_CONV, SSD_CONV_DIM), f32) * SSD_CONV ** -0.5,
        "conv_b": 0.02 * jax.random.normal(ks[10], (L, SSD_CONV_DIM), f32),
        "dt_bias": dt_bias,
        "a_log": a_log,
        "d_skip": gain(ks[11], SSD_HEADS),
        "ssd_norm_w": gain(ks[12], SSD_WIDTH),
        "attn_out_norm_w": gain(ks[13], MLA_WIDTH),
        "w_out": dense(ks[14], D_MIX, D_MODEL),
        "pre_mix_norm_w": gain(ks[15], D_MODEL),
        "post_mix_norm_w": gain(ks[16], D_MODEL),
        "pre_ffn_norm_w": gain(ks[17], D_MODEL),
        "post_ffn_norm_w": gain(ks[18], D_MODEL),
        "w_gate": dense(ks[19], D_MODEL, D_FF),
        "w_up": dense(ks[20], D_MODEL, D_FF),
        "w_down": dense(ks[21], D_FF, D_MODEL),
    }


def reference(x, positions, w_in, q_norm_w, w_uq, kv_norm_w, w_ukv, conv_w, conv_b, dt_bias, a_log,
              d_skip, ssd_norm_w, attn_out_norm_w, w_out, pre_mix_norm_w, post_mix_norm_w,
              pre_ffn_norm_w, post_ffn_norm_w, w_gate, w_up, w_down):
    cos, sin = rope_tables(positions)
    h = x
    for l in range(DEPTH):
        u = rms_norm(h, pre_mix_norm_w[l])
        c_q, c_kv, k_rope, z, xbc, dt_raw = split_cols(u @ w_in[l], IN_SIZES)
        attn = rms_norm(mla_group(c_q, c_kv, k_rope, cos, sin, q_norm_w[l], w_uq[l], kv_norm_w[l], w_ukv[l]),
                        attn_out_norm_w[l])
        ssm = ssd_group(z, xbc, dt_raw, conv_w[l], conv_b[l], dt_bias[l], a_log[l], d_skip[l], ssd_norm_w[l])
        mix = jnp.concatenate([attn, ssm], axis=-1) @ w_out[l]
        h = h + rms_norm(mix, post_mix_norm_w[l])
        v = rms_norm(h, pre_ffn_norm_w[l])
        ffn = (jax.nn.silu(v @ w_gate[l]) * (v @ w_up[l])) @ w_down[l]
        h = h + rms_norm(ffn, post_ffn_norm_w[l])
    return h
```

```python
import math
import os
from contextlib import ExitStack

import numpy as np
import ml_dtypes

import concourse.bass as bass
import concourse.mybir as mybir
from concourse.bass_utils import run_bass_kernel_spmd

F32 = mybir.dt.float32
BF16 = mybir.dt.bfloat16
I32 = mybir.dt.int32
AF = mybir.ActivationFunctionType
ALU = mybir.AluOpType
AX = mybir.AxisListType

D = 2048
DIN = 3664
DFF = 5632
NH = 8
EPS = 1e-6
TT = 512

ENGS = ("pe", "act", "dve", "pool", "sp")


class Buf:
    __slots__ = ("name", "w", "r", "dsem", "excl")

    def __init__(self, name, excl=False):
        self.name = name
        self.excl = excl
        self.w = {}
        self.r = {}
        self.dsem = None


class Prog:
    def __init__(self):
        self.q = {e: [] for e in ENGS}
        self.cnt = []
        self.waited = {e: {} for e in ENGS}
        self.esem = {e: self.new_sem() for e in ENGS}

    def new_sem(self):
        self.cnt.append(0)
        return len(self.cnt) - 1

    def _wait(self, eng, k, v):
        if eng == "pe" and k == self.esem["pe"]:
            return
        if self.waited[eng].get(k, 0) >= v:
            return
        self.waited[eng][k] = v
        self.q[eng].append(("wait", k, v))

    def deps(self, eng, reads=(), writes=(), pwrites=()):
        for b in reads:
            for k, v in b.w.items():
                self._wait(eng, k, v)
            if b.excl:
                for k, v in b.r.items():
                    if k != self.esem[eng]:
                        self._wait(eng, k, v)
        for b in writes:
            for k, v in b.w.items():
                self._wait(eng, k, v)
            for k, v in b.r.items():
                self._wait(eng, k, v)
        for b in pwrites:
            for k, v in b.r.items():
                self._wait(eng, k, v)

    def emit(self, eng, fn):
        self.q[eng].append(("op", fn, None, 0))

    def _register(self, ev, reads, writes, pwrites):
        k, v = ev
        for b in reads:
            b.r[k] = max(b.r.get(k, 0), v)
        for b in writes:
            b.w = {k: v}
        for b in pwrites:
            b.w[k] = max(b.w.get(k, 0), v)

    def commit(self, eng, fn, reads=(), writes=(), pwrites=()):
        k = self.esem[eng]
        self.cnt[k] += 1
        self.q[eng].append(("op", fn, k, 1))
        self._register((k, self.cnt[k]), reads, writes, pwrites)

    def op(self, eng, fn, reads=(), writes=(), pwrites=()):
        self.deps(eng, reads, writes, pwrites)
        self.commit(eng, fn, reads, writes, pwrites)

    def dma(self, eng, out, in_, sb, reads=(), writes=(), pwrites=()):
        self.deps(eng, reads, writes, pwrites)
        if sb.dsem is None:
            sb.dsem = self.new_sem()
        k = sb.dsem
        self.cnt[k] += 16
        self.q[eng].append(("op", lambda e: e.dma_start(out=out, in_=in_), k, 16))
        self._register((k, self.cnt[k]), reads, writes, pwrites)

    def barrier(self):
        for e in ENGS:
            for k in range(len(self.cnt)):
                if self.cnt[k] > 0:
                    self._wait(e, k, self.cnt[k])

    def replay(self, eng, e, sems):
        for it in self.q[eng]:
            if it[0] == "wait":
                e.wait_ge(sems[it[1]], it[2])
            else:
                ins = it[1](e)
                if it[2] is not None:
                    ins.then_inc(sems[it[2]], it[3])


class SB:
    def __init__(self, big, nbytes):
        self.big = big
        self.nbytes = nbytes
        self.off = 0

    def mark(self):
        return self.off

    def reset(self, m):
        self.off = m

    def alloc(self, shape, dtype):
        n = int(np.prod(shape))
        esz = 4 if dtype in (F32, I32) else 2
        nb = n * esz
        self.off = (self.off + 63) // 64 * 64
        assert self.off + nb <= self.nbytes, f"SBUF overflow {self.off + nb} > {self.nbytes}"
        ap = self.big[:, self.off // 2:(self.off + nb) // 2]
        self.off += nb
        if esz == 4:
            ap = ap.bitcast(dtype)
        if len(shape) == 2:
            ap = ap.rearrange("p (a b) -> p a b", b=shape[1])
        elif len(shape) == 3:
            ap = ap.rearrange("p (a b c) -> p a b c", b=shape[1], c=shape[2])
        return ap


def build_program(S, debug=False):
    assert S % TT == 0
    NT = S // TT
    NCH = S // 128
    nc = bass.Bass("TRN2", target_bir_lowering=False)

    def din(name, shape, dt=F32):
        return nc.dram_tensor(name, list(shape), dt, kind="ExternalInput").ap()

    skind = "ExternalOutput" if debug else "Internal"

    def dscr(name, shape, dt):
        return nc.dram_tensor(name, list(shape), dt, kind=skind).ap()

    x = din("x", [S, D])
    posb = din("posb", [64, S], I32)
    w_in = din("w_in", [D, DIN])
    w_uq = din("w_uq", [512, 1536])
    w_ukv = din("w_ukv", [512, 2048])
    w_out = din("w_out", [D, D])
    w_gate = din("w_gate", [D, DFF])
    w_up = din("w_up", [D, DFF])
    w_down = din("w_down", [DFF, D])
    c_ident = din("c_ident", [128, 128], BF16)
    c_tri = din("c_tri", [128, 128])
    c_negm = din("c_negm", [128, 128])
    c_cmask = din("c_cmask", [128, 4 * 512], BF16)
    c_rope = din("c_rope", [64, 2])
    g_pre = din("g_pre", [128, D])
    g_preffn = din("g_preffn", [128, D])
    g_postmix = din("g_postmix", [128, D])
    g_postffn = din("g_postffn", [128, D])
    g_q = din("g_q", [128, 4])
    g_kv = din("g_kv", [128, 4])
    g_attn = din("g_attn", [128, 8])
    g_ssd = din("g_ssd", [128, 1024])
    conv_wb = din("conv_wb", [128, 12 * 5])
    ssd_small = din("ssd_small", [128, 48])
    out = nc.dram_tensor("out", [S, D], F32, kind="ExternalOutput").ap()

    qn_s = dscr("qn_s", [NH, 128, S], BF16)
    qr_s = dscr("qr_s", [NH, 64, S], BF16)
    kn_s = dscr("kn_s", [NH, 128, S], BF16)
    kr_s = dscr("kr_s", [64, S], BF16)
    v_s = dscr("v_s", [S, 1024], BF16)
    zs_s = dscr("zs_s", [S, 1024], BF16)
    xbc_s = dscr("xbc_s", [1536, S], BF16)
    dt_s = dscr("dt_s", [S, 16], F32)
    cos_s = dscr("cos_s", [64, S], F32)
    sin_s = dscr("sin_s", [64, S], F32)
    ssmT_s = dscr("ssmT_s", [1024, S], BF16)
    attnT_s = dscr("attnT_s", [1024, S], BF16)

    P = Prog()
    SBYTES = 207 * 1024

    with ExitStack() as es:
        big = es.enter_context(nc.sbuf_tensor("big", [128, SBYTES // 2], BF16))
        sb = SB(big, SBYTES)
        psum = [es.enter_context(nc.psum_tensor(f"ps{i}", [128, 1024], BF16) if i < 2 else nc.psum_tensor(f"ps{i}", [128, 512], F32))
                for i in range(8)]
        PS = [Buf(f"ps{i}", excl=True) for i in range(8)]
        psf = [(p[:].bitcast(F32) if i < 2 else p[:]) for i, p in enumerate(psum)]
        psb = [(p[:] if i < 2 else p[:].bitcast(BF16)) for i, p in enumerate(psum)]

        ident = sb.alloc([128], BF16)
        ones = sb.alloc([128], BF16)
        B_const = Buf("const")
        P.dma("sp", ident, c_ident, B_const, writes=[B_const])
        P.op("dve", lambda e: e.memset(ones, 1.0), pwrites=[B_const])
        m_persist = sb.mark()

        class WStream:
            def __init__(self, nslots, shape, loads):
                self.nslots = nslots
                self.slots = [sb.alloc(shape, BF16) for _ in range(nslots)]
                self.bufs = [Buf(f"wslot{i}") for i in range(nslots)]
                self.loads = loads
                self.issued = 0
                self.cons = 0
                for _ in range(nslots):
                    self._issue()

            def _issue(self):
                i = self.issued
                if i >= len(self.loads):
                    return
                s = i % self.nslots
                for dst_fn, src in self.loads[i]:
                    P.dma("pool", dst_fn(self.slots[s]), src, self.bufs[s], pwrites=[self.bufs[s]])
                self.issued += 1

            def get(self):
                s = self.cons % self.nslots
                return self.slots[s], self.bufs[s]

            def release(self):
                self.cons += 1
                if int(os.environ.get('KNOREFILL', '0')):
                    return
                self._issue()

        def phase0():
            m = sb.mark()
            crope = sb.alloc([2], F32)
            Bc = Buf("crope")
            P.dma("sp", crope[:64], c_rope, Bc, writes=[Bc])
            C1 = 6.28125
            C2 = 2 * math.pi - C1
            PI_IN = 3.1415925
            CW = 1024

            def wrap(t, msk, Bt, Bm):
                P.op("dve", lambda e: e.tensor_scalar(out=msk[:64], in0=t[:64], scalar1=-math.pi, scalar2=None, op0=ALU.is_lt),
                     reads=[Bt], writes=[Bm])
                P.op("dve", lambda e: e.scalar_tensor_tensor(out=t[:64], in0=msk[:64], scalar=2 * math.pi, in1=t[:64],
                                                             op0=ALU.mult, op1=ALU.add), reads=[Bm, Bt], writes=[Bt])
                P.op("dve", lambda e: e.tensor_scalar(out=msk[:64], in0=t[:64], scalar1=math.pi, scalar2=None, op0=ALU.is_gt),
                     reads=[Bt], writes=[Bm])
                P.op("dve", lambda e: e.scalar_tensor_tensor(out=t[:64], in0=msk[:64], scalar=-2 * math.pi, in1=t[:64],
                                                             op0=ALU.mult, op1=ALU.add), reads=[Bm, Bt], writes=[Bt])
                P.op("dve", lambda e: e.tensor_scalar(out=t[:64], in0=t[:64], scalar1=PI_IN, scalar2=-PI_IN,
                                                      op0=ALU.min, op1=ALU.max), reads=[Bt], writes=[Bt])

            for c0 in range(0, S, CW):
                cw = min(CW, S - c0)
                pi_t = sb.alloc([cw], I32)
                ang = sb.alloc([cw], F32)
                t1 = sb.alloc([cw], F32)
                t2 = sb.alloc([cw], F32)
                msk = sb.alloc([cw], F32)
                ni = sb.alloc([cw], I32)
                Bp, Ba, B1, B2, Bm, Bn = Buf("pi"), Buf("ang"), Buf("t1"), Buf("t2"), Buf("msk"), Buf("ni")
                P.dma("sp", pi_t[:64], posb[:, c0:c0 + cw], Bp, writes=[Bp])
                P.op("dve", lambda e, a=ang, p=pi_t: e.tensor_copy(out=a[:64], in_=p[:64]), reads=[Bp], writes=[Ba])
                P.op("dve", lambda e, a=ang: e.tensor_scalar(out=a[:64], in0=a[:64], scalar1=crope[:64, 0:1], scalar2=None,
                                                             op0=ALU.mult), reads=[Ba, Bc], writes=[Ba])
                P.op("dve", lambda e, a=ang, t=t1: e.tensor_scalar(out=t[:64], in0=a[:64], scalar1=1.0 / (2 * math.pi), scalar2=0.5,
                                                                   op0=ALU.mult, op1=ALU.add), reads=[Ba], writes=[B1])
                P.op("dve", lambda e, t=t1, n=ni: e.tensor_copy(out=n[:64], in_=t[:64]), reads=[B1], writes=[Bn])
                P.op("dve", lambda e, t=t2, n=ni: e.tensor_copy(out=t[:64], in_=n[:64]), reads=[Bn], writes=[B2])
                P.op("dve", lambda e, a=ang, t=t1, n=t2: e.scalar_tensor_tensor(out=t[:64], in0=n[:64], scalar=-C1, in1=a[:64],
                                                                                op0=ALU.mult, op1=ALU.add), reads=[B2, Ba], writes=[B1])
                P.op("dve", lambda e, t=t1, n=t2: e.scalar_tensor_tensor(out=t[:64], in0=n[:64], scalar=-C2, in1=t[:64],
                                                                         op0=ALU.mult, op1=ALU.add), reads=[B2, B1], writes=[B1])
                wrap(t1, msk, B1, Bm)
                P.op("dve", lambda e, t=t1, u=t2: e.tensor_scalar(out=u[:64], in0=t[:64], scalar1=0.5 * math.pi, scalar2=None,
                                                                  op0=ALU.add), reads=[B1], writes=[B2])
                wrap(t2, msk, B2, Bm)
                P.op("act", lambda e, t=t1: e.activation(out=t[:64], in_=t[:64], func=AF.Sin), reads=[B1], writes=[B1])
                P.op("act", lambda e, t=t2: e.activation(out=t[:64], in_=t[:64], func=AF.Sin), reads=[B2], writes=[B2])
                P.op("dve", lambda e, t=t1: e.tensor_scalar(out=t[:64], in0=t[:64], scalar1=crope[:64, 1:2], scalar2=None,
                                                            op0=ALU.mult), reads=[B1, Bc], writes=[B1])
                P.dma("sp", sin_s[:, c0:c0 + cw], t1[:64], B1, reads=[B1])
                P.dma("sp", cos_s[:, c0:c0 + cw], t2[:64], B2, reads=[B2])
            P.barrier()
            sb.reset(m)

        phase0()

        def act_(fn, **kw):
            P.op("act", fn, **kw)

        def dve_(fn, **kw):
            P.op("dve", fn, **kw)

        def mm_group(out_ap, pairs, reads, psbuf):
            P.deps("pe", reads=reads, writes=[psbuf])
            n = len(pairs)
            for i, (l, r) in enumerate(pairs):
                fn = (lambda e, l=l, r=r, st=(i == 0), sp=(i == n - 1): e.matmul(out_ap, l, r, start=st, stop=sp))
                if i == n - 1:
                    P.commit("pe", fn, reads=reads, writes=[psbuf])
                else:
                    P.emit("pe", fn)

        class Rot:
            def __init__(self, items):
                self.items = items
                self.i = 0

            def next(self):
                it = self.items[self.i % len(self.items)]
                self.i += 1
                return it

        def rms_rstd(dst, src, n, Bdst, Bsrc):
            act_(lambda e: e.activation(out=dst, in_=src, func=AF.Sqrt, scale=1.0 / n, bias=EPS), reads=[Bsrc], writes=[Bdst])
            dve_(lambda e: e.reciprocal(out=dst, in_=dst), reads=[Bdst], writes=[Bdst])

        def norm_transpose(src_fn, gain, Bgain, uT, BuT, xs, Bxs, xn, Bxn, ss, Bss, tr_banks, sub, pre_loaded=False):
            KSUB = int(os.environ.get('KSUB', '99'))
            if src_fn is not None:
                src_fn(xs, Bxs)
            dve_(lambda e: e.memset(ss, 0.0), writes=[Bss])
            if KSUB < 1:
                return
            act_(lambda e: e.activation(out=xn, in_=xs, func=AF.Square, accum_out=ss), reads=[Bxs], writes=[Bxn, Bss])
            if KSUB < 2:
                return
            rms_rstd(ss, ss, float(D), Bss, Bss)
            if KSUB < 3:
                return
            dve_(lambda e: e.scalar_tensor_tensor(out=xn, in0=xs, scalar=ss, in1=gain, op0=ALU.mult, op1=ALU.mult),
                 reads=[Bxs, Bss, Bgain], writes=[Bxn])
            if KSUB < 4:
                return
            for half in range(2):
                if KSUB < 5 + half:
                    return
                bank = tr_banks.next()
                P.deps("pe", reads=[Bxn, B_const], writes=[PS[bank]])
                for i in range(8):
                    c = half * 8 + i
                    fn = lambda e, c=c, i=i, bank=bank: e.transpose(psb[bank][:, i * 128:(i + 1) * 128], xn[:, c * 128:(c + 1) * 128], ident)
                    if i == 7:
                        P.commit("pe", fn, reads=[Bxn, B_const], writes=[PS[bank]])
                    else:
                        P.emit("pe", fn)
                if KSUB < 7:
                    continue
                src = psb[bank].rearrange("p (c t) -> p c t", t=128)
                dst = uT[:, half * 8:(half + 1) * 8, sub * 128:(sub + 1) * 128]
                if half == 0:
                    P.op("act", lambda e, s_=src, d_=dst: e.activation(out=d_, in_=s_, func=AF.Copy), reads=[PS[bank]], pwrites=[BuT])
                else:
                    P.op("dve", lambda e, s_=src, d_=dst: e.tensor_copy(out=d_, in_=s_), reads=[PS[bank]], pwrites=[BuT])

        def phase1a():
            m = sb.mark()
            w_in_v = w_in.rearrange("(kc p) n -> p kc n", p=128)
            w_uq_v = w_uq.rearrange("(kc p) n -> p kc n", p=128)
            wuq = sb.alloc([4, 1536], BF16)
            wuqr = sb.alloc([4, 512], BF16)
            wukv = sb.alloc([4, 2048], BF16)
            Bw = Buf("w_res")
            P.dma("pool", wuq, w_uq_v, Bw, pwrites=[Bw])
            for h in range(NH):
                base = h * 192 + 128
                P.dma("pool", wuqr[:, :, h * 64:h * 64 + 32], w_uq_v[:, :, base + 32:base + 64], Bw, pwrites=[Bw])
                P.dma("pool", wuqr[:, :, h * 64 + 32:h * 64 + 64], w_uq_v[:, :, base:base + 32], Bw, pwrites=[Bw])
            P.dma("pool", wukv, w_ukv.rearrange("(kc p) n -> p kc n", p=128), Bw, pwrites=[Bw])
            wukv_h = wukv.rearrange("p kc (h c) -> p kc h c", c=256)
            gpre = sb.alloc([D], F32)
            gq = sb.alloc([4], F32)
            gkv = sb.alloc([4], F32)
            cwb = sb.alloc([12, 5], F32)
            small = sb.alloc([48], F32)
            Bg = Buf("gains")
            P.dma("sp", gpre, g_pre, Bg, pwrites=[Bg])
            P.dma("sp", gq, g_q, Bg, pwrites=[Bg])
            P.dma("sp", gkv, g_kv, Bg, pwrites=[Bg])
            P.dma("sp", cwb, conv_wb.rearrange("p (c k) -> p c k", k=5), Bg, pwrites=[Bg])
            P.dma("sp", small, ssd_small, Bg, pwrites=[Bg])

            xs = [sb.alloc([D], F32) for _ in range(2)]
            Bxs = [Buf("xs0"), Buf("xs1")]
            xn = [sb.alloc([D], BF16) for _ in range(2)]
            Bxn = [Buf("xn0"), Buf("xn1")]
            ssq = [sb.alloc([1], F32) for _ in range(2)]
            Bss = [Buf("ss0"), Buf("ss1")]
            uTs = [sb.alloc([16, TT], BF16) for _ in range(2)]
            BuTs = [Buf("uT0"), Buf("uT1")]
            craw = [sb.alloc([4, TT], F32) for _ in range(2)]
            Bcraw = [Buf("craw0"), Buf("craw1")]
            sq = [sb.alloc([TT], BF16) for _ in range(2)]
            Bsq = [Buf("sq0"), Buf("sq1")]
            rstdb = sb.alloc([TT], F32)
            Brstdb = Buf("rstdb")
            cn = [sb.alloc([4, TT], BF16) for _ in range(2)]
            Bcn = [Buf("cqn"), Buf("ckvn")]
            cst = [sb.alloc([TT], F32) for _ in range(2)]
            Bcst = Buf("cossin")
            rt = [sb.alloc([TT], F32) for _ in range(4)]
            Brt = [Buf(f"rt{i}") for i in range(4)]
            rt_rot = Rot([0, 2])
            stg = [sb.alloc([TT], BF16) for _ in range(4)]
            Bstg = [Buf(f"stg{i}") for i in range(4)]
            stg_rot = Rot(list(range(4)))
            vst = [sb.alloc([1024], BF16) for _ in range(2)]
            Bvst = [Buf("vst0"), Buf("vst1")]
            zst = sb.alloc([4, 1024], BF16)
            Bzst = [Buf(f"zst{i}") for i in range(4)]
            xraw = [sb.alloc([TT + 3], F32) for _ in range(2)]
            Bxraw = [Buf("xraw0"), Buf("xraw1")]
            halo = sb.alloc([12, 3], F32)
            Bhalo = Buf("halo")
            acc = [sb.alloc([TT], F32) for _ in range(2)]
            Bacc = [Buf("acc0"), Buf("acc1")]
            dtst = sb.alloc([4, 16], F32)
            Bdtst = Buf("dtst")
            dttmp4 = [sb.alloc([16], F32) for _ in range(4)]
            Bdttmp4 = [Buf(f"dttmp{i}") for i in range(4)]
            dve_(lambda e: e.memset(halo, 0.0), writes=[Bhalo])

            def mk_loads():
                L = []
                for _t in range(NT):
                    for c0 in (0, 256, 512, 768):
                        L.append([(lambda sl: sl, w_in_v[:, :, c0:c0 + 256])])
                    L.append([(lambda sl: sl[:, :, 0:64], w_in_v[:, :, 1024:1088]),
                              (lambda sl: sl[:, :, 64:96], w_in_v[:, :, 1056:1088]),
                              (lambda sl: sl[:, :, 96:128], w_in_v[:, :, 1024:1056])])
                    for i in range(4):
                        c0 = 1088 + 256 * i
                        L.append([(lambda sl: sl, w_in_v[:, :, c0:c0 + 256])])
                    for i in range(6):
                        c0 = 2112 + 256 * i
                        L.append([(lambda sl: sl, w_in_v[:, :, c0:c0 + 256])])
                    L.append([(lambda sl: sl[:, :, 0:16], w_in_v[:, :, 3648:3664])])
                return L
            ws = WStream(4, [16, 256], mk_loads())

            tr_banks = Rot([0, 1])
            mm_banks = Rot([2, 3, 4, 5, 6])
            SSB = 7

            def store(dst, src, Bsrc):
                P.dma("sp", dst, src, Bsrc, reads=[Bsrc])

            def rope_out(dst_dram, RA, RB):
                r0_ = rt_rot.next()
                r1_ = r0_ + 1
                dve_(lambda e: e.tensor_tensor(out=rt[r0_][:64], in0=psf[RA][:64], in1=cst[0][:64], op=ALU.mult),
                     reads=[PS[RA], Bcst], writes=[Brt[r0_]])
                dve_(lambda e: e.tensor_tensor(out=rt[r1_][:64], in0=psf[RB][:64], in1=cst[1][:64], op=ALU.mult),
                     reads=[PS[RB], Bcst], writes=[Brt[r1_]])
                si = stg_rot.next()
                dve_(lambda e: e.tensor_tensor(out=stg[si][:64], in0=rt[r0_][:64], in1=rt[r1_][:64], op=ALU.add),
                     reads=[Brt[r0_], Brt[r1_]], writes=[Bstg[si]])
                store(dst_dram, stg[si][:64], Bstg[si])

            KB = int(os.environ.get('KB', '99'))

            def latent_norm(which, gl):
                for half in range(2):
                    wt, Bwt = ws.get()
                    for cc in range(2):
                        ch = half * 2 + cc
                        bank = mm_banks.next()
                        mm_group(psf[bank], [(wt[:, kc, cc * 128:(cc + 1) * 128], uT[:, kc, :]) for kc in range(16)],
                                 [Bwt, BuT], PS[bank])
                        if KB < 1:
                            continue
                        KX = int(os.environ.get('KX', '3'))
                        if KX & 1:
                            act_(lambda e, bank=bank, ch=ch: e.activation(out=sq[ch % 2], in_=psf[bank], func=AF.Square),
                                 reads=[PS[bank]], writes=[Bsq[ch % 2]])
                        if KX & 2:
                            dve_(lambda e, bank=bank, ch=ch: e.tensor_copy(out=craw[which][:, ch, :], in_=psf[bank]),
                                 reads=[PS[bank]], pwrites=[Bcraw[which]])
                        if KB < 2:
                            continue
                        P.deps("pe", reads=[Bsq[ch % 2], B_const], writes=[PS[SSB]] if ch == 0 else [])
                        P.commit("pe", lambda e, ch=ch: e.matmul(psf[SSB], ones, sq[ch % 2], start=(ch == 0), stop=(ch == 3)),
                                 reads=[Bsq[ch % 2], B_const], writes=[PS[SSB]] if ch == 3 else [], pwrites=[PS[SSB]] if ch < 3 else [])
                    ws.release()
                if KB < 3:
                    return
                rms_rstd(rstdb, psf[SSB], 512.0, Brstdb, PS[SSB])
                if KB < 4:
                    return
                for ch in range(4):
                    dve_(lambda e, ch=ch: e.scalar_tensor_tensor(out=cn[which][:, ch, :], in0=craw[which][:, ch, :],
                                                                 scalar=gl[:, ch:ch + 1], in1=rstdb, op0=ALU.mult, op1=ALU.mult),
                         reads=[Bcraw[which], Brstdb, Bg], pwrites=[Bcn[which]])

            KST = int(os.environ.get('KSTAGE', '99'))
            def normT(t):
                for sub in range(4):
                    r0 = t * TT + sub * 128
                    b = sub % 2
                    norm_transpose(lambda xs_, Bxs_, r0=r0: P.dma("sp", xs_, x[r0:r0 + 128, :], Bxs_, writes=[Bxs_]),
                                   gpre, Bg, uTs[t % 2], BuTs[t % 2], xs[b], Bxs[b], xn[b], Bxn[b], ssq[b], Bss[b], tr_banks, sub)

            for t in range(NT if KST >= 0 else 0):
                tok = slice(t * TT, (t + 1) * TT)
                P.dma("sp", cst[0][:64], cos_s[:, tok], Bcst, writes=[Bcst])
                P.dma("sp", cst[1][:64], sin_s[:, tok], Bcst, pwrites=[Bcst])
                uT, BuT = uTs[t % 2], BuTs[t % 2]
                if t == 0:
                    normT(0)
                if KST < 1:
                    continue
                latent_norm(0, gq)
                latent_norm(1, gkv)
                for h in range(NH if KB >= 5 else 0):
                    bank = mm_banks.next()
                    mm_group(psf[bank], [(wuq[:, kc, h * 192:h * 192 + 128], cn[0][:, kc, :]) for kc in range(4)], [Bw, Bcn[0]], PS[bank])
                    si = stg_rot.next()
                    act_(lambda e, bank=bank, si=si: e.activation(out=stg[si], in_=psf[bank], func=AF.Copy), reads=[PS[bank]], writes=[Bstg[si]])
                    store(qn_s[h, :, tok], stg[si], Bstg[si])
                    RA, RB = mm_banks.next(), mm_banks.next()
                    mm_group(psf[RA][:64], [(wuq[:, kc, h * 192 + 128:h * 192 + 192], cn[0][:, kc, :]) for kc in range(4)], [Bw, Bcn[0]], PS[RA])
                    mm_group(psf[RB][:64], [(wuqr[:, kc, h * 64:h * 64 + 64], cn[0][:, kc, :]) for kc in range(4)], [Bw, Bcn[0]], PS[RB])
                    rope_out(qr_s[h, :, tok], RA, RB)
                if KST < 2:
                    continue
                for h in range(NH):
                    bank = mm_banks.next()
                    mm_group(psf[bank], [(wukv[:, kc, h * 256:h * 256 + 128], cn[1][:, kc, :]) for kc in range(4)], [Bw, Bcn[1]], PS[bank])
                    si = stg_rot.next()
                    act_(lambda e, bank=bank, si=si: e.activation(out=stg[si], in_=psf[bank], func=AF.Copy), reads=[PS[bank]], writes=[Bstg[si]])
                    store(kn_s[h, :, tok], stg[si], Bstg[si])
                for sub in range(4):
                    vb = sub % 2
                    for half in range(2):
                        bank = mm_banks.next()
                        mm_group(psf[bank].rearrange("p (h c) -> p h c", c=128),
                                 [(cn[1][:, kc, sub * 128:(sub + 1) * 128], wukv_h[:, kc, half * 4:half * 4 + 4, 128:256]) for kc in range(4)],
                                 [Bw, Bcn[1]], PS[bank])
                        if half == 0:
                            act_(lambda e, bank=bank, vb=vb: e.activation(out=vst[vb][:, 0:512], in_=psf[bank], func=AF.Copy),
                                 reads=[PS[bank]], writes=[Bvst[vb]])
                        else:
                            dve_(lambda e, bank=bank, vb=vb: e.tensor_copy(out=vst[vb][:, 512:1024], in_=psf[bank]),
                                 reads=[PS[bank]], pwrites=[Bvst[vb]])
                    r0 = t * TT + sub * 128
                    store(v_s[r0:r0 + 128, :], vst[vb], Bvst[vb])
                if t + 1 < NT:
                    normT(t + 1)
                if KST < 3:
                    continue
                wt, Bwt = ws.get()
                RA, RB = mm_banks.next(), mm_banks.next()
                mm_group(psf[RA][:64], [(wt[:, kc, 0:64], uT[:, kc, :]) for kc in range(16)], [Bwt, BuT], PS[RA])
                mm_group(psf[RB][:64], [(wt[:, kc, 64:128], uT[:, kc, :]) for kc in range(16)], [Bwt, BuT], PS[RB])
                ws.release()
                rope_out(kr_s[:, tok], RA, RB)
                if KST < 4:
                    continue
                for i in range(4):
                    wt, Bwt = ws.get()
                    for sub in range(4):
                        bank = mm_banks.next()
                        mm_group(psf[bank][:, 0:256], [(uT[:, kc, sub * 128:(sub + 1) * 128], wt[:, kc, :]) for kc in range(16)],
                                 [Bwt, BuT], PS[bank])
                        act_(lambda e, bank=bank, sub=sub, i=i: e.activation(out=zst[:, sub, i * 256:(i + 1) * 256], in_=psf[bank][:, 0:256], func=AF.Silu),
                             reads=[PS[bank]], writes=[Bzst[sub]] if i == 0 else [], pwrites=[Bzst[sub]] if i > 0 else [])
                    ws.release()
                for sub in range(4):
                    r0 = t * TT + sub * 128
                    store(zs_s[r0:r0 + 128, :], zst[:, sub, :], Bzst[sub])
                if KST < 5:
                    continue
                for i in range(6):
                    wt, Bwt = ws.get()
                    for cc in range(2):
                        c = 2 * i + cc
                        xb = c % 2
                        bank = mm_banks.next()
                        mm_group(psf[bank], [(wt[:, kc, cc * 128:(cc + 1) * 128], uT[:, kc, :]) for kc in range(16)], [Bwt, BuT], PS[bank])
                        act_(lambda e, bank=bank, xb=xb: e.activation(out=xraw[xb][:, 3:TT + 3], in_=psf[bank], func=AF.Copy),
                             reads=[PS[bank]], writes=[Bxraw[xb]])
                        dve_(lambda e, xb=xb, c=c: e.tensor_copy(out=xraw[xb][:, 0:3], in_=halo[:, c, :]), reads=[Bhalo], pwrites=[Bxraw[xb]])
                        dve_(lambda e, xb=xb, c=c: e.tensor_copy(out=halo[:, c, :], in_=xraw[xb][:, TT:TT + 3]), reads=[Bxraw[xb]], pwrites=[Bhalo])
                        dve_(lambda e, xb=xb, c=c: e.tensor_scalar(out=acc[xb], in0=xraw[xb][:, 0:TT], scalar1=cwb[:, c, 0:1], scalar2=None, op0=ALU.mult),
                             reads=[Bxraw[xb], Bg], writes=[Bacc[xb]])
                        for k in (1, 2, 3):
                            dve_(lambda e, xb=xb, c=c, k=k: e.scalar_tensor_tensor(out=acc[xb], in0=xraw[xb][:, k:TT + k], scalar=cwb[:, c, k:k + 1],
                                                                                   in1=acc[xb], op0=ALU.mult, op1=ALU.add),
                                 reads=[Bxraw[xb], Bg, Bacc[xb]], writes=[Bacc[xb]])
                        si = stg_rot.next()
                        act_(lambda e, xb=xb, c=c, si=si: e.activation(out=stg[si], in_=acc[xb], func=AF.Silu, bias=cwb[:, c, 4:5]),
                             reads=[Bacc[xb], Bg], writes=[Bstg[si]])
                        store(xbc_s[c * 128:(c + 1) * 128, tok], stg[si], Bstg[si])
                    ws.release()
                if KST < 6:
                    continue
                wt, Bwt = ws.get()
                for sub in range(4):
                    bank = mm_banks.next()
                    mm_group(psf[bank][:, 0:16], [(uT[:, kc, sub * 128:(sub + 1) * 128], wt[:, kc, 0:16]) for kc in range(16)], [Bwt, BuT], PS[bank])
                    dttmp, Bdttmp = dttmp4[sub], Bdttmp4[sub]
                    dve_(lambda e, bank=bank, dttmp=dttmp: e.tensor_tensor(out=dttmp, in0=psf[bank][:, 0:16], in1=small[:, 0:16], op=ALU.add),
                         reads=[PS[bank], Bg], writes=[Bdttmp])
                    act_(lambda e, dttmp=dttmp: e.activation(out=dttmp, in_=dttmp, func=AF.Exp), reads=[Bdttmp], writes=[Bdttmp])
                    act_(lambda e, sub=sub, dttmp=dttmp: e.activation(out=dtst[:, sub, :], in_=dttmp, func=AF.Ln, bias=1.0), reads=[Bdttmp],
                         writes=[Bdtst] if sub == 0 else [], pwrites=[Bdtst] if sub > 0 else [])
                ws.release()
                store(dt_s[tok, :].rearrange("(s p) h -> p s h", p=128), dtst, Bdtst)
            P.barrier()
            sb.reset(m)

        phase1a()

        def bcast_mid(ap2, k):
            n = ap2.shape[1]
            return ap2.unsqueeze(1).to_broadcast([128, k, n])

        def bcast_last(ap2, k):
            n = ap2.shape[1]
            return ap2.unsqueeze(2).to_broadcast([128, n, k])

        def phase1b():
            m = sb.mark()
            tri = sb.alloc([128], F32)
            negm = sb.alloc([128], F32)
            onesf = sb.alloc([128], F32)
            small = sb.alloc([48], F32)
            aneg = sb.alloc([16], F32)
            gssd = sb.alloc([1024], F32)
            Bc = Buf("c1b")
            P.dma("sp", tri, c_tri, Bc, pwrites=[Bc])
            P.dma("sp", negm, c_negm, Bc, pwrites=[Bc])
            P.dma("sp", small, ssd_small, Bc, pwrites=[Bc])
            P.dma("sp", gssd, g_ssd, Bc, pwrites=[Bc])
            dve_(lambda e: e.memset(onesf, 1.0), pwrites=[Bc])
            act_(lambda e: e.activation(out=aneg, in_=small[:, 16:32], func=AF.Exp), reads=[Bc], pwrites=[Bc])
            dve_(lambda e: e.tensor_scalar(out=aneg, in0=aneg, scalar1=-1.0, scalar2=None, op0=ALU.mult), reads=[Bc], pwrites=[Bc])
            dskip = small[:, 32:48]
            hT = sb.alloc([1024], F32)
            hTb = sb.alloc([1024], BF16)
            Bh, Bhb = Buf("hT"), Buf("hTb")
            dve_(lambda e: e.memset(hT, 0.0), writes=[Bh])
            dve_(lambda e: e.memset(hTb, 0.0), writes=[Bhb])
            xbcT = [sb.alloc([12, 128], BF16) for _ in range(3)]
            zs = [sb.alloc([1024], BF16) for _ in range(3)]
            dtt = [sb.alloc([16], F32) for _ in range(3)]
            Bin = [Buf("in0"), Buf("in1"), Buf("in2")]
            a_t = sb.alloc([16], F32); Ba = Buf("a")
            acol = sb.alloc([16], F32); Bacol = Buf("acol")
            rhsA = sb.alloc([16, 128], F32); BrhsA = Buf("rhsA")
            arow = sb.alloc([16, 128], F32); Barow = Buf("arow")
            tmp = sb.alloc([16, 128], F32); Btmp = Buf("tmp")
            cbt = sb.alloc([2, 128], F32); Bcbt = Buf("cbt")
            xtok = sb.alloc([16, 64], BF16); Bxtok = Buf("xtok")
            MT2 = [sb.alloc([16, 128], BF16) for _ in range(2)]; BMT2 = [Buf("MT0"), Buf("MT1")]
            btok2 = [sb.alloc([256], BF16) for _ in range(2)]; Bbtok2 = [Buf("btok0"), Buf("btok1")]
            xdt2 = [sb.alloc([16, 64], BF16) for _ in range(2)]; Bxdt2 = [Buf("xdt0"), Buf("xdt1")]
            xdtw2 = [sb.alloc([16, 64], BF16) for _ in range(2)]; Bxdtw2 = [Buf("xdtw0"), Buf("xdtw1")]
            sm2 = [sb.alloc([4, 16], F32) for _ in range(2)]; Bsm2 = [Buf("sm0"), Buf("sm1")]
            xD2 = [sb.alloc([16, 64], F32) for _ in range(2)]; BxD2 = [Buf("xD0"), Buf("xD1")]
            y = sb.alloc([16, 64], F32); By = Buf("y")
            ss2 = sb.alloc([2], F32); Bss2 = Buf("ss2")
            junk = sb.alloc([512], BF16); Bjunk = Buf("junk")
            ssm = sb.alloc([1024], BF16); Bssm = Buf("ssm")
            sst = [sb.alloc([8, 128], BF16) for _ in range(2)]; Bsst = [Buf("sst0"), Buf("sst1")]
            xbc_v = xbc_s.rearrange("(c p) s -> p c s", p=128)
            ssmT_v = ssmT_s.rearrange("(c p) s -> p c s", p=128)

            def load(c):
                b = c % 3
                tok = slice(c * 128, (c + 1) * 128)
                P.dma("sp", xbcT[b], xbc_v[:, :, tok], Bin[b], writes=[Bin[b]])
                P.dma("sp", zs[b], zs_s[tok, :], Bin[b], pwrites=[Bin[b]])
                P.dma("sp", dtt[b], dt_s[tok, :], Bin[b], pwrites=[Bin[b]])

            def front(c):
                p = c % 2
                X, DT, BI = xbcT[c % 3], dtt[c % 3], Bin[c % 3]
                MT, BMT, btok, Bbtok = MT2[p], BMT2[p], btok2[p], Bbtok2[p]
                xdt, Bxdt, xdtw, Bxdtw, sm, Bsm, xD, BxD = xdt2[p], Bxdt2[p], xdtw2[p], Bxdtw2[p], sm2[p], Bsm2[p], xD2[p], BxD2[p]
                dve_(lambda e: e.tensor_tensor(out=a_t, in0=DT, in1=aneg, op=ALU.mult), reads=[BI, Bc], writes=[Ba])
                mm_group(psf[2][:, 0:16], [(tri, a_t)], [Bc, Ba], PS[2])
                dve_(lambda e: e.tensor_copy(out=acol, in_=psf[2][:, 0:16]), reads=[PS[2]], writes=[Bacol])
                dve_(lambda e: e.tensor_tensor(out=rhsA, in0=bcast_mid(tri, 16), in1=bcast_last(a_t, 128), op=ALU.mult),
                     reads=[Bc, Ba], writes=[BrhsA])
                for g in range(2):
                    mm_group(psf[2][:, 128 + g * 128:128 + (g + 1) * 128], [(X[:, 8 + g, :], X[:, 10 + g, :])], [BI], PS[2])
                act_(lambda e: e.activation(out=cbt, in_=psf[2][:, 128:384], func=AF.Copy), reads=[PS[2]], writes=[Bcbt])
                for q4 in range(4):
                    bank = 7
                    mm_group(psf[bank], [(onesf, rhsA[:, q4 * 4:(q4 + 1) * 4, :])], [Bc, BrhsA], PS[bank])
                    act_(lambda e, bank=bank, q4=q4: e.activation(out=arow[:, q4 * 4:(q4 + 1) * 4, :], in_=psf[bank], func=AF.Copy),
                         reads=[PS[bank]], pwrites=[Barow])
                dve_(lambda e: e.tensor_tensor(out=tmp, in0=arow, in1=bcast_last(acol, 128), op=ALU.subtract),
                     reads=[Barow, Bacol], writes=[Btmp])
                dve_(lambda e: e.tensor_tensor(out=tmp, in0=tmp, in1=bcast_mid(negm, 16), op=ALU.add),
                     reads=[Btmp, Bc], writes=[Btmp])
                act_(lambda e: e.activation(out=tmp, in_=tmp, func=AF.Exp), reads=[Btmp], writes=[Btmp])
                for g in range(2):
                    dve_(lambda e, g=g: e.tensor_tensor(out=MT[:, g * 8:(g + 1) * 8, :], in0=tmp[:, g * 8:(g + 1) * 8, :],
                                                        in1=bcast_mid(cbt[:, g, :], 8), op=ALU.mult),
                         reads=[Btmp, Bcbt], writes=[BMT] if g == 0 else [], pwrites=[BMT] if g else [])
                P.deps("pe", reads=[BI, B_const], writes=[PS[0]])
                for j in range(8):
                    fn = lambda e, j=j: e.transpose(psb[0][:, j * 128:(j + 1) * 128], X[:, j, :], ident)
                    if j == 7:
                        P.commit("pe", fn, reads=[BI, B_const], writes=[PS[0]])
                    else:
                        P.emit("pe", fn)
                act_(lambda e: e.activation(out=xtok, in_=psb[0], func=AF.Copy), reads=[PS[0]], writes=[Bxtok])
                P.deps("pe", reads=[BI, B_const], writes=[PS[1]])
                P.emit("pe", lambda e: e.transpose(psb[1][:, 0:128], X[:, 8, :], ident))
                P.commit("pe", lambda e: e.transpose(psb[1][:, 128:256], X[:, 9, :], ident), reads=[BI, B_const], writes=[PS[1]])
                dve_(lambda e: e.tensor_copy(out=btok, in_=psb[1][:, 0:256]), reads=[PS[1]], writes=[Bbtok])
                alast = arow[:, :, 127]
                dve_(lambda e: e.tensor_tensor(out=sm[:, 3, :], in0=alast, in1=acol, op=ALU.subtract), reads=[Barow, Bacol], writes=[Bsm])
                act_(lambda e: e.activation(out=sm[:, 0, :], in_=sm[:, 3, :], func=AF.Exp), reads=[Bsm], pwrites=[Bsm])
                act_(lambda e: e.activation(out=sm[:, 1, :], in_=acol, func=AF.Exp), reads=[Bacol], pwrites=[Bsm])
                act_(lambda e: e.activation(out=sm[:, 2, :], in_=alast, func=AF.Exp), reads=[Barow], pwrites=[Bsm])
                dve_(lambda e: e.tensor_tensor(out=xdt, in0=xtok, in1=bcast_last(DT, 64), op=ALU.mult), reads=[Bxtok, BI], writes=[Bxdt])
                dve_(lambda e: e.tensor_tensor(out=xdtw, in0=xdt, in1=bcast_last(sm[:, 0, :], 64), op=ALU.mult), reads=[Bxdt, Bsm], writes=[Bxdtw])
                dve_(lambda e: e.tensor_tensor(out=xD, in0=xtok, in1=bcast_last(dskip, 64), op=ALU.mult), reads=[Bxtok, Bc], writes=[BxD])

            def back(c):
                p = c % 2
                b = c % 2
                tok = slice(c * 128, (c + 1) * 128)
                X, Z, BI = xbcT[c % 3], zs[c % 3], Bin[c % 3]
                MT, BMT, btok, Bbtok = MT2[p], BMT2[p], btok2[p], Bbtok2[p]
                xdt, Bxdt, xdtw, Bxdtw, sm, Bsm, xD, BxD = xdt2[p], Bxdt2[p], xdtw2[p], Bxdtw2[p], sm2[p], Bsm2[p], xD2[p], BxD2[p]
                for g in range(2):
                    bank = 3 + g
                    P.deps("pe", reads=[BMT, Bxdt], writes=[PS[bank]])
                    for e8 in range(8):
                        h = g * 8 + e8
                        fn = lambda e, h=h, e8=e8, bank=bank: e.matmul(psf[bank][:, e8 * 64:(e8 + 1) * 64], MT[:, h, :], xdt[:, h, :], start=True, stop=True)
                        if e8 == 7:
                            P.commit("pe", fn, reads=[BMT, Bxdt], writes=[PS[bank]])
                        else:
                            P.emit("pe", fn)
                    mm_group(psf[5 + g], [(X[:, 10 + g, :], hTb[:, g * 512:(g + 1) * 512])], [BI, Bhb], PS[5 + g])
                for g in range(2):
                    hs = slice(g * 8, (g + 1) * 8)
                    dve_(lambda e, g=g, hs=hs: e.tensor_tensor(out=y[:, hs, :], in0=psf[5 + g].rearrange("p (h q) -> p h q", q=64),
                                                               in1=bcast_last(sm[:, 1, hs], 64), op=ALU.mult), reads=[PS[5 + g], Bsm], pwrites=[By])
                    dve_(lambda e, g=g, hs=hs: e.tensor_tensor(out=y[:, hs, :], in0=psf[3 + g].rearrange("p (h q) -> p h q", q=64),
                                                               in1=y[:, hs, :], op=ALU.add), reads=[PS[3 + g], By], pwrites=[By])
                dve_(lambda e: e.tensor_tensor(out=y, in0=y, in1=xD, op=ALU.add), reads=[By, BxD], writes=[By])
                for g in range(2):
                    bank = (3, 4)[g]
                    mm_group(psf[bank], [(btok[:, g * 128:(g + 1) * 128], xdtw[:, g * 8:(g + 1) * 8, :])], [Bbtok, Bxdtw], PS[bank])
                hT3 = hT.rearrange("p (h q) -> p h q", q=64)
                dve_(lambda e: e.tensor_tensor(out=hT3, in0=hT3, in1=bcast_last(sm[:, 2, :], 64), op=ALU.mult), reads=[Bh, Bsm], writes=[Bh])
                for g in range(2):
                    bank = (3, 4)[g]
                    dve_(lambda e, g=g, bank=bank: e.tensor_tensor(out=hT[:, g * 512:(g + 1) * 512], in0=psf[bank], in1=hT[:, g * 512:(g + 1) * 512], op=ALU.add),
                         reads=[PS[bank], Bh], writes=[Bh])
                act_(lambda e: e.activation(out=hTb, in_=hT, func=AF.Copy), reads=[Bh], writes=[Bhb])
                y2 = y.rearrange("p h q -> p (h q)")
                dve_(lambda e: e.tensor_tensor(out=y2, in0=y2, in1=Z, op=ALU.mult), reads=[By, BI], writes=[By])
                dve_(lambda e: e.memset(ss2, 0.0), writes=[Bss2])
                for g in range(2):
                    act_(lambda e, g=g: e.activation(out=junk, in_=y2[:, g * 512:(g + 1) * 512], func=AF.Square, accum_out=ss2[:, g:g + 1]),
                         reads=[By, Bss2], writes=[Bjunk], pwrites=[Bss2])
                rms_rstd(ss2, ss2, 512.0, Bss2, Bss2)
                for g in range(2):
                    dve_(lambda e, g=g: e.scalar_tensor_tensor(out=ssm[:, g * 512:(g + 1) * 512], in0=y2[:, g * 512:(g + 1) * 512], scalar=ss2[:, g:g + 1],
                                                               in1=gssd[:, g * 512:(g + 1) * 512], op0=ALU.mult, op1=ALU.mult),
                         reads=[By, Bss2, Bc], pwrites=[Bssm])
                P.deps("pe", reads=[Bssm, B_const], writes=[PS[0]])
                for j in range(8):
                    fn = lambda e, j=j: e.transpose(psb[0][:, j * 128:(j + 1) * 128], ssm[:, j * 128:(j + 1) * 128], ident)
                    if j == 7:
                        P.commit("pe", fn, reads=[Bssm, B_const], writes=[PS[0]])
                    else:
                        P.emit("pe", fn)
                act_(lambda e: e.activation(out=sst[b], in_=psb[0].rearrange("p (c t) -> p c t", t=128), func=AF.Copy),
                     reads=[PS[0]], writes=[Bsst[b]])
                P.dma("sp", ssmT_v[:, :, tok], sst[b], Bsst[b], reads=[Bsst[b]])

            load(0)
            if NCH > 1:
                load(1)
            front(0)
            for c in range(NCH):
                if c + 2 < NCH:
                    load(c + 2)
                if c + 1 < NCH:
                    front(c + 1)
                back(c)
            P.barrier()
            sb.reset(m)

        if int(os.environ.get("KPH", "9")) >= 2:
            phase1b()

        def phase2():
            m = sb.mark()
            scale = 192.0 ** -0.5
            cmask = sb.alloc([4, 512], BF16)
            gat = sb.alloc([8], F32)
            Bc = Buf("c2")
            P.dma("sp", cmask, c_cmask.rearrange("p (d q) -> p d q", q=512), Bc, pwrites=[Bc])
            P.dma("sp", gat, g_attn, Bc, pwrites=[Bc])
            krT = sb.alloc([S], BF16)
            Bkr = Buf("krT")
            P.dma("sp", krT[:64], kr_s, Bkr, writes=[Bkr])
            KT = [sb.alloc([S], BF16) for _ in range(2)]
            V = [sb.alloc([NCH, 128], BF16) for _ in range(2)]
            Qn = [sb.alloc([TT], BF16) for _ in range(2)]
            Qr = [sb.alloc([TT], BF16) for _ in range(2)]
            Bin = [Buf("a_in0"), Buf("a_in1")]
            PT = [sb.alloc([TT], BF16) for _ in range(5)]
            BPT = [Buf(f"PT{i}") for i in range(5)]
            pt_rot = Rot([0, 1, 2, 3, 4])
            attn = sb.alloc([8, TT], F32)
            Battn = Buf("attn")
            recip = sb.alloc([TT], F32)
            Brecip = Buf("recip")
            sq = [sb.alloc([TT], BF16) for _ in range(2)]
            Bsq = [Buf("asq0"), Buf("asq1")]
            rstdb = sb.alloc([TT], F32)
            Brstdb = Buf("arstd")
            outst = sb.alloc([8, TT], BF16)
            Boutst = Buf("outst")
            v_v = v_s.rearrange("(c p) f -> p c f", p=128)
            attnT_v = attnT_s.rearrange("(h p) s -> p h s", p=128)
            s_rot = Rot([2, 3, 4])
            OB, SB_, SSB = 5, 6, 7

            def load(j, h, b):
                nk = (j + 1) * TT
                tq = slice(j * TT, (j + 1) * TT)
                P.dma("sp", KT[b][:, 0:nk], kn_s[h, :, 0:nk], Bin[b], writes=[Bin[b]])
                P.dma("sp", V[b][:, 0:nk // 128, :], v_v[:, 0:nk // 128, h * 128:(h + 1) * 128], Bin[b], pwrites=[Bin[b]])
                P.dma("sp", Qn[b], qn_s[h, :, tq], Bin[b], pwrites=[Bin[b]])
                P.dma("sp", Qr[b][:64], qr_s[h, :, tq], Bin[b], pwrites=[Bin[b]])

            jobs = [(j, h) for j in range(NT) for h in range(NH)]
            load(jobs[0][0], jobs[0][1], 0)
            LOOK = 2
            pend = []

            def stageA(idx, kb):
                j, h = jobs[idx]
                b = idx % 2
                d = kb - 4 * j
                q0 = d * 128 if d > 0 else 0
                ks = slice(kb * 128, (kb + 1) * 128)
                sbank = s_rot.next()
                mm_group(psf[sbank][:, q0:TT], [(KT[b][:, ks], Qn[b][:, q0:TT]), (krT[:64, ks], Qr[b][:64, q0:TT])],
                         [Bin[b], Bkr], PS[sbank])
                pi = pt_rot.next()
                act_(lambda e: e.activation(out=PT[pi][:, q0:TT], in_=psf[sbank][:, q0:TT], func=AF.Exp, scale=scale),
                     reads=[PS[sbank]], writes=[BPT[pi]])
                if d >= 0:
                    dve_(lambda e: e.tensor_tensor(out=PT[pi][:, q0:q0 + 128], in0=PT[pi][:, q0:q0 + 128], in1=cmask[:, 0, 0:128], op=ALU.mult),
                         reads=[BPT[pi], Bc], writes=[BPT[pi]])
                return (idx, kb, pi, q0)

            def stageB(idx, kb, pi, q0):
                j, h = jobs[idx]
                b = idx % 2
                nkb = 4 * (j + 1)
                first, last = (kb == 0), (kb == nkb - 1)
                P.deps("pe", reads=[BPT[pi], Bin[b], B_const], writes=[PS[OB], PS[SB_]] if first else [])
                P.emit("pe", lambda e: e.matmul(psf[OB][:, q0:TT], V[b][:, kb, :], PT[pi][:, q0:TT], start=first, stop=last))
                P.commit("pe", lambda e: e.matmul(psf[SB_][:, q0:TT], ones, PT[pi][:, q0:TT], start=first, stop=last),
                         reads=[BPT[pi], Bin[b], B_const], writes=[PS[OB], PS[SB_]] if last else [], pwrites=[] if last else [PS[OB], PS[SB_]])
                if last:
                    finish(idx)

            def finish(idx):
                j, h = jobs[idx]
                dve_(lambda e: e.reciprocal(out=recip, in_=psf[SB_]), reads=[PS[SB_]], writes=[Brecip])
                dve_(lambda e: e.tensor_tensor(out=attn[:, h, :], in0=psf[OB], in1=recip, op=ALU.mult), reads=[PS[OB], Brecip], pwrites=[Battn])
                if h == NH - 1:
                    tq = slice(j * TT, (j + 1) * TT)
                    for hh in range(NH):
                        act_(lambda e, hh=hh: e.activation(out=sq[hh % 2], in_=attn[:, hh, :], func=AF.Square), reads=[Battn], writes=[Bsq[hh % 2]])
                        P.deps("pe", reads=[Bsq[hh % 2], B_const], writes=[PS[SSB]] if hh == 0 else [])
                        P.commit("pe", lambda e, hh=hh: e.matmul(psf[SSB], ones, sq[hh % 2], start=(hh == 0), stop=(hh == NH - 1)),
                                 reads=[Bsq[hh % 2], B_const], writes=[PS[SSB]] if hh == NH - 1 else [], pwrites=[PS[SSB]] if hh < NH - 1 else [])
                    rms_rstd(rstdb, psf[SSB], 1024.0, Brstdb, PS[SSB])
                    for hh in range(NH):
                        dve_(lambda e, hh=hh: e.scalar_tensor_tensor(out=outst[:, hh, :], in0=attn[:, hh, :], scalar=gat[:, hh:hh + 1], in1=rstdb,
                                                                     op0=ALU.mult, op1=ALU.mult), reads=[Battn, Bc, Brstdb],
                             writes=[Boutst] if hh == 0 else [], pwrites=[Boutst] if hh > 0 else [])
                    P.dma("sp", attnT_v[:, :, tq], outst, Boutst, reads=[Boutst])

            for idx, (j, h) in enumerate(jobs):
                for kb in range(4 * (j + 1)):
                    pend.append(stageA(idx, kb))
                    if len(pend) > LOOK:
                        stageB(*pend.pop(0))
                    if kb == LOOK - 1 and idx + 1 < len(jobs):
                        load(jobs[idx + 1][0], jobs[idx + 1][1], (idx + 1) % 2)
            while pend:
                stageB(*pend.pop(0))
            P.barrier()
            sb.reset(m)

        if int(os.environ.get("KPH", "9")) >= 3:
            phase2()

        h1_s = dscr("h1_s", [S, D], F32)

        def phase3():
            m = sb.mark()
            gpm = sb.alloc([D], F32)
            gpf = sb.alloc([D], F32)
            gpo = sb.alloc([D], F32)
            Bg = Buf("g3")
            P.dma("sp", gpm, g_postmix, Bg, pwrites=[Bg])
            P.dma("sp", gpf, g_preffn, Bg, pwrites=[Bg])
            P.dma("sp", gpo, g_postffn, Bg, pwrites=[Bg])
            actT = sb.alloc([16, TT], BF16)
            BactT = Buf("actT")
            mix = sb.alloc([4, D], F32)
            Bmix = [Buf(f"mix{i}") for i in range(4)]
            xs2 = [sb.alloc([D], F32) for _ in range(2)]
            Bxs2 = [Buf("xs3a"), Buf("xs3b")]
            xn2 = [sb.alloc([D], BF16) for _ in range(2)]
            Bxn2 = [Buf("xn3a"), Buf("xn3b")]
            ssq2 = [sb.alloc([1], F32) for _ in range(2)]
            Bss2_ = [Buf("ss3a"), Buf("ss3b")]
            hid = sb.alloc([44, TT], BF16)
            Bhid = Buf("hid")
            sg = [sb.alloc([TT], F32) for _ in range(2)]
            Bsg = [Buf("sg0"), Buf("sg1")]
            w_out_v = w_out.rearrange("(kc p) n -> p kc n", p=128)
            w_gate_v = w_gate.rearrange("(kc p) n -> p kc n", p=128)
            w_up_v = w_up.rearrange("(kc p) n -> p kc n", p=128)
            w_down_v = w_down.rearrange("(kc p) n -> p kc n", p=128)
            L1 = []
            L2 = []
            for _t in range(NT):
                for i in range(8):
                    L1.append([(lambda sl: sl, w_out_v[:, :, i * 256:(i + 1) * 256])])
                for i in range(22):
                    L1.append([(lambda sl: sl, w_gate_v[:, :, i * 256:(i + 1) * 256])])
                    L1.append([(lambda sl: sl, w_up_v[:, :, i * 256:(i + 1) * 256])])
                for n in range(4):
                    for kg in range(4):
                        L2.append([(lambda sl: sl, w_down_v[:, kg * 11:(kg + 1) * 11, n * 512:(n + 1) * 512])])
            ws1 = WStream(4, [16, 256], L1)
            ws2 = WStream(2, [11, 512], L2)
            tr_banks = Rot([0, 1])
            mm_banks = Rot([2, 3, 4, 5, 6, 7])
            attnT_v = attnT_s.rearrange("(c p) s -> p c s", p=128)
            ssmT_v = ssmT_s.rearrange("(c p) s -> p c s", p=128)
            junk3 = sb.alloc([TT], BF16)
            Bjunk3 = Buf("junk3")
            ssacc = sb.alloc([4, 12], F32)
            Bssacc = Buf("ssacc")
            ev = [0]
            Bh1 = [Buf(f"h1_{i}") for i in range(4)]

            def evac(dst, src, reads, **kw):
                ev[0] += 1
                if ev[0] % 2:
                    act_(lambda e: e.activation(out=dst, in_=src, func=AF.Copy), reads=reads, **kw)
                else:
                    dve_(lambda e: e.tensor_copy(out=dst, in_=src), reads=reads, **kw)

            for t in range(NT):
                tok = slice(t * TT, (t + 1) * TT)
                if t == 0:
                    P.dma("sp", actT[:, 0:8, :], attnT_v[:, :, tok], BactT, writes=[BactT])
                    P.dma("sp", actT[:, 8:16, :], ssmT_v[:, :, tok], BactT, pwrites=[BactT])
                dve_(lambda e: e.memset(ssacc, 0.0), writes=[Bssacc])
                for i in range(8):
                    wt, Bwt = ws1.get()
                    for sub in range(4):
                        bank = mm_banks.next()
                        mm_group(psf[bank][:, 0:256], [(actT[:, kc, sub * 128:(sub + 1) * 128], wt[:, kc, :]) for kc in range(16)],
                                 [Bwt, BactT], PS[bank])
                        act_(lambda e, bank=bank, sub=sub, i=i: e.activation(out=mix[:, sub, i * 256:(i + 1) * 256], in_=psf[bank][:, 0:256], func=AF.Copy),
                             reads=[PS[bank]], pwrites=[Bmix[sub]])
                        act_(lambda e, bank=bank, sub=sub, i=i: e.activation(out=junk3[:, 0:256], in_=psf[bank][:, 0:256], func=AF.Square,
                                                                             accum_out=ssacc[:, sub, i:i + 1]),
                             reads=[PS[bank], Bssacc], writes=[Bjunk3], pwrites=[Bssacc])
                    ws1.release()
                for sub in range(4):
                    r0 = t * TT + sub * 128
                    M = mix[:, sub, :]
                    xs, Bxs, xn, Bxn, ssq, Bss = xs2[sub % 2], Bxs2[sub % 2], xn2[sub % 2], Bxn2[sub % 2], ssq2[sub % 2], Bss2_[sub % 2]
                    P.dma("sp", xs, x[r0:r0 + 128, :], Bxs, writes=[Bxs])
                    dve_(lambda e, ssq=ssq, sub=sub: e.reduce_sum(out=ssq, in_=ssacc[:, sub, 0:8], axis=AX.X), reads=[Bssacc], writes=[Bss])
                    rms_rstd(ssq, ssq, float(D), Bss, Bss)
                    dve_(lambda e, M=M, ssq=ssq: e.scalar_tensor_tensor(out=M, in0=M, scalar=ssq, in1=gpm, op0=ALU.mult, op1=ALU.mult),
                         reads=[Bmix[sub], Bss, Bg], writes=[Bmix[sub]])
                    dve_(lambda e, M=M, xs=xs: e.tensor_tensor(out=M, in0=M, in1=xs, op=ALU.add), reads=[Bmix[sub], Bxs], writes=[Bmix[sub]])
                    P.dma("sp", h1_s[r0:r0 + 128, :], M, Bmix[sub], reads=[Bmix[sub]], writes=[Bh1[sub]])
                    norm_transpose(None, gpf, Bg, actT, BactT, M, Bmix[sub], xn, Bxn, ssq, Bss, tr_banks, sub)
                for i in range(22):
                    wg, Bwg = ws1.get()
                    ws1.cons += 1
                    wu, Bwu = ws1.get()
                    ws1.cons -= 1
                    gbs = [mm_banks.next(), mm_banks.next()]
                    for cc in range(2):
                        mm_group(psf[gbs[cc]], [(wg[:, kc, cc * 128:(cc + 1) * 128], actT[:, kc, :]) for kc in range(16)], [Bwg, BactT], PS[gbs[cc]])
                    ws1.release()
                    for cc in range(2):
                        fc = 2 * i + cc
                        gb = gbs[cc]
                        ub = mm_banks.next()
                        mm_group(psf[ub], [(wu[:, kc, cc * 128:(cc + 1) * 128], actT[:, kc, :]) for kc in range(16)], [Bwu, BactT], PS[ub])
                        k2 = fc % 2
                        act_(lambda e, gb=gb, k2=k2: e.activation(out=sg[k2], in_=psf[gb], func=AF.Silu), reads=[PS[gb]], writes=[Bsg[k2]])
                        dve_(lambda e, ub=ub, k2=k2, fc=fc: e.tensor_tensor(out=hid[:, fc, :], in0=psf[ub], in1=sg[k2], op=ALU.mult),
                             reads=[PS[ub], Bsg[k2]], pwrites=[Bhid])
                    ws1.release()
                if t + 1 < NT:
                    tokn = slice((t + 1) * TT, (t + 2) * TT)
                    P.dma("sp", actT[:, 0:8, :], attnT_v[:, :, tokn], BactT, writes=[BactT])
                    P.dma("sp", actT[:, 8:16, :], ssmT_v[:, :, tokn], BactT, pwrites=[BactT])
                for n in range(4):
                    for kg in range(4):
                        wd, Bwd = ws2.get()
                        for sub in range(4):
                            bank = 4 + sub
                            P.deps("pe", reads=[Bwd, Bhid], writes=[PS[bank]] if kg == 0 else [])
                            for kk in range(11):
                                fcn = kg * 11 + kk
                                fn = (lambda e, bank=bank, fcn=fcn, kk=kk, sub=sub, wd=wd:
                                      e.matmul(psf[bank], hid[:, fcn, sub * 128:(sub + 1) * 128], wd[:, kk, :], start=(fcn == 0), stop=(fcn == 43)))
                                if kk == 10:
                                    P.commit("pe", fn, reads=[Bwd, Bhid], writes=[PS[bank]] if kg == 3 else [], pwrites=[PS[bank]] if kg < 3 else [])
                                else:
                                    P.emit("pe", fn)
                        ws2.release()
                    for sub in range(4):
                        act_(lambda e, sub=sub, n=n: e.activation(out=mix[:, sub, n * 512:(n + 1) * 512], in_=psf[4 + sub], func=AF.Copy),
                             reads=[PS[4 + sub]], pwrites=[Bmix[sub]])
                        act_(lambda e, sub=sub, n=n: e.activation(out=junk3, in_=psf[4 + sub], func=AF.Square, accum_out=ssacc[:, sub, 8 + n:9 + n]),
                             reads=[PS[4 + sub], Bssacc], writes=[Bjunk3], pwrites=[Bssacc])
                for sub in range(4):
                    r0 = t * TT + sub * 128
                    M = mix[:, sub, :]
                    xs, Bxs, xn, Bxn, ssq, Bss = xs2[sub % 2], Bxs2[sub % 2], xn2[sub % 2], Bxn2[sub % 2], ssq2[sub % 2], Bss2_[sub % 2]
                    P.dma("sp", xs, h1_s[r0:r0 + 128, :], Bxs, reads=[Bh1[sub]], writes=[Bxs])
                    dve_(lambda e, ssq=ssq, sub=sub: e.reduce_sum(out=ssq, in_=ssacc[:, sub, 8:12], axis=AX.X), reads=[Bssacc], writes=[Bss])
                    rms_rstd(ssq, ssq, float(D), Bss, Bss)
                    dve_(lambda e, M=M, ssq=ssq: e.scalar_tensor_tensor(out=M, in0=M, scalar=ssq, in1=gpo, op0=ALU.mult, op1=ALU.mult),
                         reads=[Bmix[sub], Bss, Bg], writes=[Bmix[sub]])
                    dve_(lambda e, M=M, xs=xs: e.tensor_tensor(out=M, in0=M, in1=xs, op=ALU.add), reads=[Bmix[sub], Bxs], writes=[Bmix[sub]])
                    P.dma("sp", out[r0:r0 + 128, :], M, Bmix[sub], reads=[Bmix[sub]])
            P.barrier()
            sb.reset(m)

        if int(os.environ.get("KPH", "9")) >= 4:
            phase3()

        outs_done = []

        P.barrier()

        sems = [es.enter_context(nc.semaphore(f"s{i}")) for i in range(len(P.cnt))]
        with nc.Block() as block:
            @block.tensor
            def _(e):
                P.replay("pe", e, sems)

            @block.scalar
            def _(e):
                P.replay("act", e, sems)

            @block.vector
            def _(e):
                P.replay("dve", e, sems)

            @block.gpsimd
            def _(e):
                P.replay("pool", e, sems)

            @block.sync
            def _(e):
                P.replay("sp", e, sems)
    return nc


def _consts():
    bf = ml_dtypes.bfloat16
    k = np.arange(128)
    ident = np.eye(128, dtype=np.float32).astype(bf)
    tri = (k[:, None] <= k[None, :]).astype(np.float32)
    negm = np.where(k[:, None] <= k[None, :], 0.0, -1e30).astype(np.float32)
    q = np.arange(512)
    cm = np.zeros((128, 4, 512), np.float32)
    for d in range(4):
        cm[:, d, :] = (q[None, :] >= d * 128 + k[:, None])
    cm = cm.reshape(128, 2048).astype(bf)
    inv_freq = (np.float32(10000.0) ** (-np.arange(0, 64, 2, dtype=np.float32) / np.float32(64))).astype(np.float32)
    rope = np.zeros((64, 2), np.float32)
    rope[:, 0] = np.concatenate([inv_freq, inv_freq])
    rope[:32, 1] = -1.0
    rope[32:, 1] = 1.0
    return dict(c_ident=ident, c_tri=tri, c_negm=negm, c_cmask=cm, c_rope=rope)


def make_in_maps(inputs, S, batches):
    f = lambda a: np.ascontiguousarray(np.asarray(a, dtype=np.float32))
    rep = lambda v, n=128: np.ascontiguousarray(np.broadcast_to(np.asarray(v, np.float32)[None, :], (n, len(v))))
    pp = lambda v: np.ascontiguousarray(np.asarray(v, np.float32).reshape(-1, 128).T)
    shared = dict(_consts())
    shared.update(
        w_in=f(inputs["w_in"][0]), w_uq=f(inputs["w_uq"][0]), w_ukv=f(inputs["w_ukv"][0]),
        w_out=f(inputs["w_out"][0]), w_gate=f(inputs["w_gate"][0]), w_up=f(inputs["w_up"][0]),
        w_down=f(inputs["w_down"][0]),
        g_pre=rep(inputs["pre_mix_norm_w"][0]), g_preffn=rep(inputs["pre_ffn_norm_w"][0]),
        g_postmix=rep(inputs["post_mix_norm_w"][0]), g_postffn=rep(inputs["post_ffn_norm_w"][0]),
        g_q=pp(inputs["q_norm_w"][0]), g_kv=pp(inputs["kv_norm_w"][0]), g_attn=pp(inputs["attn_out_norm_w"][0]),
        g_ssd=rep(inputs["ssd_norm_w"][0]),
    )
    cw = np.asarray(inputs["conv_w"][0], np.float32)
    cb = np.asarray(inputs["conv_b"][0], np.float32)
    cwb = np.concatenate([cw, cb[None, :]], axis=0)
    shared["conv_wb"] = np.ascontiguousarray(cwb.reshape(5, 12, 128).transpose(2, 1, 0).reshape(128, 60))
    small = np.concatenate([np.asarray(inputs["dt_bias"][0], np.float32), np.asarray(inputs["a_log"][0], np.float32),
                            np.asarray(inputs["d_skip"][0], np.float32)])
    shared["ssd_small"] = rep(small)
    maps = []
    for b in batches:
        m = dict(shared)
        m["x"] = f(inputs["x"][b][:S])
        pos = np.asarray(inputs["positions"][b][:S], np.int32)
        m["posb"] = np.ascontiguousarray(np.broadcast_to(pos[None, :], (64, S)))
        maps.append(m)
    return maps


def kernel(**inputs):
    S = inputs["x"].shape[1]
    B = inputs["x"].shape[0]
    nc = build_program(S)
    maps = make_in_maps(inputs, S, list(range(B)))
    res = run_bass_kernel_spmd(nc, maps, core_ids=list(range(B)))
    return np.stack([np.asarray(r["out"], dtype=np.float32) for r in res.results], axis=0)
```

```python
import math
import os
from contextlib import ExitStack

import numpy as np
import ml_dtypes

import concourse.bass as bass
import concourse.mybir as mybir
from concourse.bass_utils import run_bass_kernel_spmd

F32 = mybir.dt.float32
BF16 = mybir.dt.bfloat16
I32 = mybir.dt.int32
AF = mybir.ActivationFunctionType
ALU = mybir.AluOpType
AX = mybir.AxisListType

D = 2048
DIN = 3664
DFF = 5632
NH = 8
EPS = 1e-6
TT = 512

ENGS = ("pe", "act", "dve", "pool", "sp")


class Buf:
    __slots__ = ("name", "w", "r", "dsem", "excl")

    def __init__(self, name, excl=False):
        self.name = name
        self.excl = excl
        self.w = {}
        self.r = {}
        self.dsem = None


class Prog:
    def __init__(self):
        self.q = {e: [] for e in ENGS}
        self.cnt = []
        self.waited = {e: {} for e in ENGS}
        self.esem = {e: self.new_sem() for e in ENGS}

    def new_sem(self):
        self.cnt.append(0)
        return len(self.cnt) - 1

    def _wait(self, eng, k, v):
        if eng == "pe" and k == self.esem["pe"]:
            return
        if self.waited[eng].get(k, 0) >= v:
            return
        self.waited[eng][k] = v
        self.q[eng].append(("wait", k, v))

    def deps(self, eng, reads=(), writes=(), pwrites=()):
        for b in reads:
            for k, v in b.w.items():
                self._wait(eng, k, v)
            if b.excl:
                for k, v in b.r.items():
                    if k != self.esem[eng]:
                        self._wait(eng, k, v)
        for b in writes:
            for k, v in b.w.items():
                self._wait(eng, k, v)
            for k, v in b.r.items():
                self._wait(eng, k, v)
        for b in pwrites:
            for k, v in b.r.items():
                self._wait(eng, k, v)

    def emit(self, eng, fn):
        self.q[eng].append(("op", fn, None, 0))

    def _register(self, ev, reads, writes, pwrites):
        k, v = ev
        for b in reads:
            b.r[k] = max(b.r.get(k, 0), v)
        for b in writes:
            b.w = {k: v}
        for b in pwrites:
            b.w[k] = max(b.w.get(k, 0), v)

    def commit(self, eng, fn, reads=(), writes=(), pwrites=()):
        k = self.esem[eng]
        self.cnt[k] += 1
        self.q[eng].append(("op", fn, k, 1))
        self._register((k, self.cnt[k]), reads, writes, pwrites)

    def op(self, eng, fn, reads=(), writes=(), pwrites=()):
        self.deps(eng, reads, writes, pwrites)
        self.commit(eng, fn, reads, writes, pwrites)

    def dma(self, eng, out, in_, sb, reads=(), writes=(), pwrites=()):
        self.deps(eng, reads, writes, pwrites)
        if sb.dsem is None:
            sb.dsem = self.new_sem()
        k = sb.dsem
        self.cnt[k] += 16
        self.q[eng].append(("op", lambda e: e.dma_start(out=out, in_=in_), k, 16))
        self._register((k, self.cnt[k]), reads, writes, pwrites)

    def barrier(self):
        for e in ENGS:
            for k in range(len(self.cnt)):
                if self.cnt[k] > 0:
                    self._wait(e, k, self.cnt[k])

    def replay(self, eng, e, sems):
        for it in self.q[eng]:
            if it[0] == "wait":
                e.wait_ge(sems[it[1]], it[2])
            else:
                ins = it[1](e)
                if it[2] is not None:
                    ins.then_inc(sems[it[2]], it[3])


class SB:
    def __init__(self, big, nbytes):
        self.big = big
        self.nbytes = nbytes
        self.off = 0

    def mark(self):
        return self.off

    def reset(self, m):
        self.off = m

    def alloc(self, shape, dtype):
        n = int(np.prod(shape))
        esz = 4 if dtype in (F32, I32) else 2
        nb = n * esz
        self.off = (self.off + 63) // 64 * 64
        assert self.off + nb <= self.nbytes, f"SBUF overflow {self.off + nb} > {self.nbytes}"
        ap = self.big[:, self.off // 2:(self.off + nb) // 2]
        self.off += nb
        if esz == 4:
            ap = ap.bitcast(dtype)
        if len(shape) == 2:
            ap = ap.rearrange("p (a b) -> p a b", b=shape[1])
        elif len(shape) == 3:
            ap = ap.rearrange("p (a b c) -> p a b c", b=shape[1], c=shape[2])
        return ap


def build_program(S, debug=False):
    assert S % TT == 0
    NT = S // TT
    NCH = S // 128
    nc = bass.Bass("TRN2", target_bir_lowering=False)

    def din(name, shape, dt=F32):
        return nc.dram_tensor(name, list(shape), dt, kind="ExternalInput").ap()

    skind = "ExternalOutput" if debug else "Internal"

    def dscr(name, shape, dt):
        return nc.dram_tensor(name, list(shape), dt, kind=skind).ap()

    x = din("x", [S, D])
    posb = din("posb", [64, S], I32)
    w_in = din("w_in", [D, DIN])
    w_uq = din("w_uq", [512, 1536])
    w_ukv = din("w_ukv", [512, 2048])
    w_out = din("w_out", [D, D])
    w_gate = din("w_gate", [D, DFF])
    w_up = din("w_up", [D, DFF])
    w_down = din("w_down", [DFF, D])
    c_ident = din("c_ident", [128, 128], BF16)
    c_tri = din("c_tri", [128, 128])
    c_negm = din("c_negm", [128, 128])
    c_cmask = din("c_cmask", [128, 4 * 512], BF16)
    c_rope = din("c_rope", [64, 2])
    g_pre = din("g_pre", [128, D])
    g_preffn = din("g_preffn", [128, D])
    g_postmix = din("g_postmix", [128, D])
    g_postffn = din("g_postffn", [128, D])
    g_q = din("g_q", [128, 4])
    g_kv = din("g_kv", [128, 4])
    g_attn = din("g_attn", [128, 8])
    g_ssd = din("g_ssd", [128, 1024])
    conv_wb = din("conv_wb", [128, 12 * 5])
    ssd_small = din("ssd_small", [128, 48])
    out = nc.dram_tensor("out", [S, D], F32, kind="ExternalOutput").ap()

    qn_s = dscr("qn_s", [NH, 128, S], BF16)
    qr_s = dscr("qr_s", [NH, 64, S], BF16)
    kn_s = dscr("kn_s", [NH, 128, S], BF16)
    kr_s = dscr("kr_s", [64, S], BF16)
    v_s = dscr("v_s", [S, 1024], BF16)
    zs_s = dscr("zs_s", [S, 1024], BF16)
    xbc_s = dscr("xbc_s", [1536, S], BF16)
    dt_s = dscr("dt_s", [S, 16], F32)
    cos_s = dscr("cos_s", [64, S], F32)
    sin_s = dscr("sin_s", [64, S], F32)
    ssmT_s = dscr("ssmT_s", [1024, S], BF16)
    attnT_s = dscr("attnT_s", [1024, S], BF16)

    P = Prog()
    SBYTES = 207 * 1024

    with ExitStack() as es:
        big = es.enter_context(nc.sbuf_tensor("big", [128, SBYTES // 2], BF16))
        sb = SB(big, SBYTES)
        psum = [es.enter_context(nc.psum_tensor(f"ps{i}", [128, 512], F32)) for i in range(8)]
        PS = [Buf(f"ps{i}", excl=True) for i in range(8)]
        psf = [p[:] for p in psum]
        psb = [p[:].bitcast(BF16) for p in psum]

        ident = sb.alloc([128], BF16)
        ones = sb.alloc([128], BF16)
        B_const = Buf("const")
        P.dma("sp", ident, c_ident, B_const, writes=[B_const])
        P.op("dve", lambda e: e.memset(ones, 1.0), pwrites=[B_const])
        m_persist = sb.mark()

        class WStream:
            def __init__(self, nslots, shape, loads):
                self.nslots = nslots
                self.slots = [sb.alloc(shape, BF16) for _ in range(nslots)]
                self.bufs = [Buf(f"wslot{i}") for i in range(nslots)]
                self.loads = loads
                self.issued = 0
                self.cons = 0
                for _ in range(nslots):
                    self._issue()

            def _issue(self):
                i = self.issued
                if i >= len(self.loads):
                    return
                s = i % self.nslots
                for dst_fn, src in self.loads[i]:
                    P.dma("pool", dst_fn(self.slots[s]), src, self.bufs[s], pwrites=[self.bufs[s]])
                self.issued += 1

            def get(self):
                s = self.cons % self.nslots
                return self.slots[s], self.bufs[s]

            def release(self):
                self.cons += 1
                if int(os.environ.get('KNOREFILL', '0')):
                    return
                self._issue()

        def phase0():
            m = sb.mark()
            crope = sb.alloc([2], F32)
            Bc = Buf("crope")
            P.dma("sp", crope[:64], c_rope, Bc, writes=[Bc])
            C1 = 6.28125
            C2 = 2 * math.pi - C1
            PI_IN = 3.1415925
            CW = 1024

            def wrap(t, msk, Bt, Bm):
                P.op("dve", lambda e: e.tensor_scalar(out=msk[:64], in0=t[:64], scalar1=-math.pi, scalar2=None, op0=ALU.is_lt),
                     reads=[Bt], writes=[Bm])
                P.op("dve", lambda e: e.scalar_tensor_tensor(out=t[:64], in0=msk[:64], scalar=2 * math.pi, in1=t[:64],
                                                             op0=ALU.mult, op1=ALU.add), reads=[Bm, Bt], writes=[Bt])
                P.op("dve", lambda e: e.tensor_scalar(out=msk[:64], in0=t[:64], scalar1=math.pi, scalar2=None, op0=ALU.is_gt),
                     reads=[Bt], writes=[Bm])
                P.op("dve", lambda e: e.scalar_tensor_tensor(out=t[:64], in0=msk[:64], scalar=-2 * math.pi, in1=t[:64],
                                                             op0=ALU.mult, op1=ALU.add), reads=[Bm, Bt], writes=[Bt])
                P.op("dve", lambda e: e.tensor_scalar(out=t[:64], in0=t[:64], scalar1=PI_IN, scalar2=-PI_IN,
                                                      op0=ALU.min, op1=ALU.max), reads=[Bt], writes=[Bt])

            for c0 in range(0, S, CW):
                cw = min(CW, S - c0)
                pi_t = sb.alloc([cw], I32)
                ang = sb.alloc([cw], F32)
                t1 = sb.alloc([cw], F32)
                t2 = sb.alloc([cw], F32)
                msk = sb.alloc([cw], F32)
                ni = sb.alloc([cw], I32)
                Bp, Ba, B1, B2, Bm, Bn = Buf("pi"), Buf("ang"), Buf("t1"), Buf("t2"), Buf("msk"), Buf("ni")
                P.dma("sp", pi_t[:64], posb[:, c0:c0 + cw], Bp, writes=[Bp])
                P.op("dve", lambda e, a=ang, p=pi_t: e.tensor_copy(out=a[:64], in_=p[:64]), reads=[Bp], writes=[Ba])
                P.op("dve", lambda e, a=ang: e.tensor_scalar(out=a[:64], in0=a[:64], scalar1=crope[:64, 0:1], scalar2=None,
                                                             op0=ALU.mult), reads=[Ba, Bc], writes=[Ba])
                P.op("dve", lambda e, a=ang, t=t1: e.tensor_scalar(out=t[:64], in0=a[:64], scalar1=1.0 / (2 * math.pi), scalar2=0.5,
                                                                   op0=ALU.mult, op1=ALU.add), reads=[Ba], writes=[B1])
                P.op("dve", lambda e, t=t1, n=ni: e.tensor_copy(out=n[:64], in_=t[:64]), reads=[B1], writes=[Bn])
                P.op("dve", lambda e, t=t2, n=ni: e.tensor_copy(out=t[:64], in_=n[:64]), reads=[Bn], writes=[B2])
                P.op("dve", lambda e, a=ang, t=t1, n=t2: e.scalar_tensor_tensor(out=t[:64], in0=n[:64], scalar=-C1, in1=a[:64],
                                                                                op0=ALU.mult, op1=ALU.add), reads=[B2, Ba], writes=[B1])
                P.op("dve", lambda e, t=t1, n=t2: e.scalar_tensor_tensor(out=t[:64], in0=n[:64], scalar=-C2, in1=t[:64],
                                                                         op0=ALU.mult, op1=ALU.add), reads=[B2, B1], writes=[B1])
                wrap(t1, msk, B1, Bm)
                P.op("dve", lambda e, t=t1, u=t2: e.tensor_scalar(out=u[:64], in0=t[:64], scalar1=0.5 * math.pi, scalar2=None,
                                                                  op0=ALU.add), reads=[B1], writes=[B2])
                wrap(t2, msk, B2, Bm)
                P.op("act", lambda e, t=t1: e.activation(out=t[:64], in_=t[:64], func=AF.Sin), reads=[B1], writes=[B1])
                P.op("act", lambda e, t=t2: e.activation(out=t[:64], in_=t[:64], func=AF.Sin), reads=[B2], writes=[B2])
                P.op("dve", lambda e, t=t1: e.tensor_scalar(out=t[:64], in0=t[:64], scalar1=crope[:64, 1:2], scalar2=None,
                                                            op0=ALU.mult), reads=[B1, Bc], writes=[B1])
                P.dma("sp", sin_s[:, c0:c0 + cw], t1[:64], B1, reads=[B1])
                P.dma("sp", cos_s[:, c0:c0 + cw], t2[:64], B2, reads=[B2])
            P.barrier()
            sb.reset(m)

        phase0()

        def act_(fn, **kw):
            P.op("act", fn, **kw)

        def dve_(fn, **kw):
            P.op("dve", fn, **kw)

        def mm_group(out_ap, pairs, reads, psbuf):
            P.deps("pe", reads=reads, writes=[psbuf])
            n = len(pairs)
            for i, (l, r) in enumerate(pairs):
                fn = (lambda e, l=l, r=r, st=(i == 0), sp=(i == n - 1): e.matmul(out_ap, l, r, start=st, stop=sp))
                if i == n - 1:
                    P.commit("pe", fn, reads=reads, writes=[psbuf])
                else:
                    P.emit("pe", fn)

        class Rot:
            def __init__(self, items):
                self.items = items
                self.i = 0

            def next(self):
                it = self.items[self.i % len(self.items)]
                self.i += 1
                return it

        def rms_rstd(dst, src, n, Bdst, Bsrc):
            act_(lambda e: e.activation(out=dst, in_=src, func=AF.Sqrt, scale=1.0 / n, bias=EPS), reads=[Bsrc], writes=[Bdst])
            dve_(lambda e: e.reciprocal(out=dst, in_=dst), reads=[Bdst], writes=[Bdst])

        def norm_transpose(src_fn, gain, Bgain, uT, BuT, xs, Bxs, xn, Bxn, ss, Bss, tr_banks, sub, pre_loaded=False):
            KSUB = int(os.environ.get('KSUB', '99'))
            if src_fn is not None:
                src_fn(xs, Bxs)
            dve_(lambda e: e.memset(ss, 0.0), writes=[Bss])
            if KSUB < 1:
                return
            act_(lambda e: e.activation(out=xn, in_=xs, func=AF.Square, accum_out=ss), reads=[Bxs], writes=[Bxn, Bss])
            if KSUB < 2:
                return
            rms_rstd(ss, ss, float(D), Bss, Bss)
            if KSUB < 3:
                return
            dve_(lambda e: e.scalar_tensor_tensor(out=xn, in0=xs, scalar=ss, in1=gain, op0=ALU.mult, op1=ALU.mult),
                 reads=[Bxs, Bss, Bgain], writes=[Bxn])
            if KSUB < 4:
                return
            for half in range(2):
                if KSUB < 5 + half:
                    return
                bank = tr_banks.next()
                P.deps("pe", reads=[Bxn, B_const], writes=[PS[bank]])
                for i in range(8):
                    c = half * 8 + i
                    fn = lambda e, c=c, i=i, bank=bank: e.transpose(psb[bank][:, i * 128:(i + 1) * 128], xn[:, c * 128:(c + 1) * 128], ident)
                    if i == 7:
                        P.commit("pe", fn, reads=[Bxn, B_const], writes=[PS[bank]])
                    else:
                        P.emit("pe", fn)
                if KSUB < 7:
                    continue
                src = psb[bank].rearrange("p (c t) -> p c t", t=128)
                dst = uT[:, half * 8:(half + 1) * 8, sub * 128:(sub + 1) * 128]
                if half == 0:
                    P.op("act", lambda e, s_=src, d_=dst: e.activation(out=d_, in_=s_, func=AF.Copy), reads=[PS[bank]], pwrites=[BuT])
                else:
                    P.op("dve", lambda e, s_=src, d_=dst: e.tensor_copy(out=d_, in_=s_), reads=[PS[bank]], pwrites=[BuT])

        def phase1a():
            m = sb.mark()
            w_in_v = w_in.rearrange("(kc p) n -> p kc n", p=128)
            w_uq_v = w_uq.rearrange("(kc p) n -> p kc n", p=128)
            wuq = sb.alloc([4, 1536], BF16)
            wuqr = sb.alloc([4, 512], BF16)
            wukv = sb.alloc([4, 2048], BF16)
            Bw = Buf("w_res")
            P.dma("pool", wuq, w_uq_v, Bw, pwrites=[Bw])
            for h in range(NH):
                base = h * 192 + 128
                P.dma("pool", wuqr[:, :, h * 64:h * 64 + 32], w_uq_v[:, :, base + 32:base + 64], Bw, pwrites=[Bw])
                P.dma("pool", wuqr[:, :, h * 64 + 32:h * 64 + 64], w_uq_v[:, :, base:base + 32], Bw, pwrites=[Bw])
            P.dma("pool", wukv, w_ukv.rearrange("(kc p) n -> p kc n", p=128), Bw, pwrites=[Bw])
            wukv_h = wukv.rearrange("p kc (h c) -> p kc h c", c=256)
            gpre = sb.alloc([D], F32)
            gq = sb.alloc([4], F32)
            gkv = sb.alloc([4], F32)
            cwb = sb.alloc([12, 5], F32)
            small = sb.alloc([48], F32)
            Bg = Buf("gains")
            P.dma("sp", gpre, g_pre, Bg, pwrites=[Bg])
            P.dma("sp", gq, g_q, Bg, pwrites=[Bg])
            P.dma("sp", gkv, g_kv, Bg, pwrites=[Bg])
            P.dma("sp", cwb, conv_wb.rearrange("p (c k) -> p c k", k=5), Bg, pwrites=[Bg])
            P.dma("sp", small, ssd_small, Bg, pwrites=[Bg])

            xs = [sb.alloc([D], F32) for _ in range(2)]
            Bxs = [Buf("xs0"), Buf("xs1")]
            xn = [sb.alloc([D], BF16) for _ in range(2)]
            Bxn = [Buf("xn0"), Buf("xn1")]
            ssq = [sb.alloc([1], F32) for _ in range(2)]
            Bss = [Buf("ss0"), Buf("ss1")]
            uTs = [sb.alloc([16, TT], BF16) for _ in range(2)]
            BuTs = [Buf("uT0"), Buf("uT1")]
            craw = [sb.alloc([4, TT], F32) for _ in range(2)]
            Bcraw = [Buf("craw0"), Buf("craw1")]
            sq = [sb.alloc([TT], BF16) for _ in range(2)]
            Bsq = [Buf("sq0"), Buf("sq1")]
            rstdb = sb.alloc([TT], F32)
            Brstdb = Buf("rstdb")
            cn = [sb.alloc([4, TT], BF16) for _ in range(2)]
            Bcn = [Buf("cqn"), Buf("ckvn")]
            cst = [sb.alloc([TT], F32) for _ in range(2)]
            Bcst = Buf("cossin")
            rt = [sb.alloc([TT], F32) for _ in range(4)]
            Brt = [Buf(f"rt{i}") for i in range(4)]
            rt_rot = Rot([0, 2])
            stg = [sb.alloc([TT], BF16) for _ in range(4)]
            Bstg = [Buf(f"stg{i}") for i in range(4)]
            stg_rot = Rot(list(range(4)))
            vst = [sb.alloc([1024], BF16) for _ in range(2)]
            Bvst = [Buf("vst0"), Buf("vst1")]
            zst = sb.alloc([4, 1024], BF16)
            Bzst = [Buf(f"zst{i}") for i in range(4)]
            xraw = [sb.alloc([TT + 3], F32) for _ in range(2)]
            Bxraw = [Buf("xraw0"), Buf("xraw1")]
            halo = sb.alloc([12, 3], F32)
            Bhalo = Buf("halo")
            acc = [sb.alloc([TT], F32) for _ in range(2)]
            Bacc = [Buf("acc0"), Buf("acc1")]
            dtst = sb.alloc([4, 16], F32)
            Bdtst = Buf("dtst")
            dttmp4 = [sb.alloc([16], F32) for _ in range(4)]
            Bdttmp4 = [Buf(f"dttmp{i}") for i in range(4)]
            dve_(lambda e: e.memset(halo, 0.0), writes=[Bhalo])

            def mk_loads():
                L = []
                for _t in range(NT):
                    for c0 in (0, 256, 512, 768):
                        L.append([(lambda sl: sl, w_in_v[:, :, c0:c0 + 256])])
                    L.append([(lambda sl: sl[:, :, 0:64], w_in_v[:, :, 1024:1088]),
                              (lambda sl: sl[:, :, 64:96], w_in_v[:, :, 1056:1088]),
                              (lambda sl: sl[:, :, 96:128], w_in_v[:, :, 1024:1056])])
                    for i in range(4):
                        c0 = 1088 + 256 * i
                        L.append([(lambda sl: sl, w_in_v[:, :, c0:c0 + 256])])
                    for i in range(6):
                        c0 = 2112 + 256 * i
                        L.append([(lambda sl: sl, w_in_v[:, :, c0:c0 + 256])])
                    L.append([(lambda sl: sl[:, :, 0:16], w_in_v[:, :, 3648:3664])])
                return L
            ws = WStream(4, [16, 256], mk_loads())

            tr_banks = Rot([0, 1])
            mm_banks = Rot([2, 3, 4, 5, 6])
            SSB = 7

            def store(dst, src, Bsrc):
                P.dma("sp", dst, src, Bsrc, reads=[Bsrc])

            def rope_out(dst_dram, RA, RB):
                r0_ = rt_rot.next()
                r1_ = r0_ + 1
                dve_(lambda e: e.tensor_tensor(out=rt[r0_][:64], in0=psf[RA][:64], in1=cst[0][:64], op=ALU.mult),
                     reads=[PS[RA], Bcst], writes=[Brt[r0_]])
                dve_(lambda e: e.tensor_tensor(out=rt[r1_][:64], in0=psf[RB][:64], in1=cst[1][:64], op=ALU.mult),
                     reads=[PS[RB], Bcst], writes=[Brt[r1_]])
                si = stg_rot.next()
                dve_(lambda e: e.tensor_tensor(out=stg[si][:64], in0=rt[r0_][:64], in1=rt[r1_][:64], op=ALU.add),
                     reads=[Brt[r0_], Brt[r1_]], writes=[Bstg[si]])
                store(dst_dram, stg[si][:64], Bstg[si])

            KB = int(os.environ.get('KB', '99'))

            def latent_norm(which, gl):
                for half in range(2):
                    wt, Bwt = ws.get()
                    for cc in range(2):
                        ch = half * 2 + cc
                        bank = mm_banks.next()
                        mm_group(psf[bank], [(wt[:, kc, cc * 128:(cc + 1) * 128], uT[:, kc, :]) for kc in range(16)],
                                 [Bwt, BuT], PS[bank])
                        if KB < 1:
                            continue
                        KX = int(os.environ.get('KX', '3'))
                        if KX & 1:
                            act_(lambda e, bank=bank, ch=ch: e.activation(out=sq[ch % 2], in_=psf[bank], func=AF.Square),
                                 reads=[PS[bank]], writes=[Bsq[ch % 2]])
                        if KX & 2:
                            dve_(lambda e, bank=bank, ch=ch: e.tensor_copy(out=craw[which][:, ch, :], in_=psf[bank]),
                                 reads=[PS[bank]], pwrites=[Bcraw[which]])
                        if KB < 2:
                            continue
                        P.deps("pe", reads=[Bsq[ch % 2], B_const], writes=[PS[SSB]] if ch == 0 else [])
                        P.commit("pe", lambda e, ch=ch: e.matmul(psf[SSB], ones, sq[ch % 2], start=(ch == 0), stop=(ch == 3)),
                                 reads=[Bsq[ch % 2], B_const], writes=[PS[SSB]] if ch == 3 else [], pwrites=[PS[SSB]] if ch < 3 else [])
                    ws.release()
                if KB < 3:
                    return
                rms_rstd(rstdb, psf[SSB], 512.0, Brstdb, PS[SSB])
                if KB < 4:
                    return
                for ch in range(4):
                    dve_(lambda e, ch=ch: e.scalar_tensor_tensor(out=cn[which][:, ch, :], in0=craw[which][:, ch, :],
                                                                 scalar=gl[:, ch:ch + 1], in1=rstdb, op0=ALU.mult, op1=ALU.mult),
                         reads=[Bcraw[which], Brstdb, Bg], pwrites=[Bcn[which]])

            KST = int(os.environ.get('KSTAGE', '99'))
            def normT(t):
                for sub in range(4):
                    r0 = t * TT + sub * 128
                    b = sub % 2
                    norm_transpose(lambda xs_, Bxs_, r0=r0: P.dma("sp", xs_, x[r0:r0 + 128, :], Bxs_, writes=[Bxs_]),
                                   gpre, Bg, uTs[t % 2], BuTs[t % 2], xs[b], Bxs[b], xn[b], Bxn[b], ssq[b], Bss[b], tr_banks, sub)

            for t in range(NT if KST >= 0 else 0):
                tok = slice(t * TT, (t + 1) * TT)
                P.dma("sp", cst[0][:64], cos_s[:, tok], Bcst, writes=[Bcst])
                P.dma("sp", cst[1][:64], sin_s[:, tok], Bcst, pwrites=[Bcst])
                uT, BuT = uTs[t % 2], BuTs[t % 2]
                if t == 0:
                    normT(0)
                if KST < 1:
                    continue
                latent_norm(0, gq)
                latent_norm(1, gkv)
                for h in range(NH if KB >= 5 else 0):
                    bank = mm_banks.next()
                    mm_group(psf[bank], [(wuq[:, kc, h * 192:h * 192 + 128], cn[0][:, kc, :]) for kc in range(4)], [Bw, Bcn[0]], PS[bank])
                    si = stg_rot.next()
                    act_(lambda e, bank=bank, si=si: e.activation(out=stg[si], in_=psf[bank], func=AF.Copy), reads=[PS[bank]], writes=[Bstg[si]])
                    store(qn_s[h, :, tok], stg[si], Bstg[si])
                    RA, RB = mm_banks.next(), mm_banks.next()
                    mm_group(psf[RA][:64], [(wuq[:, kc, h * 192 + 128:h * 192 + 192], cn[0][:, kc, :]) for kc in range(4)], [Bw, Bcn[0]], PS[RA])
                    mm_group(psf[RB][:64], [(wuqr[:, kc, h * 64:h * 64 + 64], cn[0][:, kc, :]) for kc in range(4)], [Bw, Bcn[0]], PS[RB])
                    rope_out(qr_s[h, :, tok], RA, RB)
                if KST < 2:
                    continue
                for h in range(NH):
                    bank = mm_banks.next()
                    mm_group(psf[bank], [(wukv[:, kc, h * 256:h * 256 + 128], cn[1][:, kc, :]) for kc in range(4)], [Bw, Bcn[1]], PS[bank])
                    si = stg_rot.next()
                    act_(lambda e, bank=bank, si=si: e.activation(out=stg[si], in_=psf[bank], func=AF.Copy), reads=[PS[bank]], writes=[Bstg[si]])
                    store(kn_s[h, :, tok], stg[si], Bstg[si])
                for sub in range(4):
                    vb = sub % 2
                    for half in range(2):
                        bank = mm_banks.next()
                        mm_group(psf[bank].rearrange("p (h c) -> p h c", c=128),
                                 [(cn[1][:, kc, sub * 128:(sub + 1) * 128], wukv_h[:, kc, half * 4:half * 4 + 4, 128:256]) for kc in range(4)],
                                 [Bw, Bcn[1]], PS[bank])
                        if half == 0:
                            act_(lambda e, bank=bank, vb=vb: e.activation(out=vst[vb][:, 0:512], in_=psf[bank], func=AF.Copy),
                                 reads=[PS[bank]], writes=[Bvst[vb]])
                        else:
                            dve_(lambda e, bank=bank, vb=vb: e.tensor_copy(out=vst[vb][:, 512:1024], in_=psf[bank]),
                                 reads=[PS[bank]], pwrites=[Bvst[vb]])
                    r0 = t * TT + sub * 128
                    store(v_s[r0:r0 + 128, :], vst[vb], Bvst[vb])
                if t + 1 < NT:
                    normT(t + 1)
                if KST < 3:
                    continue
                wt, Bwt = ws.get()
                RA, RB = mm_banks.next(), mm_banks.next()
                mm_group(psf[RA][:64], [(wt[:, kc, 0:64], uT[:, kc, :]) for kc in range(16)], [Bwt, BuT], PS[RA])
                mm_group(psf[RB][:64], [(wt[:, kc, 64:128], uT[:, kc, :]) for kc in range(16)], [Bwt, BuT], PS[RB])
                ws.release()
                rope_out(kr_s[:, tok], RA, RB)
                if KST < 4:
                    continue
                for i in range(4):
                    wt, Bwt = ws.get()
                    for sub in range(4):
                        bank = mm_banks.next()
                        mm_group(psf[bank][:, 0:256], [(uT[:, kc, sub * 128:(sub + 1) * 128], wt[:, kc, :]) for kc in range(16)],
                                 [Bwt, BuT], PS[bank])
                        act_(lambda e, bank=bank, sub=sub, i=i: e.activation(out=zst[:, sub, i * 256:(i + 1) * 256], in_=psf[bank][:, 0:256], func=AF.Silu),
                             reads=[PS[bank]], writes=[Bzst[sub]] if i == 0 else [], pwrites=[Bzst[sub]] if i > 0 else [])
                    ws.release()
                for sub in range(4):
                    r0 = t * TT + sub * 128
                    store(zs_s[r0:r0 + 128, :], zst[:, sub, :], Bzst[sub])
                if KST < 5:
                    continue
                for i in range(6):
                    wt, Bwt = ws.get()
                    for cc in range(2):
                        c = 2 * i + cc
                        xb = c % 2
                        bank = mm_banks.next()
                        mm_group(psf[bank], [(wt[:, kc, cc * 128:(cc + 1) * 128], uT[:, kc, :]) for kc in range(16)], [Bwt, BuT], PS[bank])
                        act_(lambda e, bank=bank, xb=xb: e.activation(out=xraw[xb][:, 3:TT + 3], in_=psf[bank], func=AF.Copy),
                             reads=[PS[bank]], writes=[Bxraw[xb]])
                        dve_(lambda e, xb=xb, c=c: e.tensor_copy(out=xraw[xb][:, 0:3], in_=halo[:, c, :]), reads=[Bhalo], pwrites=[Bxraw[xb]])
                        dve_(lambda e, xb=xb, c=c: e.tensor_copy(out=halo[:, c, :], in_=xraw[xb][:, TT:TT + 3]), reads=[Bxraw[xb]], pwrites=[Bhalo])
                        dve_(lambda e, xb=xb, c=c: e.tensor_scalar(out=acc[xb], in0=xraw[xb][:, 0:TT], scalar1=cwb[:, c, 0:1], scalar2=None, op0=ALU.mult),
                             reads=[Bxraw[xb], Bg], writes=[Bacc[xb]])
                        for k in (1, 2, 3):
                            dve_(lambda e, xb=xb, c=c, k=k: e.scalar_tensor_tensor(out=acc[xb], in0=xraw[xb][:, k:TT + k], scalar=cwb[:, c, k:k + 1],
                                                                                   in1=acc[xb], op0=ALU.mult, op1=ALU.add),
                                 reads=[Bxraw[xb], Bg, Bacc[xb]], writes=[Bacc[xb]])
                        si = stg_rot.next()
                        act_(lambda e, xb=xb, c=c, si=si: e.activation(out=stg[si], in_=acc[xb], func=AF.Silu, bias=cwb[:, c, 4:5]),
                             reads=[Bacc[xb], Bg], writes=[Bstg[si]])
                        store(xbc_s[c * 128:(c + 1) * 128, tok], stg[si], Bstg[si])
                    ws.release()
                if KST < 6:
                    continue
                wt, Bwt = ws.get()
                for sub in range(4):
                    bank = mm_banks.next()
                    mm_group(psf[bank][:, 0:16], [(uT[:, kc, sub * 128:(sub + 1) * 128], wt[:, kc, 0:16]) for kc in range(16)], [Bwt, BuT], PS[bank])
                    dttmp, Bdttmp = dttmp4[sub], Bdttmp4[sub]
                    dve_(lambda e, bank=bank, dttmp=dttmp: e.tensor_tensor(out=dttmp, in0=psf[bank][:, 0:16], in1=small[:, 0:16], op=ALU.add),
                         reads=[PS[bank], Bg], writes=[Bdttmp])
                    act_(lambda e, dttmp=dttmp: e.activation(out=dttmp, in_=dttmp, func=AF.Exp), reads=[Bdttmp], writes=[Bdttmp])
                    act_(lambda e, sub=sub, dttmp=dttmp: e.activation(out=dtst[:, sub, :], in_=dttmp, func=AF.Ln, bias=1.0), reads=[Bdttmp],
                         writes=[Bdtst] if sub == 0 else [], pwrites=[Bdtst] if sub > 0 else [])
                ws.release()
                store(dt_s[tok, :].rearrange("(s p) h -> p s h", p=128), dtst, Bdtst)
            P.barrier()
            sb.reset(m)

        phase1a()

        def bcast_mid(ap2, k):
            n = ap2.shape[1]
            return ap2.unsqueeze(1).to_broadcast([128, k, n])

        def bcast_last(ap2, k):
            n = ap2.shape[1]
            return ap2.unsqueeze(2).to_broadcast([128, n, k])

        def ssd_setup():
            tri = sb.alloc([128], F32)
            negm = sb.alloc([128], F32)
            onesf = sb.alloc([128], F32)
            small = sb.alloc([48], F32)
            aneg = sb.alloc([16], F32)
            gssd = sb.alloc([1024], F32)
            Bc = Buf("c1b")
            P.dma("sp", tri, c_tri, Bc, pwrites=[Bc])
            P.dma("sp", negm, c_negm, Bc, pwrites=[Bc])
            P.dma("sp", small, ssd_small, Bc, pwrites=[Bc])
            P.dma("sp", gssd, g_ssd, Bc, pwrites=[Bc])
            dve_(lambda e: e.memset(onesf, 1.0), pwrites=[Bc])
            act_(lambda e: e.activation(out=aneg, in_=small[:, 16:32], func=AF.Exp), reads=[Bc], pwrites=[Bc])
            dve_(lambda e: e.tensor_scalar(out=aneg, in0=aneg, scalar1=-1.0, scalar2=None, op0=ALU.mult), reads=[Bc], pwrites=[Bc])
            dskip = small[:, 32:48]
            hT = sb.alloc([1024], F32)
            hTb = sb.alloc([1024], BF16)
            Bh, Bhb = Buf("hT"), Buf("hTb")
            dve_(lambda e: e.memset(hT, 0.0), writes=[Bh])
            dve_(lambda e: e.memset(hTb, 0.0), writes=[Bhb])
            xbcT = [sb.alloc([12, 128], BF16) for _ in range(3)]
            zs = [sb.alloc([1024], BF16) for _ in range(3)]
            dtt = [sb.alloc([16], F32) for _ in range(3)]
            Bin = [Buf("in0"), Buf("in1"), Buf("in2")]
            tri_bf = sb.alloc([128], BF16)
            dve_(lambda e: e.tensor_copy(out=tri_bf, in_=tri), reads=[Bc], pwrites=[Bc])
            a_hi = sb.alloc([16], BF16); a_lo = sb.alloc([16], BF16); a_hf = sb.alloc([16], F32); Bahl = Buf("ahl")
            rhsA_hi = sb.alloc([16, 128], BF16); rhsA_lo = sb.alloc([16, 128], BF16)
            a_t = sb.alloc([16], F32); Ba = Buf("a")
            acol = sb.alloc([16], F32); Bacol = Buf("acol")
            BrhsA = Buf("rhsA")
            arow = sb.alloc([16, 128], F32); Barow = Buf("arow")
            tmp = sb.alloc([16, 128], F32); Btmp = Buf("tmp")
            cbt = sb.alloc([2, 128], F32); Bcbt = Buf("cbt")
            xtok = sb.alloc([16, 64], BF16); Bxtok = Buf("xtok")
            MT2 = [sb.alloc([16, 128], BF16) for _ in range(2)]; BMT2 = [Buf("MT0"), Buf("MT1")]
            btok2 = [sb.alloc([256], BF16) for _ in range(2)]; Bbtok2 = [Buf("btok0"), Buf("btok1")]
            xdt2 = [sb.alloc([16, 64], BF16) for _ in range(2)]; Bxdt2 = [Buf("xdt0"), Buf("xdt1")]
            xdtw2 = [sb.alloc([16, 64], BF16) for _ in range(2)]; Bxdtw2 = [Buf("xdtw0"), Buf("xdtw1")]
            sm2 = [sb.alloc([4, 16], F32) for _ in range(2)]; Bsm2 = [Buf("sm0"), Buf("sm1")]
            xD2 = [sb.alloc([16, 64], F32) for _ in range(2)]; BxD2 = [Buf("xD0"), Buf("xD1")]
            y = sb.alloc([16, 64], F32); By = Buf("y")
            ss2 = sb.alloc([2], F32); Bss2 = Buf("ss2")
            junk = sb.alloc([512], BF16); Bjunk = Buf("junk")
            ssm = sb.alloc([1024], BF16); Bssm = Buf("ssm")
            sst = [sb.alloc([8, 128], BF16) for _ in range(2)]; Bsst = [Buf("sst0"), Buf("sst1")]
            xbc_v = xbc_s.rearrange("(c p) s -> p c s", p=128)
            ssmT_v = ssmT_s.rearrange("(c p) s -> p c s", p=128)

            def load(c):
                b = c % 3
                tok = slice(c * 128, (c + 1) * 128)
                P.dma("sp", xbcT[b], xbc_v[:, :, tok], Bin[b], writes=[Bin[b]])
                P.dma("sp", zs[b], zs_s[tok, :], Bin[b], pwrites=[Bin[b]])
                P.dma("sp", dtt[b], dt_s[tok, :], Bin[b], pwrites=[Bin[b]])

            def front(c):
                p = c % 2
                X, DT, BI = xbcT[c % 3], dtt[c % 3], Bin[c % 3]
                MT, BMT, btok, Bbtok = MT2[p], BMT2[p], btok2[p], Bbtok2[p]
                xdt, Bxdt, xdtw, Bxdtw, sm, Bsm, xD, BxD = xdt2[p], Bxdt2[p], xdtw2[p], Bxdtw2[p], sm2[p], Bsm2[p], xD2[p], BxD2[p]
                dve_(lambda e: e.tensor_tensor(out=a_t, in0=DT, in1=aneg, op=ALU.mult), reads=[BI, Bc], writes=[Ba])
                yield
                mm_group(psf[1][:, 0:16], [(tri, a_t)], [Bc, Ba], PS[1])
                dve_(lambda e: e.tensor_copy(out=a_hi, in_=a_t), reads=[Ba], writes=[Bahl])
                dve_(lambda e: e.tensor_copy(out=a_hf, in_=a_hi), reads=[Bahl], pwrites=[Bahl])
                dve_(lambda e: e.tensor_tensor(out=a_lo, in0=a_t, in1=a_hf, op=ALU.subtract), reads=[Ba, Bahl], pwrites=[Bahl])
                dve_(lambda e: e.tensor_copy(out=acol, in_=psf[1][:, 0:16]), reads=[PS[1]], writes=[Bacol])
                dve_(lambda e: e.tensor_tensor(out=rhsA_hi, in0=bcast_mid(tri_bf, 16), in1=bcast_last(a_hi, 128), op=ALU.mult),
                     reads=[Bc, Bahl], writes=[BrhsA])
                dve_(lambda e: e.tensor_tensor(out=rhsA_lo, in0=bcast_mid(tri_bf, 16), in1=bcast_last(a_lo, 128), op=ALU.mult),
                     reads=[Bc, Bahl], pwrites=[BrhsA])
                yield
                for g in range(2):
                    mm_group(psf[1][:, 128 + g * 128:128 + (g + 1) * 128], [(X[:, 8 + g, :], X[:, 10 + g, :])], [BI], PS[1])
                act_(lambda e: e.activation(out=cbt, in_=psf[1][:, 128:384], func=AF.Copy), reads=[PS[1]], writes=[Bcbt])
                P.deps("pe", reads=[BI, B_const], writes=[PS[0]])
                for j in range(8):
                    fn = lambda e, j=j: e.transpose(psb[0][:, j * 128:(j + 1) * 128], X[:, j, :], ident)
                    if j == 7:
                        P.commit("pe", fn, reads=[BI, B_const], writes=[PS[0]])
                    else:
                        P.emit("pe", fn)
                act_(lambda e: e.activation(out=xtok, in_=psb[0], func=AF.Copy), reads=[PS[0]], writes=[Bxtok])
                yield
                for q4 in range(4):
                    hs4 = slice(q4 * 4, (q4 + 1) * 4)
                    mm_group(psf[1], [(ones, rhsA_hi[:, hs4, :]), (ones, rhsA_lo[:, hs4, :])], [B_const, BrhsA], PS[1])
                    act_(lambda e, hs4=hs4: e.activation(out=arow[:, hs4, :], in_=psf[1], func=AF.Copy),
                         reads=[PS[1]], pwrites=[Barow])
                    yield
                P.deps("pe", reads=[BI, B_const], writes=[PS[0]])
                P.emit("pe", lambda e: e.transpose(psb[0][:, 0:128], X[:, 8, :], ident))
                P.commit("pe", lambda e: e.transpose(psb[0][:, 128:256], X[:, 9, :], ident), reads=[BI, B_const], writes=[PS[0]])
                dve_(lambda e: e.tensor_copy(out=btok, in_=psb[0][:, 0:256]), reads=[PS[0]], writes=[Bbtok])
                dve_(lambda e: e.tensor_tensor(out=tmp, in0=arow, in1=bcast_last(acol, 128), op=ALU.subtract),
                     reads=[Barow, Bacol], writes=[Btmp])
                dve_(lambda e: e.tensor_tensor(out=tmp, in0=tmp, in1=bcast_mid(negm, 16), op=ALU.add),
                     reads=[Btmp, Bc], writes=[Btmp])
                yield
                act_(lambda e: e.activation(out=tmp, in_=tmp, func=AF.Exp), reads=[Btmp], writes=[Btmp])
                alast = arow[:, :, 127]
                dve_(lambda e: e.tensor_tensor(out=sm[:, 3, :], in0=alast, in1=acol, op=ALU.subtract), reads=[Barow, Bacol], writes=[Bsm])
                act_(lambda e: e.activation(out=sm[:, 0, :], in_=sm[:, 3, :], func=AF.Exp), reads=[Bsm], pwrites=[Bsm])
                act_(lambda e: e.activation(out=sm[:, 1, :], in_=acol, func=AF.Exp), reads=[Bacol], pwrites=[Bsm])
                act_(lambda e: e.activation(out=sm[:, 2, :], in_=alast, func=AF.Exp), reads=[Barow], pwrites=[Bsm])
                dve_(lambda e: e.tensor_tensor(out=xdt, in0=xtok, in1=bcast_last(DT, 64), op=ALU.mult), reads=[Bxtok, BI], writes=[Bxdt])
                dve_(lambda e: e.tensor_tensor(out=xD, in0=xtok, in1=bcast_last(dskip, 64), op=ALU.mult), reads=[Bxtok, Bc], writes=[BxD])
                yield
                dve_(lambda e: e.tensor_tensor(out=xdtw, in0=xdt, in1=bcast_last(sm[:, 0, :], 64), op=ALU.mult), reads=[Bxdt, Bsm], writes=[Bxdtw])
                for g in range(2):
                    dve_(lambda e, g=g: e.tensor_tensor(out=MT[:, g * 8:(g + 1) * 8, :], in0=tmp[:, g * 8:(g + 1) * 8, :],
                                                        in1=bcast_mid(cbt[:, g, :], 8), op=ALU.mult),
                         reads=[Btmp, Bcbt], writes=[BMT] if g == 0 else [], pwrites=[BMT] if g else [])
                yield

            def back(c):
                p = c % 2
                b = c % 2
                tok = slice(c * 128, (c + 1) * 128)
                X, Z, BI = xbcT[c % 3], zs[c % 3], Bin[c % 3]
                MT, BMT, btok, Bbtok = MT2[p], BMT2[p], btok2[p], Bbtok2[p]
                xdt, Bxdt, xdtw, Bxdtw, sm, Bsm, xD, BxD = xdt2[p], Bxdt2[p], xdtw2[p], Bxdtw2[p], sm2[p], Bsm2[p], xD2[p], BxD2[p]
                YB = 7
                for g in range(2):
                    hs = slice(g * 8, (g + 1) * 8)
                    mm_group(psf[YB], [(X[:, 10 + g, :], hTb[:, g * 512:(g + 1) * 512])], [BI, Bhb], PS[YB])
                    dve_(lambda e, hs=hs: e.tensor_tensor(out=y[:, hs, :], in0=psf[YB].rearrange("p (h q) -> p h q", q=64),
                                                          in1=bcast_last(sm[:, 1, hs], 64), op=ALU.mult), reads=[PS[YB], Bsm], pwrites=[By])
                    yield
                    P.deps("pe", reads=[BMT, Bxdt], writes=[PS[YB]])
                    for e8 in range(8):
                        h = g * 8 + e8
                        fn = lambda e, h=h, e8=e8: e.matmul(psf[YB][:, e8 * 64:(e8 + 1) * 64], MT[:, h, :], xdt[:, h, :], start=True, stop=True)
                        if e8 == 7:
                            P.commit("pe", fn, reads=[BMT, Bxdt], writes=[PS[YB]])
                        else:
                            P.emit("pe", fn)
                    dve_(lambda e, hs=hs: e.tensor_tensor(out=y[:, hs, :], in0=psf[YB].rearrange("p (h q) -> p h q", q=64),
                                                          in1=y[:, hs, :], op=ALU.add), reads=[PS[YB], By], pwrites=[By])
                    yield
                dve_(lambda e: e.tensor_tensor(out=y, in0=y, in1=xD, op=ALU.add), reads=[By, BxD], writes=[By])
                hT3 = hT.rearrange("p (h q) -> p h q", q=64)
                dve_(lambda e: e.tensor_tensor(out=hT3, in0=hT3, in1=bcast_last(sm[:, 2, :], 64), op=ALU.mult), reads=[Bh, Bsm], writes=[Bh])
                for g in range(2):
                    mm_group(psf[YB], [(btok[:, g * 128:(g + 1) * 128], xdtw[:, g * 8:(g + 1) * 8, :])], [Bbtok, Bxdtw], PS[YB])
                    dve_(lambda e, g=g: e.tensor_tensor(out=hT[:, g * 512:(g + 1) * 512], in0=psf[YB], in1=hT[:, g * 512:(g + 1) * 512], op=ALU.add),
                         reads=[PS[YB], Bh], writes=[Bh])
                    yield
                act_(lambda e: e.activation(out=hTb, in_=hT, func=AF.Copy), reads=[Bh], writes=[Bhb])
                y2 = y.rearrange("p h q -> p (h q)")
                dve_(lambda e: e.tensor_tensor(out=y2, in0=y2, in1=Z, op=ALU.mult), reads=[By, BI], writes=[By])
                dve_(lambda e: e.memset(ss2, 0.0), writes=[Bss2])
                for g in range(2):
                    act_(lambda e, g=g: e.activation(out=junk, in_=y2[:, g * 512:(g + 1) * 512], func=AF.Square, accum_out=ss2[:, g:g + 1]),
                         reads=[By, Bss2], writes=[Bjunk], pwrites=[Bss2])
                yield
                rms_rstd(ss2, ss2, 512.0, Bss2, Bss2)
                for g in range(2):
                    dve_(lambda e, g=g: e.scalar_tensor_tensor(out=ssm[:, g * 512:(g + 1) * 512], in0=y2[:, g * 512:(g + 1) * 512], scalar=ss2[:, g:g + 1],
                                                               in1=gssd[:, g * 512:(g + 1) * 512], op0=ALU.mult, op1=ALU.mult),
                         reads=[By, Bss2, Bc], pwrites=[Bssm])
                yield
                P.deps("pe", reads=[Bssm, B_const], writes=[PS[0]])
                for j in range(8):
                    fn = lambda e, j=j: e.transpose(psb[0][:, j * 128:(j + 1) * 128], ssm[:, j * 128:(j + 1) * 128], ident)
                    if j == 7:
                        P.commit("pe", fn, reads=[Bssm, B_const], writes=[PS[0]])
                    else:
                        P.emit("pe", fn)
                act_(lambda e: e.activation(out=sst[b], in_=psb[0].rearrange("p (c t) -> p c t", t=128), func=AF.Copy),
                     reads=[PS[0]], writes=[Bsst[b]])
                P.dma("sp", ssmT_v[:, :, tok], sst[b], Bsst[b], reads=[Bsst[b]])
                yield

            def gen():
                load(0)
                if NCH > 1:
                    load(1)
                yield from front(0)
                for c in range(NCH):
                    if c + 2 < NCH:
                        load(c + 2)
                    if c + 1 < NCH:
                        yield from front(c + 1)
                    yield from back(c)
            return gen()


        def attn_setup():
            scale = 192.0 ** -0.5
            cmask = sb.alloc([4, 512], BF16)
            gat = sb.alloc([8], F32)
            Bc = Buf("c2")
            P.dma("sp", cmask, c_cmask.rearrange("p (d q) -> p d q", q=512), Bc, pwrites=[Bc])
            P.dma("sp", gat, g_attn, Bc, pwrites=[Bc])
            krT = sb.alloc([S], BF16)
            Bkr = Buf("krT")
            P.dma("sp", krT[:64], kr_s, Bkr, writes=[Bkr])
            KT = [sb.alloc([S], BF16) for _ in range(2)]
            V = [sb.alloc([NCH, 128], BF16) for _ in range(2)]
            Qn = [sb.alloc([TT], BF16) for _ in range(2)]
            Qr = [sb.alloc([TT], BF16) for _ in range(2)]
            Bin = [Buf("a_in0"), Buf("a_in1")]
            PT = [sb.alloc([TT], BF16) for _ in range(5)]
            BPT = [Buf(f"PT{i}") for i in range(5)]
            pt_rot = Rot([0, 1, 2, 3, 4])
            attn = sb.alloc([8, TT], F32)
            Battn = Buf("attn")
            recip = sb.alloc([TT], F32)
            Brecip = Buf("recip")
            sq = [sb.alloc([TT], BF16) for _ in range(2)]
            Bsq = [Buf("asq0"), Buf("asq1")]
            rstdb = sb.alloc([TT], F32)
            Brstdb = Buf("arstd")
            outst = sb.alloc([8, TT], BF16)
            Boutst = Buf("outst")
            v_v = v_s.rearrange("(c p) f -> p c f", p=128)
            attnT_v = attnT_s.rearrange("(h p) s -> p h s", p=128)
            s_rot = Rot([2, 3, 4])
            OB, SB_, SSB = 5, 6, 1

            def load(j, h, b):
                nk = (j + 1) * TT
                tq = slice(j * TT, (j + 1) * TT)
                P.dma("sp", KT[b][:, 0:nk], kn_s[h, :, 0:nk], Bin[b], writes=[Bin[b]])
                P.dma("sp", V[b][:, 0:nk // 128, :], v_v[:, 0:nk // 128, h * 128:(h + 1) * 128], Bin[b], pwrites=[Bin[b]])
                P.dma("sp", Qn[b], qn_s[h, :, tq], Bin[b], pwrites=[Bin[b]])
                P.dma("sp", Qr[b][:64], qr_s[h, :, tq], Bin[b], pwrites=[Bin[b]])

            jobs = [(j, h) for j in range(NT) for h in range(NH)]
            load(jobs[0][0], jobs[0][1], 0)
            LOOK = 2
            pend = []

            def stageA(idx, kb):
                j, h = jobs[idx]
                b = idx % 2
                d = kb - 4 * j
                q0 = d * 128 if d > 0 else 0
                ks = slice(kb * 128, (kb + 1) * 128)
                sbank = s_rot.next()
                mm_group(psf[sbank][:, q0:TT], [(KT[b][:, ks], Qn[b][:, q0:TT]), (krT[:64, ks], Qr[b][:64, q0:TT])],
                         [Bin[b], Bkr], PS[sbank])
                pi = pt_rot.next()
                act_(lambda e: e.activation(out=PT[pi][:, q0:TT], in_=psf[sbank][:, q0:TT], func=AF.Exp, scale=scale),
                     reads=[PS[sbank]], writes=[BPT[pi]])
                if d >= 0:
                    dve_(lambda e: e.tensor_tensor(out=PT[pi][:, q0:q0 + 128], in0=PT[pi][:, q0:q0 + 128], in1=cmask[:, 0, 0:128], op=ALU.mult),
                         reads=[BPT[pi], Bc], writes=[BPT[pi]])
                return (idx, kb, pi, q0)

            def stageB(idx, kb, pi, q0):
                j, h = jobs[idx]
                b = idx % 2
                nkb = 4 * (j + 1)
                first, last = (kb == 0), (kb == nkb - 1)
                P.deps("pe", reads=[BPT[pi], Bin[b], B_const], writes=[PS[OB], PS[SB_]] if first else [])
                P.emit("pe", lambda e: e.matmul(psf[OB][:, q0:TT], V[b][:, kb, :], PT[pi][:, q0:TT], start=first, stop=last))
                P.commit("pe", lambda e: e.matmul(psf[SB_][:, q0:TT], ones, PT[pi][:, q0:TT], start=first, stop=last),
                         reads=[BPT[pi], Bin[b], B_const], writes=[PS[OB], PS[SB_]] if last else [], pwrites=[] if last else [PS[OB], PS[SB_]])
                if last:
                    finish(idx)

            def finish(idx):
                j, h = jobs[idx]
                dve_(lambda e: e.reciprocal(out=recip, in_=psf[SB_]), reads=[PS[SB_]], writes=[Brecip])
                dve_(lambda e: e.tensor_tensor(out=attn[:, h, :], in0=psf[OB], in1=recip, op=ALU.mult), reads=[PS[OB], Brecip], pwrites=[Battn])
                if h == NH - 1:
                    tq = slice(j * TT, (j + 1) * TT)
                    for hh in range(NH):
                        act_(lambda e, hh=hh: e.activation(out=sq[hh % 2], in_=attn[:, hh, :], func=AF.Square), reads=[Battn], writes=[Bsq[hh % 2]])
                        P.deps("pe", reads=[Bsq[hh % 2], B_const], writes=[PS[SSB]] if hh == 0 else [])
                        P.commit("pe", lambda e, hh=hh: e.matmul(psf[SSB], ones, sq[hh % 2], start=(hh == 0), stop=(hh == NH - 1)),
                                 reads=[Bsq[hh % 2], B_const], writes=[PS[SSB]] if hh == NH - 1 else [], pwrites=[PS[SSB]] if hh < NH - 1 else [])
                    rms_rstd(rstdb, psf[SSB], 1024.0, Brstdb, PS[SSB])
                    for hh in range(NH):
                        dve_(lambda e, hh=hh: e.scalar_tensor_tensor(out=outst[:, hh, :], in0=attn[:, hh, :], scalar=gat[:, hh:hh + 1], in1=rstdb,
                                                                     op0=ALU.mult, op1=ALU.mult), reads=[Battn, Bc, Brstdb],
                             writes=[Boutst] if hh == 0 else [], pwrites=[Boutst] if hh > 0 else [])
                    P.dma("sp", attnT_v[:, :, tq], outst, Boutst, reads=[Boutst])

            def run(tick):
                for idx, (j, h) in enumerate(jobs):
                    for kb in range(4 * (j + 1)):
                        pend.append(stageA(idx, kb))
                        if len(pend) > LOOK:
                            stageB(*pend.pop(0))
                        if kb == LOOK - 1 and idx + 1 < len(jobs):
                            load(jobs[idx + 1][0], jobs[idx + 1][1], (idx + 1) % 2)
                        tick()
                while pend:
                    stageB(*pend.pop(0))
            return run

        def phase12():
            m = sb.mark()
            ssd = ssd_setup()
            run = attn_setup()
            nblocks = sum(4 * (j + 1) for j in range(NT)) * NH
            nyield = NCH * 19 + 8
            st = {"acc": 0.0, "done": False}

            def tick():
                st["acc"] += nyield / nblocks
                while st["acc"] >= 1.0 and not st["done"]:
                    st["acc"] -= 1.0
                    try:
                        next(ssd)
                    except StopIteration:
                        st["done"] = True
            run(tick)
            while not st["done"]:
                try:
                    next(ssd)
                except StopIteration:
                    st["done"] = True
            P.barrier()
            sb.reset(m)

        if int(os.environ.get("KPH", "9")) >= 3:
            phase12()

        h1_s = dscr("h1_s", [S, D], F32)

        def phase3():
            m = sb.mark()
            gpm = sb.alloc([D], F32)
            gpf = sb.alloc([D], F32)
            gpo = sb.alloc([D], F32)
            Bg = Buf("g3")
            P.dma("sp", gpm, g_postmix, Bg, pwrites=[Bg])
            P.dma("sp", gpf, g_preffn, Bg, pwrites=[Bg])
            P.dma("sp", gpo, g_postffn, Bg, pwrites=[Bg])
            actT = sb.alloc([16, TT], BF16)
            BactT = Buf("actT")
            mix = sb.alloc([4, D], F32)
            Bmix = [Buf(f"mix{i}") for i in range(4)]
            xs2 = [sb.alloc([D], F32) for _ in range(2)]
            Bxs2 = [Buf("xs3a"), Buf("xs3b")]
            xn2 = [sb.alloc([D], BF16) for _ in range(2)]
            Bxn2 = [Buf("xn3a"), Buf("xn3b")]
            ssq2 = [sb.alloc([1], F32) for _ in range(2)]
            Bss2_ = [Buf("ss3a"), Buf("ss3b")]
            hid = sb.alloc([44, TT], BF16)
            Bhid = Buf("hid")
            sg = [sb.alloc([TT], F32) for _ in range(2)]
            Bsg = [Buf("sg0"), Buf("sg1")]
            w_out_v = w_out.rearrange("(kc p) n -> p kc n", p=128)
            w_gate_v = w_gate.rearrange("(kc p) n -> p kc n", p=128)
            w_up_v = w_up.rearrange("(kc p) n -> p kc n", p=128)
            w_down_v = w_down.rearrange("(kc p) n -> p kc n", p=128)
            L1 = []
            L2 = []
            for _t in range(NT):
                for i in range(8):
                    L1.append([(lambda sl: sl, w_out_v[:, :, i * 256:(i + 1) * 256])])
                for i in range(22):
                    L1.append([(lambda sl: sl, w_gate_v[:, :, i * 256:(i + 1) * 256])])
                    L1.append([(lambda sl: sl, w_up_v[:, :, i * 256:(i + 1) * 256])])
                for n in range(4):
                    for kg in range(4):
                        L2.append([(lambda sl: sl, w_down_v[:, kg * 11:(kg + 1) * 11, n * 512:(n + 1) * 512])])
            ws1 = WStream(4, [16, 256], L1)
            ws2 = WStream(2, [11, 512], L2)
            tr_banks = Rot([0, 1])
            mm_banks = Rot([2, 3, 4, 5, 6, 7])
            attnT_v = attnT_s.rearrange("(c p) s -> p c s", p=128)
            ssmT_v = ssmT_s.rearrange("(c p) s -> p c s", p=128)
            junk3 = sb.alloc([TT], BF16)
            Bjunk3 = Buf("junk3")
            ssacc = sb.alloc([4, 12], F32)
            Bssacc = Buf("ssacc")
            ev = [0]
            Bh1 = [Buf(f"h1_{i}") for i in range(4)]

            def evac(dst, src, reads, **kw):
                ev[0] += 1
                if ev[0] % 2:
                    act_(lambda e: e.activation(out=dst, in_=src, func=AF.Copy), reads=reads, **kw)
                else:
                    dve_(lambda e: e.tensor_copy(out=dst, in_=src), reads=reads, **kw)

            for t in range(NT):
                tok = slice(t * TT, (t + 1) * TT)
                if t == 0:
                    P.dma("sp", actT[:, 0:8, :], attnT_v[:, :, tok], BactT, writes=[BactT])
                    P.dma("sp", actT[:, 8:16, :], ssmT_v[:, :, tok], BactT, pwrites=[BactT])
                dve_(lambda e: e.memset(ssacc, 0.0), writes=[Bssacc])
                for i in range(8):
                    wt, Bwt = ws1.get()
                    for sub in range(4):
                        bank = mm_banks.next()
                        mm_group(psf[bank][:, 0:256], [(actT[:, kc, sub * 128:(sub + 1) * 128], wt[:, kc, :]) for kc in range(16)],
                                 [Bwt, BactT], PS[bank])
                        act_(lambda e, bank=bank, sub=sub, i=i: e.activation(out=mix[:, sub, i * 256:(i + 1) * 256], in_=psf[bank][:, 0:256], func=AF.Copy),
                             reads=[PS[bank]], pwrites=[Bmix[sub]])
                        act_(lambda e, bank=bank, sub=sub, i=i: e.activation(out=junk3[:, 0:256], in_=psf[bank][:, 0:256], func=AF.Square,
                                                                             accum_out=ssacc[:, sub, i:i + 1]),
                             reads=[PS[bank], Bssacc], writes=[Bjunk3], pwrites=[Bssacc])
                    ws1.release()
                for sub in range(4):
                    r0 = t * TT + sub * 128
                    M = mix[:, sub, :]
                    xs, Bxs, xn, Bxn, ssq, Bss = xs2[sub % 2], Bxs2[sub % 2], xn2[sub % 2], Bxn2[sub % 2], ssq2[sub % 2], Bss2_[sub % 2]
                    P.dma("sp", xs, x[r0:r0 + 128, :], Bxs, writes=[Bxs])
                    dve_(lambda e, ssq=ssq, sub=sub: e.reduce_sum(out=ssq, in_=ssacc[:, sub, 0:8], axis=AX.X), reads=[Bssacc], writes=[Bss])
                    rms_rstd(ssq, ssq, float(D), Bss, Bss)
                    dve_(lambda e, M=M, ssq=ssq: e.scalar_tensor_tensor(out=M, in0=M, scalar=ssq, in1=gpm, op0=ALU.mult, op1=ALU.mult),
                         reads=[Bmix[sub], Bss, Bg], writes=[Bmix[sub]])
                    dve_(lambda e, M=M, xs=xs: e.tensor_tensor(out=M, in0=M, in1=xs, op=ALU.add), reads=[Bmix[sub], Bxs], writes=[Bmix[sub]])
                    P.dma("sp", h1_s[r0:r0 + 128, :], M, Bmix[sub], reads=[Bmix[sub]], writes=[Bh1[sub]])
                    norm_transpose(None, gpf, Bg, actT, BactT, M, Bmix[sub], xn, Bxn, ssq, Bss, tr_banks, sub)
                for i in range(22):
                    wg, Bwg = ws1.get()
                    ws1.cons += 1
                    wu, Bwu = ws1.get()
                    ws1.cons -= 1
                    gbs = [mm_banks.next(), mm_banks.next()]
                    for cc in range(2):
                        mm_group(psf[gbs[cc]], [(wg[:, kc, cc * 128:(cc + 1) * 128], actT[:, kc, :]) for kc in range(16)], [Bwg, BactT], PS[gbs[cc]])
                    ws1.release()
                    for cc in range(2):
                        fc = 2 * i + cc
                        gb = gbs[cc]
                        ub = mm_banks.next()
                        mm_group(psf[ub], [(wu[:, kc, cc * 128:(cc + 1) * 128], actT[:, kc, :]) for kc in range(16)], [Bwu, BactT], PS[ub])
                        k2 = fc % 2
                        act_(lambda e, gb=gb, k2=k2: e.activation(out=sg[k2], in_=psf[gb], func=AF.Silu), reads=[PS[gb]], writes=[Bsg[k2]])
                        dve_(lambda e, ub=ub, k2=k2, fc=fc: e.tensor_tensor(out=hid[:, fc, :], in0=psf[ub], in1=sg[k2], op=ALU.mult),
                             reads=[PS[ub], Bsg[k2]], pwrites=[Bhid])
                    ws1.release()
                if t + 1 < NT:
                    tokn = slice((t + 1) * TT, (t + 2) * TT)
                    P.dma("sp", actT[:, 0:8, :], attnT_v[:, :, tokn], BactT, writes=[BactT])
                    P.dma("sp", actT[:, 8:16, :], ssmT_v[:, :, tokn], BactT, pwrites=[BactT])
                for n in range(4):
                    for kg in range(4):
                        wd, Bwd = ws2.get()
                        for sub in range(4):
                            bank = 4 + sub
                            P.deps("pe", reads=[Bwd, Bhid], writes=[PS[bank]] if kg == 0 else [])
                            for kk in range(11):
                                fcn = kg * 11 + kk
                                fn = (lambda e, bank=bank, fcn=fcn, kk=kk, sub=sub, wd=wd:
                                      e.matmul(psf[bank], hid[:, fcn, sub * 128:(sub + 1) * 128], wd[:, kk, :], start=(fcn == 0), stop=(fcn == 43)))
                                if kk == 10:
                                    P.commit("pe", fn, reads=[Bwd, Bhid], writes=[PS[bank]] if kg == 3 else [], pwrites=[PS[bank]] if kg < 3 else [])
                                else:
                                    P.emit("pe", fn)
                        ws2.release()
                    for sub in range(4):
                        act_(lambda e, sub=sub, n=n: e.activation(out=mix[:, sub, n * 512:(n + 1) * 512], in_=psf[4 + sub], func=AF.Copy),
                             reads=[PS[4 + sub]], pwrites=[Bmix[sub]])
                        act_(lambda e, sub=sub, n=n: e.activation(out=junk3, in_=psf[4 + sub], func=AF.Square, accum_out=ssacc[:, sub, 8 + n:9 + n]),
                             reads=[PS[4 + sub], Bssacc], writes=[Bjunk3], pwrites=[Bssacc])
                for sub in range(4):
                    r0 = t * TT + sub * 128
                    M = mix[:, sub, :]
                    xs, Bxs, xn, Bxn, ssq, Bss = xs2[sub % 2], Bxs2[sub % 2], xn2[sub % 2], Bxn2[sub % 2], ssq2[sub % 2], Bss2_[sub % 2]
                    P.dma("sp", xs, h1_s[r0:r0 + 128, :], Bxs, reads=[Bh1[sub]], writes=[Bxs])
                    dve_(lambda e, ssq=ssq, sub=sub: e.reduce_sum(out=ssq, in_=ssacc[:, sub, 8:12], axis=AX.X), reads=[Bssacc], writes=[Bss])
                    rms_rstd(ssq, ssq, float(D), Bss, Bss)
                    dve_(lambda e, M=M, ssq=ssq: e.scalar_tensor_tensor(out=M, in0=M, scalar=ssq, in1=gpo, op0=ALU.mult, op1=ALU.mult),
                         reads=[Bmix[sub], Bss, Bg], writes=[Bmix[sub]])
                    dve_(lambda e, M=M, xs=xs: e.tensor_tensor(out=M, in0=M, in1=xs, op=ALU.add), reads=[Bmix[sub], Bxs], writes=[Bmix[sub]])
                    P.dma("sp", out[r0:r0 + 128, :], M, Bmix[sub], reads=[Bmix[sub]])
            P.barrier()
            sb.reset(m)

        if int(os.environ.get("KPH", "9")) >= 4:
            phase3()

        outs_done = []

        P.barrier()

        sems = [es.enter_context(nc.semaphore(f"s{i}")) for i in range(len(P.cnt))]
        with nc.Block() as block:
            @block.tensor
            def _(e):
                P.replay("pe", e, sems)

            @block.scalar
            def _(e):
                P.replay("act", e, sems)

            @block.vector
            def _(e):
                P.replay("dve", e, sems)

            @block.gpsimd
            def _(e):
                P.replay("pool", e, sems)

            @block.sync
            def _(e):
                P.replay("sp", e, sems)
    return nc


def _consts():
    bf = ml_dtypes.bfloat16
    k = np.arange(128)
    ident = np.eye(128, dtype=np.float32).astype(bf)
    tri = (k[:, None] <= k[None, :]).astype(np.float32)
    negm = np.where(k[:, None] <= k[None, :], 0.0, -1e30).astype(np.float32)
    q = np.arange(512)
    cm = np.zeros((128, 4, 512), np.float32)
    for d in range(4):
        cm[:, d, :] = (q[None, :] >= d * 128 + k[:, None])
    cm = cm.reshape(128, 2048).astype(bf)
    inv_freq = (np.float32(10000.0) ** (-np.arange(0, 64, 2, dtype=np.float32) / np.float32(64))).astype(np.float32)
    rope = np.zeros((64, 2), np.float32)
    rope[:, 0] = np.concatenate([inv_freq, inv_freq])
    rope[:32, 1] = -1.0
    rope[32:, 1] = 1.0
    return dict(c_ident=ident, c_tri=tri, c_negm=negm, c_cmask=cm, c_rope=rope)


def make_in_maps(inputs, S, batches):
    f = lambda a: np.ascontiguousarray(np.asarray(a, dtype=np.float32))
    rep = lambda v, n=128: np.ascontiguousarray(np.broadcast_to(np.asarray(v, np.float32)[None, :], (n, len(v))))
    pp = lambda v: np.ascontiguousarray(np.asarray(v, np.float32).reshape(-1, 128).T)
    shared = dict(_consts())
    shared.update(
        w_in=f(inputs["w_in"][0]), w_uq=f(inputs["w_uq"][0]), w_ukv=f(inputs["w_ukv"][0]),
        w_out=f(inputs["w_out"][0]), w_gate=f(inputs["w_gate"][0]), w_up=f(inputs["w_up"][0]),
        w_down=f(inputs["w_down"][0]),
        g_pre=rep(inputs["pre_mix_norm_w"][0]), g_preffn=rep(inputs["pre_ffn_norm_w"][0]),
        g_postmix=rep(inputs["post_mix_norm_w"][0]), g_postffn=rep(inputs["post_ffn_norm_w"][0]),
        g_q=pp(inputs["q_norm_w"][0]), g_kv=pp(inputs["kv_norm_w"][0]), g_attn=pp(inputs["attn_out_norm_w"][0]),
        g_ssd=rep(inputs["ssd_norm_w"][0]),
    )
    cw = np.asarray(inputs["conv_w"][0], np.float32)
    cb = np.asarray(inputs["conv_b"][0], np.float32)
    cwb = np.concatenate([cw, cb[None, :]], axis=0)
    shared["conv_wb"] = np.ascontiguousarray(cwb.reshape(5, 12, 128).transpose(2, 1, 0).reshape(128, 60))
    small = np.concatenate([np.asarray(inputs["dt_bias"][0], np.float32), np.asarray(inputs["a_log"][0], np.float32),
                            np.asarray(inputs["d_skip"][0], np.float32)])
    shared["ssd_small"] = rep(small)
    maps = []
    for b in batches:
        m = dict(shared)
        m["x"] = f(inputs["x"][b][:S])
        pos = np.asarray(inputs["positions"][b][:S], np.int32)
        m["posb"] = np.ascontiguousarray(np.broadcast_to(pos[None, :], (64, S)))
        maps.append(m)
    return maps


def kernel(**inputs):
    S = inputs["x"].shape[1]
    B = inputs["x"].shape[0]
    nc = build_program(S)
    maps = make_in_maps(inputs, S, list(range(B)))
    res = run_bass_kernel_spmd(nc, maps, core_ids=list(range(B)))
    return np.stack([np.asarray(r["out"], dtype=np.float32) for r in res.results], axis=0)
```

```python
import math
import os
from contextlib import ExitStack

import numpy as np
import ml_dtypes

import concourse.bass as bass
import concourse.mybir as mybir
from concourse.bass_utils import run_bass_kernel_spmd

F32 = mybir.dt.float32
BF16 = mybir.dt.bfloat16
I32 = mybir.dt.int32
AF = mybir.ActivationFunctionType
ALU = mybir.AluOpType
AX = mybir.AxisListType

D = 2048
DIN = 3664
DFF = 5632
NH = 8
EPS = 1e-6
TT = 512

ENGS = ("pe", "act", "dve", "pool", "sp")


class Buf:
    __slots__ = ("name", "w", "r", "dsem", "excl")

    def __init__(self, name, excl=False):
        self.name = name
        self.excl = excl
        self.w = {}
        self.r = {}
        self.dsem = None


class Prog:
    def __init__(self):
        self.q = {e: [] for e in ENGS}
        self.cnt = []
        self.waited = {e: {} for e in ENGS}
        self.esem = {e: self.new_sem() for e in ENGS}

    def new_sem(self):
        self.cnt.append(0)
        return len(self.cnt) - 1

    def _wait(self, eng, k, v):
        if eng == "pe" and k == self.esem["pe"]:
            return
        if self.waited[eng].get(k, 0) >= v:
            return
        self.waited[eng][k] = v
        self.q[eng].append(("wait", k, v))

    def deps(self, eng, reads=(), writes=(), pwrites=()):
        for b in reads:
            for k, v in b.w.items():
                self._wait(eng, k, v)
            if b.excl:
                for k, v in b.r.items():
                    if k != self.esem[eng]:
                        self._wait(eng, k, v)
        for b in writes:
            for k, v in b.w.items():
                self._wait(eng, k, v)
            for k, v in b.r.items():
                self._wait(eng, k, v)
        for b in pwrites:
            for k, v in b.r.items():
                self._wait(eng, k, v)

    def emit(self, eng, fn):
        self.q[eng].append(("op", fn, None, 0))

    def _register(self, ev, reads, writes, pwrites):
        k, v = ev
        for b in reads:
            b.r[k] = max(b.r.get(k, 0), v)
        for b in writes:
            b.w = {k: v}
        for b in pwrites:
            b.w[k] = max(b.w.get(k, 0), v)

    def commit(self, eng, fn, reads=(), writes=(), pwrites=()):
        k = self.esem[eng]
        self.cnt[k] += 1
        self.q[eng].append(("op", fn, k, 1))
        self._register((k, self.cnt[k]), reads, writes, pwrites)

    def op(self, eng, fn, reads=(), writes=(), pwrites=()):
        self.deps(eng, reads, writes, pwrites)
        self.commit(eng, fn, reads, writes, pwrites)

    def dma(self, eng, out, in_, sb, reads=(), writes=(), pwrites=()):
        self.deps(eng, reads, writes, pwrites)
        if sb.dsem is None:
            sb.dsem = self.new_sem()
        k = sb.dsem
        self.cnt[k] += 16
        self.q[eng].append(("op", lambda e: e.dma_start(out=out, in_=in_), k, 16))
        self._register((k, self.cnt[k]), reads, writes, pwrites)

    def barrier(self):
        for e in ENGS:
            for k in range(len(self.cnt)):
                if self.cnt[k] > 0:
                    self._wait(e, k, self.cnt[k])

    def replay(self, eng, e, sems):
        for it in self.q[eng]:
            if it[0] == "wait":
                e.wait_ge(sems[it[1]], it[2])
            else:
                ins = it[1](e)
                if it[2] is not None:
                    ins.then_inc(sems[it[2]], it[3])


class SB:
    def __init__(self, big, nbytes):
        self.big = big
        self.nbytes = nbytes
        self.off = 0

    def mark(self):
        return self.off

    def reset(self, m):
        self.off = m

    def alloc(self, shape, dtype):
        n = int(np.prod(shape))
        esz = 4 if dtype in (F32, I32) else 2
        nb = n * esz
        self.off = (self.off + 63) // 64 * 64
        assert self.off + nb <= self.nbytes, f"SBUF overflow {self.off + nb} > {self.nbytes}"
        ap = self.big[:, self.off // 2:(self.off + nb) // 2]
        self.off += nb
        if esz == 4:
            ap = ap.bitcast(dtype)
        if len(shape) == 2:
            ap = ap.rearrange("p (a b) -> p a b", b=shape[1])
        elif len(shape) == 3:
            ap = ap.rearrange("p (a b c) -> p a b c", b=shape[1], c=shape[2])
        return ap


def build_program(S, debug=False):
    assert S % TT == 0
    NT = S // TT
    NCH = S // 128
    nc = bass.Bass("TRN2", target_bir_lowering=False)

    def din(name, shape, dt=F32):
        return nc.dram_tensor(name, list(shape), dt, kind="ExternalInput").ap()

    skind = "ExternalOutput" if debug else "Internal"

    def dscr(name, shape, dt):
        return nc.dram_tensor(name, list(shape), dt, kind=skind).ap()

    x = din("x", [S, D])
    posb = din("posb", [64, S], I32)
    w_in = din("w_in", [D, DIN])
    w_uq = din("w_uq", [512, 1536])
    w_ukv = din("w_ukv", [512, 2048])
    w_out = din("w_out", [D, D])
    w_gate = din("w_gate", [D, DFF])
    w_up = din("w_up", [D, DFF])
    w_down = din("w_down", [DFF, D])
    c_ident = din("c_ident", [128, 128], BF16)
    c_tri = din("c_tri", [128, 128])
    c_negm = din("c_negm", [128, 128])
    c_cmask = din("c_cmask", [128, 4 * 512], BF16)
    c_rope = din("c_rope", [64, 2])
    g_pre = din("g_pre", [128, D])
    g_preffn = din("g_preffn", [128, D])
    g_postmix = din("g_postmix", [128, D])
    g_postffn = din("g_postffn", [128, D])
    g_q = din("g_q", [128, 4])
    g_kv = din("g_kv", [128, 4])
    g_attn = din("g_attn", [128, 8])
    g_ssd = din("g_ssd", [128, 1024])
    conv_wb = din("conv_wb", [128, 12 * 5])
    ssd_small = din("ssd_small", [128, 48])
    out = nc.dram_tensor("out", [S, D], F32, kind="ExternalOutput").ap()

    qn_s = dscr("qn_s", [NH, 128, S], BF16)
    qr_s = dscr("qr_s", [NH, 64, S], BF16)
    kn_s = dscr("kn_s", [NH, 128, S], BF16)
    kr_s = dscr("kr_s", [64, S], BF16)
    v_s = dscr("v_s", [S, 1024], BF16)
    zs_s = dscr("zs_s", [S, 1024], BF16)
    xbc_s = dscr("xbc_s", [1536, S], BF16)
    dt_s = dscr("dt_s", [S, 16], F32)
    cos_s = dscr("cos_s", [64, S], F32)
    sin_s = dscr("sin_s", [64, S], F32)
    ssmT_s = dscr("ssmT_s", [1024, S], BF16)
    attnT_s = dscr("attnT_s", [1024, S], BF16)

    P = Prog()
    SBYTES = 207 * 1024

    with ExitStack() as es:
        big = es.enter_context(nc.sbuf_tensor("big", [128, SBYTES // 2], BF16))
        sb = SB(big, SBYTES)
        psum = [es.enter_context(nc.psum_tensor(f"ps{i}", [128, 1024], BF16) if i < 2 else nc.psum_tensor(f"ps{i}", [128, 512], F32))
                for i in range(8)]
        PS = [Buf(f"ps{i}", excl=True) for i in range(8)]
        psf = [(None if i < 2 else p[:]) for i, p in enumerate(psum)]
        psb = [(p[:] if i < 2 else None) for i, p in enumerate(psum)]

        ident = sb.alloc([128], BF16)
        ones = sb.alloc([128], BF16)
        B_const = Buf("const")
        P.dma("sp", ident, c_ident, B_const, writes=[B_const])
        P.op("dve", lambda e: e.memset(ones, 1.0), pwrites=[B_const])
        m_persist = sb.mark()

        class WStream:
            def __init__(self, nslots, shape, loads):
                self.nslots = nslots
                self.slots = [sb.alloc(shape, BF16) for _ in range(nslots)]
                self.bufs = [Buf(f"wslot{i}") for i in range(nslots)]
                self.loads = loads
                self.issued = 0
                self.cons = 0
                for _ in range(nslots):
                    self._issue()

            def _issue(self):
                i = self.issued
                if i >= len(self.loads):
                    return
                s = i % self.nslots
                for dst_fn, src in self.loads[i]:
                    P.dma("pool", dst_fn(self.slots[s]), src, self.bufs[s], pwrites=[self.bufs[s]])
                self.issued += 1

            def get(self):
                s = self.cons % self.nslots
                return self.slots[s], self.bufs[s]

            def release(self):
                self.cons += 1
                if int(os.environ.get('KNOREFILL', '0')):
                    return
                self._issue()

        def phase0():
            m = sb.mark()
            crope = sb.alloc([2], F32)
            Bc = Buf("crope")
            P.dma("sp", crope[:64], c_rope, Bc, writes=[Bc])
            C1 = 6.28125
            C2 = 2 * math.pi - C1
            PI_IN = 3.1415925
            CW = 1024

            def wrap(t, msk, Bt, Bm):
                P.op("dve", lambda e: e.tensor_scalar(out=msk[:64], in0=t[:64], scalar1=-math.pi, scalar2=None, op0=ALU.is_lt),
                     reads=[Bt], writes=[Bm])
                P.op("dve", lambda e: e.scalar_tensor_tensor(out=t[:64], in0=msk[:64], scalar=2 * math.pi, in1=t[:64],
                                                             op0=ALU.mult, op1=ALU.add), reads=[Bm, Bt], writes=[Bt])
                P.op("dve", lambda e: e.tensor_scalar(out=msk[:64], in0=t[:64], scalar1=math.pi, scalar2=None, op0=ALU.is_gt),
                     reads=[Bt], writes=[Bm])
                P.op("dve", lambda e: e.scalar_tensor_tensor(out=t[:64], in0=msk[:64], scalar=-2 * math.pi, in1=t[:64],
                                                             op0=ALU.mult, op1=ALU.add), reads=[Bm, Bt], writes=[Bt])
                P.op("dve", lambda e: e.tensor_scalar(out=t[:64], in0=t[:64], scalar1=PI_IN, scalar2=-PI_IN,
                                                      op0=ALU.min, op1=ALU.max), reads=[Bt], writes=[Bt])

            for c0 in range(0, S, CW):
                cw = min(CW, S - c0)
                pi_t = sb.alloc([cw], I32)
                ang = sb.alloc([cw], F32)
                t1 = sb.alloc([cw], F32)
                t2 = sb.alloc([cw], F32)
                msk = sb.alloc([cw], F32)
                ni = sb.alloc([cw], I32)
                Bp, Ba, B1, B2, Bm, Bn = Buf("pi"), Buf("ang"), Buf("t1"), Buf("t2"), Buf("msk"), Buf("ni")
                P.dma("sp", pi_t[:64], posb[:, c0:c0 + cw], Bp, writes=[Bp])
                P.op("dve", lambda e, a=ang, p=pi_t: e.tensor_copy(out=a[:64], in_=p[:64]), reads=[Bp], writes=[Ba])
                P.op("dve", lambda e, a=ang: e.tensor_scalar(out=a[:64], in0=a[:64], scalar1=crope[:64, 0:1], scalar2=None,
                                                             op0=ALU.mult), reads=[Ba, Bc], writes=[Ba])
                P.op("dve", lambda e, a=ang, t=t1: e.tensor_scalar(out=t[:64], in0=a[:64], scalar1=1.0 / (2 * math.pi), scalar2=0.5,
                                                                   op0=ALU.mult, op1=ALU.add), reads=[Ba], writes=[B1])
                P.op("dve", lambda e, t=t1, n=ni: e.tensor_copy(out=n[:64], in_=t[:64]), reads=[B1], writes=[Bn])
                P.op("dve", lambda e, t=t2, n=ni: e.tensor_copy(out=t[:64], in_=n[:64]), reads=[Bn], writes=[B2])
                P.op("dve", lambda e, a=ang, t=t1, n=t2: e.scalar_tensor_tensor(out=t[:64], in0=n[:64], scalar=-C1, in1=a[:64],
                                                                                op0=ALU.mult, op1=ALU.add), reads=[B2, Ba], writes=[B1])
                P.op("dve", lambda e, t=t1, n=t2: e.scalar_tensor_tensor(out=t[:64], in0=n[:64], scalar=-C2, in1=t[:64],
                                                                         op0=ALU.mult, op1=ALU.add), reads=[B2, B1], writes=[B1])
                wrap(t1, msk, B1, Bm)
                P.op("dve", lambda e, t=t1, u=t2: e.tensor_scalar(out=u[:64], in0=t[:64], scalar1=0.5 * math.pi, scalar2=None,
                                                                  op0=ALU.add), reads=[B1], writes=[B2])
                wrap(t2, msk, B2, Bm)
                P.op("act", lambda e, t=t1: e.activation(out=t[:64], in_=t[:64], func=AF.Sin), reads=[B1], writes=[B1])
                P.op("act", lambda e, t=t2: e.activation(out=t[:64], in_=t[:64], func=AF.Sin), reads=[B2], writes=[B2])
                P.op("dve", lambda e, t=t1: e.tensor_scalar(out=t[:64], in0=t[:64], scalar1=crope[:64, 1:2], scalar2=None,
                                                            op0=ALU.mult), reads=[B1, Bc], writes=[B1])
                P.dma("sp", sin_s[:, c0:c0 + cw], t1[:64], B1, reads=[B1])
                P.dma("sp", cos_s[:, c0:c0 + cw], t2[:64], B2, reads=[B2])
            P.barrier()
            sb.reset(m)

        phase0()

        def act_(fn, **kw):
            P.op("act", fn, **kw)

        def dve_(fn, **kw):
            P.op("dve", fn, **kw)

        def mm_group(out_ap, pairs, reads, psbuf):
            P.deps("pe", reads=reads, writes=[psbuf])
            n = len(pairs)
            for i, (l, r) in enumerate(pairs):
                fn = (lambda e, l=l, r=r, st=(i == 0), sp=(i == n - 1): e.matmul(out_ap, l, r, start=st, stop=sp))
                if i == n - 1:
                    P.commit("pe", fn, reads=reads, writes=[psbuf])
                else:
                    P.emit("pe", fn)

        class Rot:
            def __init__(self, items):
                self.items = items
                self.i = 0

            def next(self):
                it = self.items[self.i % len(self.items)]
                self.i += 1
                return it

        def rms_rstd(dst, src, n, Bdst, Bsrc):
            act_(lambda e: e.activation(out=dst, in_=src, func=AF.Sqrt, scale=1.0 / n, bias=EPS), reads=[Bsrc], writes=[Bdst])
            dve_(lambda e: e.reciprocal(out=dst, in_=dst), reads=[Bdst], writes=[Bdst])

        def norm_transpose(src_fn, gain, Bgain, uT, BuT, xs, Bxs, xn, Bxn, ss, Bss, tr_banks, sub, pre_loaded=False):
            KSUB = int(os.environ.get('KSUB', '99'))
            if src_fn is not None:
                src_fn(xs, Bxs)
            dve_(lambda e: e.memset(ss, 0.0), writes=[Bss])
            if KSUB < 1:
                return
            act_(lambda e: e.activation(out=xn, in_=xs, func=AF.Square, accum_out=ss), reads=[Bxs], writes=[Bxn, Bss])
            if KSUB < 2:
                return
            rms_rstd(ss, ss, float(D), Bss, Bss)
            if KSUB < 3:
                return
            dve_(lambda e: e.scalar_tensor_tensor(out=xn, in0=xs, scalar=ss, in1=gain, op0=ALU.mult, op1=ALU.mult),
                 reads=[Bxs, Bss, Bgain], writes=[Bxn])
            if KSUB < 4:
                return
            for half in range(2):
                if KSUB < 5 + half:
                    return
                bank = tr_banks.next()
                P.deps("pe", reads=[Bxn, B_const], writes=[PS[bank]])
                for i in range(8):
                    c = half * 8 + i
                    fn = lambda e, c=c, i=i, bank=bank: e.transpose(psb[bank][:, i * 128:(i + 1) * 128], xn[:, c * 128:(c + 1) * 128], ident)
                    if i == 7:
                        P.commit("pe", fn, reads=[Bxn, B_const], writes=[PS[bank]])
                    else:
                        P.emit("pe", fn)
                if KSUB < 7:
                    continue
                src = psb[bank].rearrange("p (c t) -> p c t", t=128)
                dst = uT[:, half * 8:(half + 1) * 8, sub * 128:(sub + 1) * 128]
                if half == 0:
                    P.op("act", lambda e, s_=src, d_=dst: e.activation(out=d_, in_=s_, func=AF.Copy), reads=[PS[bank]], pwrites=[BuT])
                else:
                    P.op("dve", lambda e, s_=src, d_=dst: e.tensor_copy(out=d_, in_=s_), reads=[PS[bank]], pwrites=[BuT])

        def phase1a():
            m = sb.mark()
            w_in_v = w_in.rearrange("(kc p) n -> p kc n", p=128)
            w_uq_v = w_uq.rearrange("(kc p) n -> p kc n", p=128)
            wuq = sb.alloc([4, 1536], BF16)
            wuqr = sb.alloc([4, 512], BF16)
            wukv = sb.alloc([4, 2048], BF16)
            Bw = Buf("w_res")
            P.dma("pool", wuq, w_uq_v, Bw, pwrites=[Bw])
            for h in range(NH):
                base = h * 192 + 128
                P.dma("pool", wuqr[:, :, h * 64:h * 64 + 32], w_uq_v[:, :, base + 32:base + 64], Bw, pwrites=[Bw])
                P.dma("pool", wuqr[:, :, h * 64 + 32:h * 64 + 64], w_uq_v[:, :, base:base + 32], Bw, pwrites=[Bw])
            P.dma("pool", wukv, w_ukv.rearrange("(kc p) n -> p kc n", p=128), Bw, pwrites=[Bw])
            wukv_h = wukv.rearrange("p kc (h c) -> p kc h c", c=256)
            gpre = sb.alloc([D], F32)
            gq = sb.alloc([4], F32)
            gkv = sb.alloc([4], F32)
            cwb = sb.alloc([12, 5], F32)
            small = sb.alloc([48], F32)
            Bg = Buf("gains")
            P.dma("sp", gpre, g_pre, Bg, pwrites=[Bg])
            P.dma("sp", gq, g_q, Bg, pwrites=[Bg])
            P.dma("sp", gkv, g_kv, Bg, pwrites=[Bg])
            P.dma("sp", cwb, conv_wb.rearrange("p (c k) -> p c k", k=5), Bg, pwrites=[Bg])
            P.dma("sp", small, ssd_small, Bg, pwrites=[Bg])

            xs = [sb.alloc([D], F32) for _ in range(2)]
            Bxs = [Buf("xs0"), Buf("xs1")]
            xn = [sb.alloc([D], BF16) for _ in range(2)]
            Bxn = [Buf("xn0"), Buf("xn1")]
            ssq = [sb.alloc([1], F32) for _ in range(2)]
            Bss = [Buf("ss0"), Buf("ss1")]
            uTs = [sb.alloc([16, TT], BF16) for _ in range(2)]
            BuTs = [Buf("uT0"), Buf("uT1")]
            craw = [sb.alloc([4, TT], F32) for _ in range(2)]
            Bcraw = [Buf("craw0"), Buf("craw1")]
            sq = [sb.alloc([TT], BF16) for _ in range(2)]
            Bsq = [Buf("sq0"), Buf("sq1")]
            rstdb = sb.alloc([TT], F32)
            Brstdb = Buf("rstdb")
            cn = [sb.alloc([4, TT], BF16) for _ in range(2)]
            Bcn = [Buf("cqn"), Buf("ckvn")]
            cst = [sb.alloc([TT], F32) for _ in range(2)]
            Bcst = Buf("cossin")
            rt = [sb.alloc([TT], F32) for _ in range(4)]
            Brt = [Buf(f"rt{i}") for i in range(4)]
            rt_rot = Rot([0, 2])
            stg = [sb.alloc([TT], BF16) for _ in range(4)]
            Bstg = [Buf(f"stg{i}") for i in range(4)]
            stg_rot = Rot(list(range(4)))
            vst = [sb.alloc([1024], BF16) for _ in range(2)]
            Bvst = [Buf("vst0"), Buf("vst1")]
            zst = sb.alloc([4, 1024], BF16)
            Bzst = [Buf(f"zst{i}") for i in range(4)]
            xraw = [sb.alloc([TT + 3], F32) for _ in range(2)]
            Bxraw = [Buf("xraw0"), Buf("xraw1")]
            halo = sb.alloc([12, 3], F32)
            Bhalo = Buf("halo")
            acc = [sb.alloc([TT], F32) for _ in range(2)]
            Bacc = [Buf("acc0"), Buf("acc1")]
            dtst = sb.alloc([4, 16], F32)
            Bdtst = Buf("dtst")
            dttmp4 = [sb.alloc([16], F32) for _ in range(4)]
            Bdttmp4 = [Buf(f"dttmp{i}") for i in range(4)]
            dve_(lambda e: e.memset(halo, 0.0), writes=[Bhalo])

            def mk_loads():
                L = []
                for _t in range(NT):
                    for c0 in (0, 256, 512, 768):
                        L.append([(lambda sl: sl, w_in_v[:, :, c0:c0 + 256])])
                    L.append([(lambda sl: sl[:, :, 0:64], w_in_v[:, :, 1024:1088]),
                              (lambda sl: sl[:, :, 64:96], w_in_v[:, :, 1056:1088]),
                              (lambda sl: sl[:, :, 96:128], w_in_v[:, :, 1024:1056])])
                    for i in range(4):
                        c0 = 1088 + 256 * i
                        L.append([(lambda sl: sl, w_in_v[:, :, c0:c0 + 256])])
                    for i in range(6):
                        c0 = 2112 + 256 * i
                        L.append([(lambda sl: sl, w_in_v[:, :, c0:c0 + 256])])
                    L.append([(lambda sl: sl[:, :, 0:16], w_in_v[:, :, 3648:3664])])
                return L
            ws = WStream(4, [16, 256], mk_loads())

            tr_banks = Rot([0, 1])
            mm_banks = Rot([2, 3, 4, 5, 6])
            SSB = 7

            def store(dst, src, Bsrc):
                P.dma("sp", dst, src, Bsrc, reads=[Bsrc])

            def rope_out(dst_dram, RA, RB):
                r0_ = rt_rot.next()
                r1_ = r0_ + 1
                dve_(lambda e: e.tensor_tensor(out=rt[r0_][:64], in0=psf[RA][:64], in1=cst[0][:64], op=ALU.mult),
                     reads=[PS[RA], Bcst], writes=[Brt[r0_]])
                dve_(lambda e: e.tensor_tensor(out=rt[r1_][:64], in0=psf[RB][:64], in1=cst[1][:64], op=ALU.mult),
                     reads=[PS[RB], Bcst], writes=[Brt[r1_]])
                si = stg_rot.next()
                dve_(lambda e: e.tensor_tensor(out=stg[si][:64], in0=rt[r0_][:64], in1=rt[r1_][:64], op=ALU.add),
                     reads=[Brt[r0_], Brt[r1_]], writes=[Bstg[si]])
                store(dst_dram, stg[si][:64], Bstg[si])

            KB = int(os.environ.get('KB', '99'))

            def latent_norm(which, gl):
                for half in range(2):
                    wt, Bwt = ws.get()
                    for cc in range(2):
                        ch = half * 2 + cc
                        bank = mm_banks.next()
                        mm_group(psf[bank], [(wt[:, kc, cc * 128:(cc + 1) * 128], uT[:, kc, :]) for kc in range(16)],
                                 [Bwt, BuT], PS[bank])
                        if KB < 1:
                            continue
                        KX = int(os.environ.get('KX', '3'))
                        if KX & 1:
                            act_(lambda e, bank=bank, ch=ch: e.activation(out=sq[ch % 2], in_=psf[bank], func=AF.Square),
                                 reads=[PS[bank]], writes=[Bsq[ch % 2]])
                        if KX & 2:
                            dve_(lambda e, bank=bank, ch=ch: e.tensor_copy(out=craw[which][:, ch, :], in_=psf[bank]),
                                 reads=[PS[bank]], pwrites=[Bcraw[which]])
                        if KB < 2:
                            continue
                        P.deps("pe", reads=[Bsq[ch % 2], B_const], writes=[PS[SSB]] if ch == 0 else [])
                        P.commit("pe", lambda e, ch=ch: e.matmul(psf[SSB], ones, sq[ch % 2], start=(ch == 0), stop=(ch == 3)),
                                 reads=[Bsq[ch % 2], B_const], writes=[PS[SSB]] if ch == 3 else [], pwrites=[PS[SSB]] if ch < 3 else [])
                    ws.release()
                if KB < 3:
                    return
                rms_rstd(rstdb, psf[SSB], 512.0, Brstdb, PS[SSB])
                if KB < 4:
                    return
                for ch in range(4):
                    dve_(lambda e, ch=ch: e.scalar_tensor_tensor(out=cn[which][:, ch, :], in0=craw[which][:, ch, :],
                                                                 scalar=gl[:, ch:ch + 1], in1=rstdb, op0=ALU.mult, op1=ALU.mult),
                         reads=[Bcraw[which], Brstdb, Bg], pwrites=[Bcn[which]])

            KST = int(os.environ.get('KSTAGE', '99'))
            def normT(t):
                for sub in range(4):
                    r0 = t * TT + sub * 128
                    b = sub % 2
                    norm_transpose(lambda xs_, Bxs_, r0=r0: P.dma("sp", xs_, x[r0:r0 + 128, :], Bxs_, writes=[Bxs_]),
                                   gpre, Bg, uTs[t % 2], BuTs[t % 2], xs[b], Bxs[b], xn[b], Bxn[b], ssq[b], Bss[b], tr_banks, sub)

            for t in range(NT if KST >= 0 else 0):
                tok = slice(t * TT, (t + 1) * TT)
                P.dma("sp", cst[0][:64], cos_s[:, tok], Bcst, writes=[Bcst])
                P.dma("sp", cst[1][:64], sin_s[:, tok], Bcst, pwrites=[Bcst])
                uT, BuT = uTs[t % 2], BuTs[t % 2]
                if t == 0:
                    normT(0)
                if KST < 1:
                    continue
                latent_norm(0, gq)
                latent_norm(1, gkv)
                for h in range(NH if KB >= 5 else 0):
                    bank = mm_banks.next()
                    mm_group(psf[bank], [(wuq[:, kc, h * 192:h * 192 + 128], cn[0][:, kc, :]) for kc in range(4)], [Bw, Bcn[0]], PS[bank])
                    si = stg_rot.next()
                    act_(lambda e, bank=bank, si=si: e.activation(out=stg[si], in_=psf[bank], func=AF.Copy), reads=[PS[bank]], writes=[Bstg[si]])
                    store(qn_s[h, :, tok], stg[si], Bstg[si])
                    RA, RB = mm_banks.next(), mm_banks.next()
                    mm_group(psf[RA][:64], [(wuq[:, kc, h * 192 + 128:h * 192 + 192], cn[0][:, kc, :]) for kc in range(4)], [Bw, Bcn[0]], PS[RA])
                    mm_group(psf[RB][:64], [(wuqr[:, kc, h * 64:h * 64 + 64], cn[0][:, kc, :]) for kc in range(4)], [Bw, Bcn[0]], PS[RB])
                    rope_out(qr_s[h, :, tok], RA, RB)
                if KST < 2:
                    continue
                for h in range(NH):
                    bank = mm_banks.next()
                    mm_group(psf[bank], [(wukv[:, kc, h * 256:h * 256 + 128], cn[1][:, kc, :]) for kc in range(4)], [Bw, Bcn[1]], PS[bank])
                    si = stg_rot.next()
                    act_(lambda e, bank=bank, si=si: e.activation(out=stg[si], in_=psf[bank], func=AF.Copy), reads=[PS[bank]], writes=[Bstg[si]])
                    store(kn_s[h, :, tok], stg[si], Bstg[si])
                for sub in range(4):
                    vb = sub % 2
                    for half in range(2):
                        bank = mm_banks.next()
                        mm_group(psf[bank].rearrange("p (h c) -> p h c", c=128),
                                 [(cn[1][:, kc, sub * 128:(sub + 1) * 128], wukv_h[:, kc, half * 4:half * 4 + 4, 128:256]) for kc in range(4)],
                                 [Bw, Bcn[1]], PS[bank])
                        if half == 0:
                            act_(lambda e, bank=bank, vb=vb: e.activation(out=vst[vb][:, 0:512], in_=psf[bank], func=AF.Copy),
                                 reads=[PS[bank]], writes=[Bvst[vb]])
                        else:
                            dve_(lambda e, bank=bank, vb=vb: e.tensor_copy(out=vst[vb][:, 512:1024], in_=psf[bank]),
                                 reads=[PS[bank]], pwrites=[Bvst[vb]])
                    r0 = t * TT + sub * 128
                    store(v_s[r0:r0 + 128, :], vst[vb], Bvst[vb])
                if t + 1 < NT:
                    normT(t + 1)
                if KST < 3:
                    continue
                wt, Bwt = ws.get()
                RA, RB = mm_banks.next(), mm_banks.next()
                mm_group(psf[RA][:64], [(wt[:, kc, 0:64], uT[:, kc, :]) for kc in range(16)], [Bwt, BuT], PS[RA])
                mm_group(psf[RB][:64], [(wt[:, kc, 64:128], uT[:, kc, :]) for kc in range(16)], [Bwt, BuT], PS[RB])
                ws.release()
                rope_out(kr_s[:, tok], RA, RB)
                if KST < 4:
                    continue
                for i in range(4):
                    wt, Bwt = ws.get()
                    for sub in range(4):
                        bank = mm_banks.next()
                        mm_group(psf[bank][:, 0:256], [(uT[:, kc, sub * 128:(sub + 1) * 128], wt[:, kc, :]) for kc in range(16)],
                                 [Bwt, BuT], PS[bank])
                        act_(lambda e, bank=bank, sub=sub, i=i: e.activation(out=zst[:, sub, i * 256:(i + 1) * 256], in_=psf[bank][:, 0:256], func=AF.Silu),
                             reads=[PS[bank]], writes=[Bzst[sub]] if i == 0 else [], pwrites=[Bzst[sub]] if i > 0 else [])
                    ws.release()
                for sub in range(4):
                    r0 = t * TT + sub * 128
                    store(zs_s[r0:r0 + 128, :], zst[:, sub, :], Bzst[sub])
                if KST < 5:
                    continue
                for i in range(6):
                    wt, Bwt = ws.get()
                    for cc in range(2):
                        c = 2 * i + cc
                        xb = c % 2
                        bank = mm_banks.next()
                        mm_group(psf[bank], [(wt[:, kc, cc * 128:(cc + 1) * 128], uT[:, kc, :]) for kc in range(16)], [Bwt, BuT], PS[bank])
                        act_(lambda e, bank=bank, xb=xb: e.activation(out=xraw[xb][:, 3:TT + 3], in_=psf[bank], func=AF.Copy),
                             reads=[PS[bank]], writes=[Bxraw[xb]])
                        dve_(lambda e, xb=xb, c=c: e.tensor_copy(out=xraw[xb][:, 0:3], in_=halo[:, c, :]), reads=[Bhalo], pwrites=[Bxraw[xb]])
                        dve_(lambda e, xb=xb, c=c: e.tensor_copy(out=halo[:, c, :], in_=xraw[xb][:, TT:TT + 3]), reads=[Bxraw[xb]], pwrites=[Bhalo])
                        dve_(lambda e, xb=xb, c=c: e.tensor_scalar(out=acc[xb], in0=xraw[xb][:, 0:TT], scalar1=cwb[:, c, 0:1], scalar2=None, op0=ALU.mult),
                             reads=[Bxraw[xb], Bg], writes=[Bacc[xb]])
                        for k in (1, 2, 3):
                            dve_(lambda e, xb=xb, c=c, k=k: e.scalar_tensor_tensor(out=acc[xb], in0=xraw[xb][:, k:TT + k], scalar=cwb[:, c, k:k + 1],
                                                                                   in1=acc[xb], op0=ALU.mult, op1=ALU.add),
                                 reads=[Bxraw[xb], Bg, Bacc[xb]], writes=[Bacc[xb]])
                        si = stg_rot.next()
                        act_(lambda e, xb=xb, c=c, si=si: e.activation(out=stg[si], in_=acc[xb], func=AF.Silu, bias=cwb[:, c, 4:5]),
                             reads=[Bacc[xb], Bg], writes=[Bstg[si]])
                        store(xbc_s[c * 128:(c + 1) * 128, tok], stg[si], Bstg[si])
                    ws.release()
                if KST < 6:
                    continue
                wt, Bwt = ws.get()
                for sub in range(4):
                    bank = mm_banks.next()
                    mm_group(psf[bank][:, 0:16], [(uT[:, kc, sub * 128:(sub + 1) * 128], wt[:, kc, 0:16]) for kc in range(16)], [Bwt, BuT], PS[bank])
                    dttmp, Bdttmp = dttmp4[sub], Bdttmp4[sub]
                    dve_(lambda e, bank=bank, dttmp=dttmp: e.tensor_tensor(out=dttmp, in0=psf[bank][:, 0:16], in1=small[:, 0:16], op=ALU.add),
                         reads=[PS[bank], Bg], writes=[Bdttmp])
                    act_(lambda e, dttmp=dttmp: e.activation(out=dttmp, in_=dttmp, func=AF.Exp), reads=[Bdttmp], writes=[Bdttmp])
                    act_(lambda e, sub=sub, dttmp=dttmp: e.activation(out=dtst[:, sub, :], in_=dttmp, func=AF.Ln, bias=1.0), reads=[Bdttmp],
                         writes=[Bdtst] if sub == 0 else [], pwrites=[Bdtst] if sub > 0 else [])
                ws.release()
                store(dt_s[tok, :].rearrange("(s p) h -> p s h", p=128), dtst, Bdtst)
            P.barrier()
            sb.reset(m)

        phase1a()

        def bcast_mid(ap2, k):
            n = ap2.shape[1]
            return ap2.unsqueeze(1).to_broadcast([128, k, n])

        def bcast_last(ap2, k):
            n = ap2.shape[1]
            return ap2.unsqueeze(2).to_broadcast([128, n, k])

        def ssd_setup():
            tri = sb.alloc([128], F32)
            negm = sb.alloc([128], F32)
            onesf = sb.alloc([128], F32)
            small = sb.alloc([48], F32)
            aneg = sb.alloc([16], F32)
            gssd = sb.alloc([1024], F32)
            Bc = Buf("c1b")
            P.dma("sp", tri, c_tri, Bc, pwrites=[Bc])
            P.dma("sp", negm, c_negm, Bc, pwrites=[Bc])
            P.dma("sp", small, ssd_small, Bc, pwrites=[Bc])
            P.dma("sp", gssd, g_ssd, Bc, pwrites=[Bc])
            dve_(lambda e: e.memset(onesf, 1.0), pwrites=[Bc])
            act_(lambda e: e.activation(out=aneg, in_=small[:, 16:32], func=AF.Exp), reads=[Bc], pwrites=[Bc])
            dve_(lambda e: e.tensor_scalar(out=aneg, in0=aneg, scalar1=-1.0, scalar2=None, op0=ALU.mult), reads=[Bc], pwrites=[Bc])
            dskip = small[:, 32:48]
            hT = sb.alloc([1024], F32)
            hTb = sb.alloc([1024], BF16)
            Bh, Bhb = Buf("hT"), Buf("hTb")
            dve_(lambda e: e.memset(hT, 0.0), writes=[Bh])
            dve_(lambda e: e.memset(hTb, 0.0), writes=[Bhb])
            xbcT = [sb.alloc([12, 128], BF16) for _ in range(3)]
            zs = [sb.alloc([1024], BF16) for _ in range(3)]
            dtt = [sb.alloc([16], F32) for _ in range(3)]
            Bin = [Buf("in0"), Buf("in1"), Buf("in2")]
            tri_bf = sb.alloc([128], BF16)
            dve_(lambda e: e.tensor_copy(out=tri_bf, in_=tri), reads=[Bc], pwrites=[Bc])
            a_hi = sb.alloc([16], BF16); a_lo = sb.alloc([16], BF16); a_hf = sb.alloc([16], F32); Bahl = Buf("ahl")
            rhsA_hi = sb.alloc([16, 128], BF16); rhsA_lo = sb.alloc([16, 128], BF16)
            a_t = sb.alloc([16], F32); Ba = Buf("a")
            acol = sb.alloc([16], F32); Bacol = Buf("acol")
            BrhsA = Buf("rhsA")
            arow = sb.alloc([16, 128], F32); Barow = Buf("arow")
            tmp = sb.alloc([16, 128], F32); Btmp = Buf("tmp")
            cbt = sb.alloc([2, 128], F32); Bcbt = Buf("cbt")
            xtok = sb.alloc([16, 64], BF16); Bxtok = Buf("xtok")
            MT2 = [sb.alloc([16, 128], BF16) for _ in range(2)]; BMT2 = [Buf("MT0"), Buf("MT1")]
            btok2 = [sb.alloc([256], BF16) for _ in range(2)]; Bbtok2 = [Buf("btok0"), Buf("btok1")]
            xdt2 = [sb.alloc([16, 64], BF16) for _ in range(2)]; Bxdt2 = [Buf("xdt0"), Buf("xdt1")]
            xdtw2 = [sb.alloc([16, 64], BF16) for _ in range(2)]; Bxdtw2 = [Buf("xdtw0"), Buf("xdtw1")]
            sm2 = [sb.alloc([4, 16], F32) for _ in range(2)]; Bsm2 = [Buf("sm0"), Buf("sm1")]
            xD2 = [sb.alloc([16, 64], F32) for _ in range(2)]; BxD2 = [Buf("xD0"), Buf("xD1")]
            y = sb.alloc([16, 64], F32); By = Buf("y")
            ss2 = sb.alloc([2], F32); Bss2 = Buf("ss2")
            junk = sb.alloc([512], BF16); Bjunk = Buf("junk")
            ssm = sb.alloc([1024], BF16); Bssm = Buf("ssm")
            sst = [sb.alloc([8, 128], BF16) for _ in range(2)]; Bsst = [Buf("sst0"), Buf("sst1")]
            xbc_v = xbc_s.rearrange("(c p) s -> p c s", p=128)
            ssmT_v = ssmT_s.rearrange("(c p) s -> p c s", p=128)

            def load(c):
                b = c % 3
                tok = slice(c * 128, (c + 1) * 128)
                P.dma("sp", xbcT[b], xbc_v[:, :, tok], Bin[b], writes=[Bin[b]])
                P.dma("sp", zs[b], zs_s[tok, :], Bin[b], pwrites=[Bin[b]])
                P.dma("sp", dtt[b], dt_s[tok, :], Bin[b], pwrites=[Bin[b]])

            def front(c):
                p = c % 2
                X, DT, BI = xbcT[c % 3], dtt[c % 3], Bin[c % 3]
                MT, BMT, btok, Bbtok = MT2[p], BMT2[p], btok2[p], Bbtok2[p]
                xdt, Bxdt, xdtw, Bxdtw, sm, Bsm, xD, BxD = xdt2[p], Bxdt2[p], xdtw2[p], Bxdtw2[p], sm2[p], Bsm2[p], xD2[p], BxD2[p]
                dve_(lambda e: e.tensor_tensor(out=a_t, in0=DT, in1=aneg, op=ALU.mult), reads=[BI, Bc], writes=[Ba])
                yield
                mm_group(psf[4][:, 0:16], [(tri, a_t)], [Bc, Ba], PS[4])
                dve_(lambda e: e.tensor_copy(out=a_hi, in_=a_t), reads=[Ba], writes=[Bahl])
                dve_(lambda e: e.tensor_copy(out=a_hf, in_=a_hi), reads=[Bahl], pwrites=[Bahl])
                dve_(lambda e: e.tensor_tensor(out=a_lo, in0=a_t, in1=a_hf, op=ALU.subtract), reads=[Ba, Bahl], pwrites=[Bahl])
                dve_(lambda e: e.tensor_copy(out=acol, in_=psf[4][:, 0:16]), reads=[PS[4]], writes=[Bacol])
                dve_(lambda e: e.tensor_tensor(out=rhsA_hi, in0=bcast_mid(tri_bf, 16), in1=bcast_last(a_hi, 128), op=ALU.mult),
                     reads=[Bc, Bahl], writes=[BrhsA])
                dve_(lambda e: e.tensor_tensor(out=rhsA_lo, in0=bcast_mid(tri_bf, 16), in1=bcast_last(a_lo, 128), op=ALU.mult),
                     reads=[Bc, Bahl], pwrites=[BrhsA])
                yield
                for g in range(2):
                    mm_group(psf[4][:, 128 + g * 128:128 + (g + 1) * 128], [(X[:, 8 + g, :], X[:, 10 + g, :])], [BI], PS[4])
                act_(lambda e: e.activation(out=cbt, in_=psf[4][:, 128:384], func=AF.Copy), reads=[PS[4]], writes=[Bcbt])
                P.deps("pe", reads=[BI, B_const], writes=[PS[0]])
                for j in range(8):
                    fn = lambda e, j=j: e.transpose(psb[0][:, j * 128:(j + 1) * 128], X[:, j, :], ident)
                    if j == 7:
                        P.commit("pe", fn, reads=[BI, B_const], writes=[PS[0]])
                    else:
                        P.emit("pe", fn)
                act_(lambda e: e.activation(out=xtok, in_=psb[0], func=AF.Copy), reads=[PS[0]], writes=[Bxtok])
                yield
                for q4 in range(4):
                    hs4 = slice(q4 * 4, (q4 + 1) * 4)
                    mm_group(psf[4], [(ones, rhsA_hi[:, hs4, :]), (ones, rhsA_lo[:, hs4, :])], [B_const, BrhsA], PS[4])
                    act_(lambda e, hs4=hs4: e.activation(out=arow[:, hs4, :], in_=psf[4], func=AF.Copy),
                         reads=[PS[4]], pwrites=[Barow])
                    yield
                P.deps("pe", reads=[BI, B_const], writes=[PS[0]])
                P.emit("pe", lambda e: e.transpose(psb[0][:, 0:128], X[:, 8, :], ident))
                P.commit("pe", lambda e: e.transpose(psb[0][:, 128:256], X[:, 9, :], ident), reads=[BI, B_const], writes=[PS[0]])
                dve_(lambda e: e.tensor_copy(out=btok, in_=psb[0][:, 0:256]), reads=[PS[0]], writes=[Bbtok])
                dve_(lambda e: e.tensor_tensor(out=tmp, in0=arow, in1=bcast_last(acol, 128), op=ALU.subtract),
                     reads=[Barow, Bacol], writes=[Btmp])
                dve_(lambda e: e.tensor_tensor(out=tmp, in0=tmp, in1=bcast_mid(negm, 16), op=ALU.add),
                     reads=[Btmp, Bc], writes=[Btmp])
                yield
                act_(lambda e: e.activation(out=tmp, in_=tmp, func=AF.Exp), reads=[Btmp], writes=[Btmp])
                alast = arow[:, :, 127]
                dve_(lambda e: e.tensor_tensor(out=sm[:, 3, :], in0=alast, in1=acol, op=ALU.subtract), reads=[Barow, Bacol], writes=[Bsm])
                act_(lambda e: e.activation(out=sm[:, 0, :], in_=sm[:, 3, :], func=AF.Exp), reads=[Bsm], pwrites=[Bsm])
                act_(lambda e: e.activation(out=sm[:, 1, :], in_=acol, func=AF.Exp), reads=[Bacol], pwrites=[Bsm])
                act_(lambda e: e.activation(out=sm[:, 2, :], in_=alast, func=AF.Exp), reads=[Barow], pwrites=[Bsm])
                dve_(lambda e: e.tensor_tensor(out=xdt, in0=xtok, in1=bcast_last(DT, 64), op=ALU.mult), reads=[Bxtok, BI], writes=[Bxdt])
                dve_(lambda e: e.tensor_tensor(out=xD, in0=xtok, in1=bcast_last(dskip, 64), op=ALU.mult), reads=[Bxtok, Bc], writes=[BxD])
                yield
                dve_(lambda e: e.tensor_tensor(out=xdtw, in0=xdt, in1=bcast_last(sm[:, 0, :], 64), op=ALU.mult), reads=[Bxdt, Bsm], writes=[Bxdtw])
                for g in range(2):
                    dve_(lambda e, g=g: e.tensor_tensor(out=MT[:, g * 8:(g + 1) * 8, :], in0=tmp[:, g * 8:(g + 1) * 8, :],
                                                        in1=bcast_mid(cbt[:, g, :], 8), op=ALU.mult),
                         reads=[Btmp, Bcbt], writes=[BMT] if g == 0 else [], pwrites=[BMT] if g else [])
                yield

            def back(c):
                p = c % 2
                b = c % 2
                tok = slice(c * 128, (c + 1) * 128)
                X, Z, BI = xbcT[c % 3], zs[c % 3], Bin[c % 3]
                MT, BMT, btok, Bbtok = MT2[p], BMT2[p], btok2[p], Bbtok2[p]
                xdt, Bxdt, xdtw, Bxdtw, sm, Bsm, xD, BxD = xdt2[p], Bxdt2[p], xdtw2[p], Bxdtw2[p], sm2[p], Bsm2[p], xD2[p], BxD2[p]
                YB = 7
                for g in range(2):
                    hs = slice(g * 8, (g + 1) * 8)
                    mm_group(psf[YB], [(X[:, 10 + g, :], hTb[:, g * 512:(g + 1) * 512])], [BI, Bhb], PS[YB])
                    dve_(lambda e, hs=hs: e.tensor_tensor(out=y[:, hs, :], in0=psf[YB].rearrange("p (h q) -> p h q", q=64),
                                                          in1=bcast_last(sm[:, 1, hs], 64), op=ALU.mult), reads=[PS[YB], Bsm], pwrites=[By])
                    yield
                    P.deps("pe", reads=[BMT, Bxdt], writes=[PS[YB]])
                    for e8 in range(8):
                        h = g * 8 + e8
                        fn = lambda e, h=h, e8=e8: e.matmul(psf[YB][:, e8 * 64:(e8 + 1) * 64], MT[:, h, :], xdt[:, h, :], start=True, stop=True)
                        if e8 == 7:
                            P.commit("pe", fn, reads=[BMT, Bxdt], writes=[PS[YB]])
                        else:
                            P.emit("pe", fn)
                    dve_(lambda e, hs=hs: e.tensor_tensor(out=y[:, hs, :], in0=psf[YB].rearrange("p (h q) -> p h q", q=64),
                                                          in1=y[:, hs, :], op=ALU.add), reads=[PS[YB], By], pwrites=[By])
                    yield
                dve_(lambda e: e.tensor_tensor(out=y, in0=y, in1=xD, op=ALU.add), reads=[By, BxD], writes=[By])
                hT3 = hT.rearrange("p (h q) -> p h q", q=64)
                dve_(lambda e: e.tensor_tensor(out=hT3, in0=hT3, in1=bcast_last(sm[:, 2, :], 64), op=ALU.mult), reads=[Bh, Bsm], writes=[Bh])
                for g in range(2):
                    mm_group(psf[YB], [(btok[:, g * 128:(g + 1) * 128], xdtw[:, g * 8:(g + 1) * 8, :])], [Bbtok, Bxdtw], PS[YB])
                    dve_(lambda e, g=g: e.tensor_tensor(out=hT[:, g * 512:(g + 1) * 512], in0=psf[YB], in1=hT[:, g * 512:(g + 1) * 512], op=ALU.add),
                         reads=[PS[YB], Bh], writes=[Bh])
                    yield
                act_(lambda e: e.activation(out=hTb, in_=hT, func=AF.Copy), reads=[Bh], writes=[Bhb])
                y2 = y.rearrange("p h q -> p (h q)")
                dve_(lambda e: e.tensor_tensor(out=y2, in0=y2, in1=Z, op=ALU.mult), reads=[By, BI], writes=[By])
                dve_(lambda e: e.memset(ss2, 0.0), writes=[Bss2])
                for g in range(2):
                    act_(lambda e, g=g: e.activation(out=junk, in_=y2[:, g * 512:(g + 1) * 512], func=AF.Square, accum_out=ss2[:, g:g + 1]),
                         reads=[By, Bss2], writes=[Bjunk], pwrites=[Bss2])
                yield
                rms_rstd(ss2, ss2, 512.0, Bss2, Bss2)
                for g in range(2):
                    dve_(lambda e, g=g: e.scalar_tensor_tensor(out=ssm[:, g * 512:(g + 1) * 512], in0=y2[:, g * 512:(g + 1) * 512], scalar=ss2[:, g:g + 1],
                                                               in1=gssd[:, g * 512:(g + 1) * 512], op0=ALU.mult, op1=ALU.mult),
                         reads=[By, Bss2, Bc], pwrites=[Bssm])
                yield
                P.deps("pe", reads=[Bssm, B_const], writes=[PS[0]])
                for j in range(8):
                    fn = lambda e, j=j: e.transpose(psb[0][:, j * 128:(j + 1) * 128], ssm[:, j * 128:(j + 1) * 128], ident)
                    if j == 7:
                        P.commit("pe", fn, reads=[Bssm, B_const], writes=[PS[0]])
                    else:
                        P.emit("pe", fn)
                act_(lambda e: e.activation(out=sst[b], in_=psb[0].rearrange("p (c t) -> p c t", t=128), func=AF.Copy),
                     reads=[PS[0]], writes=[Bsst[b]])
                P.dma("sp", ssmT_v[:, :, tok], sst[b], Bsst[b], reads=[Bsst[b]])
                yield

            def gen():
                load(0)
                if NCH > 1:
                    load(1)
                yield from front(0)
                for c in range(NCH):
                    if c + 2 < NCH:
                        load(c + 2)
                    if c + 1 < NCH:
                        yield from front(c + 1)
                    yield from back(c)
            return gen()


        def attn_setup():
            scale = 192.0 ** -0.5
            cmask = sb.alloc([4, 512], BF16)
            gat = sb.alloc([8], F32)
            Bc = Buf("c2")
            P.dma("sp", cmask, c_cmask.rearrange("p (d q) -> p d q", q=512), Bc, pwrites=[Bc])
            P.dma("sp", gat, g_attn, Bc, pwrites=[Bc])
            krT = sb.alloc([S], BF16)
            Bkr = Buf("krT")
            P.dma("sp", krT[:64], kr_s, Bkr, writes=[Bkr])
            KT = [sb.alloc([S], BF16) for _ in range(2)]
            V = [sb.alloc([NCH, 128], BF16) for _ in range(2)]
            Qn = [sb.alloc([TT], BF16) for _ in range(2)]
            Qr = [sb.alloc([TT], BF16) for _ in range(2)]
            Bin = [Buf("a_in0"), Buf("a_in1")]
            PT = [sb.alloc([TT], BF16) for _ in range(5)]
            BPT = [Buf(f"PT{i}") for i in range(5)]
            pt_rot = Rot([0, 1, 2, 3, 4])
            attn = sb.alloc([8, TT], F32)
            Battn = Buf("attn")
            recip = sb.alloc([TT], F32)
            Brecip = Buf("recip")
            sq = [sb.alloc([TT], BF16) for _ in range(2)]
            Bsq = [Buf("asq0"), Buf("asq1")]
            rstdb = sb.alloc([TT], F32)
            Brstdb = Buf("arstd")
            outst = sb.alloc([8, TT], BF16)
            Boutst = Buf("outst")
            v_v = v_s.rearrange("(c p) f -> p c f", p=128)
            attnT_v = attnT_s.rearrange("(h p) s -> p h s", p=128)
            s_rot = Rot([2, 3])
            OB, SB_, SSB = 5, 6, 4

            def load(j, h, b):
                nk = (j + 1) * TT
                tq = slice(j * TT, (j + 1) * TT)
                P.dma("sp", KT[b][:, 0:nk], kn_s[h, :, 0:nk], Bin[b], writes=[Bin[b]])
                P.dma("sp", V[b][:, 0:nk // 128, :], v_v[:, 0:nk // 128, h * 128:(h + 1) * 128], Bin[b], pwrites=[Bin[b]])
                P.dma("sp", Qn[b], qn_s[h, :, tq], Bin[b], pwrites=[Bin[b]])
                P.dma("sp", Qr[b][:64], qr_s[h, :, tq], Bin[b], pwrites=[Bin[b]])

            jobs = [(j, h) for j in range(NT) for h in range(NH)]
            load(jobs[0][0], jobs[0][1], 0)
            LOOK = 2
            pend = []

            def stageA(idx, kb):
                j, h = jobs[idx]
                b = idx % 2
                d = kb - 4 * j
                q0 = d * 128 if d > 0 else 0
                ks = slice(kb * 128, (kb + 1) * 128)
                sbank = s_rot.next()
                mm_group(psf[sbank][:, q0:TT], [(KT[b][:, ks], Qn[b][:, q0:TT]), (krT[:64, ks], Qr[b][:64, q0:TT])],
                         [Bin[b], Bkr], PS[sbank])
                pi = pt_rot.next()
                act_(lambda e: e.activation(out=PT[pi][:, q0:TT], in_=psf[sbank][:, q0:TT], func=AF.Exp, scale=scale),
                     reads=[PS[sbank]], writes=[BPT[pi]])
                if d >= 0:
                    dve_(lambda e: e.tensor_tensor(out=PT[pi][:, q0:q0 + 128], in0=PT[pi][:, q0:q0 + 128], in1=cmask[:, 0, 0:128], op=ALU.mult),
                         reads=[BPT[pi], Bc], writes=[BPT[pi]])
                return (idx, kb, pi, q0)

            def stageB(idx, kb, pi, q0):
                j, h = jobs[idx]
                b = idx % 2
                nkb = 4 * (j + 1)
                first, last = (kb == 0), (kb == nkb - 1)
                P.deps("pe", reads=[BPT[pi], Bin[b], B_const], writes=[PS[OB], PS[SB_]] if first else [])
                P.emit("pe", lambda e: e.matmul(psf[OB][:, q0:TT], V[b][:, kb, :], PT[pi][:, q0:TT], start=first, stop=last))
                P.commit("pe", lambda e: e.matmul(psf[SB_][:, q0:TT], ones, PT[pi][:, q0:TT], start=first, stop=last),
                         reads=[BPT[pi], Bin[b], B_const], writes=[PS[OB], PS[SB_]] if last else [], pwrites=[] if last else [PS[OB], PS[SB_]])
                if last:
                    finish(idx)

            def finish(idx):
                j, h = jobs[idx]
                dve_(lambda e: e.reciprocal(out=recip, in_=psf[SB_]), reads=[PS[SB_]], writes=[Brecip])
                dve_(lambda e: e.tensor_tensor(out=attn[:, h, :], in0=psf[OB], in1=recip, op=ALU.mult), reads=[PS[OB], Brecip], pwrites=[Battn])
                if h == NH - 1:
                    tq = slice(j * TT, (j + 1) * TT)
                    for hh in range(NH):
                        act_(lambda e, hh=hh: e.activation(out=sq[hh % 2], in_=attn[:, hh, :], func=AF.Square), reads=[Battn], writes=[Bsq[hh % 2]])
                        P.deps("pe", reads=[Bsq[hh % 2], B_const], writes=[PS[SSB]] if hh == 0 else [])
                        P.commit("pe", lambda e, hh=hh: e.matmul(psf[SSB], ones, sq[hh % 2], start=(hh == 0), stop=(hh == NH - 1)),
                                 reads=[Bsq[hh % 2], B_const], writes=[PS[SSB]] if hh == NH - 1 else [], pwrites=[PS[SSB]] if hh < NH - 1 else [])
                    rms_rstd(rstdb, psf[SSB], 1024.0, Brstdb, PS[SSB])
                    for hh in range(NH):
                        dve_(lambda e, hh=hh: e.scalar_tensor_tensor(out=outst[:, hh, :], in0=attn[:, hh, :], scalar=gat[:, hh:hh + 1], in1=rstdb,
                                                                     op0=ALU.mult, op1=ALU.mult), reads=[Battn, Bc, Brstdb],
                             writes=[Boutst] if hh == 0 else [], pwrites=[Boutst] if hh > 0 else [])
                    P.dma("sp", attnT_v[:, :, tq], outst, Boutst, reads=[Boutst])

            def run(tick):
                for idx, (j, h) in enumerate(jobs):
                    for kb in range(4 * (j + 1)):
                        pend.append(stageA(idx, kb))
                        if len(pend) > LOOK:
                            stageB(*pend.pop(0))
                        if kb == LOOK - 1 and idx + 1 < len(jobs):
                            load(jobs[idx + 1][0], jobs[idx + 1][1], (idx + 1) % 2)
                        tick()
                while pend:
                    stageB(*pend.pop(0))
            return run

        def phase12():
            m = sb.mark()
            ssd = ssd_setup()
            run = attn_setup()
            nblocks = sum(4 * (j + 1) for j in range(NT)) * NH
            nyield = NCH * 19 + 8
            st = {"acc": 0.0, "done": False}

            def tick():
                st["acc"] += nyield / nblocks
                while st["acc"] >= 1.0 and not st["done"]:
                    st["acc"] -= 1.0
                    try:
                        next(ssd)
                    except StopIteration:
                        st["done"] = True
            run(tick)
            while not st["done"]:
                try:
                    next(ssd)
                except StopIteration:
                    st["done"] = True
            P.barrier()
            sb.reset(m)

        if int(os.environ.get("KPH", "9")) >= 3:
            phase12()

        h1_s = dscr("h1_s", [S, D], F32)

        def phase3():
            m = sb.mark()
            gpm = sb.alloc([D], F32)
            gpf = sb.alloc([D], F32)
            gpo = sb.alloc([D], F32)
            Bg = Buf("g3")
            P.dma("sp", gpm, g_postmix, Bg, pwrites=[Bg])
            P.dma("sp", gpf, g_preffn, Bg, pwrites=[Bg])
            P.dma("sp", gpo, g_postffn, Bg, pwrites=[Bg])
            actT = sb.alloc([16, TT], BF16)
            BactT = Buf("actT")
            mix = sb.alloc([4, D], F32)
            Bmix = [Buf(f"mix{i}") for i in range(4)]
            xs2 = [sb.alloc([D], F32) for _ in range(2)]
            Bxs2 = [Buf("xs3a"), Buf("xs3b")]
            xn2 = [sb.alloc([D], BF16) for _ in range(2)]
            Bxn2 = [Buf("xn3a"), Buf("xn3b")]
            ssq2 = [sb.alloc([1], F32) for _ in range(2)]
            Bss2_ = [Buf("ss3a"), Buf("ss3b")]
            hid = sb.alloc([44, TT], BF16)
            Bhid = Buf("hid")
            sg = [sb.alloc([TT], F32) for _ in range(2)]
            Bsg = [Buf("sg0"), Buf("sg1")]
            w_out_v = w_out.rearrange("(kc p) n -> p kc n", p=128)
            w_gate_v = w_gate.rearrange("(kc p) n -> p kc n", p=128)
            w_up_v = w_up.rearrange("(kc p) n -> p kc n", p=128)
            w_down_v = w_down.rearrange("(kc p) n -> p kc n", p=128)
            L1 = []
            L2 = []
            for _t in range(NT):
                for i in range(8):
                    L1.append([(lambda sl: sl, w_out_v[:, :, i * 256:(i + 1) * 256])])
                for i in range(22):
                    L1.append([(lambda sl: sl, w_gate_v[:, :, i * 256:(i + 1) * 256])])
                    L1.append([(lambda sl: sl, w_up_v[:, :, i * 256:(i + 1) * 256])])
                for n in range(4):
                    for kg in range(4):
                        L2.append([(lambda sl: sl, w_down_v[:, kg * 11:(kg + 1) * 11, n * 512:(n + 1) * 512])])
            ws1 = WStream(4, [16, 256], L1)
            ws2 = WStream(2, [11, 512], L2)
            tr_banks = Rot([0, 1])
            mm_banks = Rot([2, 3, 4, 5, 6, 7])
            attnT_v = attnT_s.rearrange("(c p) s -> p c s", p=128)
            ssmT_v = ssmT_s.rearrange("(c p) s -> p c s", p=128)
            junk3 = sb.alloc([TT], BF16)
            Bjunk3 = Buf("junk3")
            ssacc = sb.alloc([4, 12], F32)
            Bssacc = Buf("ssacc")
            ev = [0]
            Bh1 = [Buf(f"h1_{i}") for i in range(4)]

            def evac(dst, src, reads, **kw):
                ev[0] += 1
                if ev[0] % 2:
                    act_(lambda e: e.activation(out=dst, in_=src, func=AF.Copy), reads=reads, **kw)
                else:
                    dve_(lambda e: e.tensor_copy(out=dst, in_=src), reads=reads, **kw)

            for t in range(NT):
                tok = slice(t * TT, (t + 1) * TT)
                if t == 0:
                    P.dma("sp", actT[:, 0:8, :], attnT_v[:, :, tok], BactT, writes=[BactT])
                    P.dma("sp", actT[:, 8:16, :], ssmT_v[:, :, tok], BactT, pwrites=[BactT])
                dve_(lambda e: e.memset(ssacc, 0.0), writes=[Bssacc])
                for i in range(8):
                    wt, Bwt = ws1.get()
                    for sub in range(4):
                        bank = mm_banks.next()
                        mm_group(psf[bank][:, 0:256], [(actT[:, kc, sub * 128:(sub + 1) * 128], wt[:, kc, :]) for kc in range(16)],
                                 [Bwt, BactT], PS[bank])
                        act_(lambda e, bank=bank, sub=sub, i=i: e.activation(out=mix[:, sub, i * 256:(i + 1) * 256], in_=psf[bank][:, 0:256], func=AF.Copy),
                             reads=[PS[bank]], pwrites=[Bmix[sub]])
                        act_(lambda e, bank=bank, sub=sub, i=i: e.activation(out=junk3[:, 0:256], in_=psf[bank][:, 0:256], func=AF.Square,
                                                                             accum_out=ssacc[:, sub, i:i + 1]),
                             reads=[PS[bank], Bssacc], writes=[Bjunk3], pwrites=[Bssacc])
                    ws1.release()
                for sub in range(4):
                    r0 = t * TT + sub * 128
                    M = mix[:, sub, :]
                    xs, Bxs, xn, Bxn, ssq, Bss = xs2[sub % 2], Bxs2[sub % 2], xn2[sub % 2], Bxn2[sub % 2], ssq2[sub % 2], Bss2_[sub % 2]
                    P.dma("sp", xs, x[r0:r0 + 128, :], Bxs, writes=[Bxs])
                    dve_(lambda e, ssq=ssq, sub=sub: e.reduce_sum(out=ssq, in_=ssacc[:, sub, 0:8], axis=AX.X), reads=[Bssacc], writes=[Bss])
                    rms_rstd(ssq, ssq, float(D), Bss, Bss)
                    dve_(lambda e, M=M, ssq=ssq: e.scalar_tensor_tensor(out=M, in0=M, scalar=ssq, in1=gpm, op0=ALU.mult, op1=ALU.mult),
                         reads=[Bmix[sub], Bss, Bg], writes=[Bmix[sub]])
                    dve_(lambda e, M=M, xs=xs: e.tensor_tensor(out=M, in0=M, in1=xs, op=ALU.add), reads=[Bmix[sub], Bxs], writes=[Bmix[sub]])
                    P.dma("sp", h1_s[r0:r0 + 128, :], M, Bmix[sub], reads=[Bmix[sub]], writes=[Bh1[sub]])
                    norm_transpose(None, gpf, Bg, actT, BactT, M, Bmix[sub], xn, Bxn, ssq, Bss, tr_banks, sub)
                for i in range(22):
                    wg, Bwg = ws1.get()
                    ws1.cons += 1
                    wu, Bwu = ws1.get()
                    ws1.cons -= 1
                    gbs = [mm_banks.next(), mm_banks.next()]
                    for cc in range(2):
                        mm_group(psf[gbs[cc]], [(wg[:, kc, cc * 128:(cc + 1) * 128], actT[:, kc, :]) for kc in range(16)], [Bwg, BactT], PS[gbs[cc]])
                    ws1.release()
                    for cc in range(2):
                        fc = 2 * i + cc
                        gb = gbs[cc]
                        ub = mm_banks.next()
                        mm_group(psf[ub], [(wu[:, kc, cc * 128:(cc + 1) * 128], actT[:, kc, :]) for kc in range(16)], [Bwu, BactT], PS[ub])
                        k2 = fc % 2
                        act_(lambda e, gb=gb, k2=k2: e.activation(out=sg[k2], in_=psf[gb], func=AF.Silu), reads=[PS[gb]], writes=[Bsg[k2]])
                        dve_(lambda e, ub=ub, k2=k2, fc=fc: e.tensor_tensor(out=hid[:, fc, :], in0=psf[ub], in1=sg[k2], op=ALU.mult),
                             reads=[PS[ub], Bsg[k2]], pwrites=[Bhid])
                    ws1.release()
                if t + 1 < NT:
                    tokn = slice((t + 1) * TT, (t + 2) * TT)
                    P.dma("sp", actT[:, 0:8, :], attnT_v[:, :, tokn], BactT, writes=[BactT])
                    P.dma("sp", actT[:, 8:16, :], ssmT_v[:, :, tokn], BactT, pwrites=[BactT])
                for n in range(4):
                    for kg in range(4):
                        wd, Bwd = ws2.get()
                        for sub in range(4):
                            bank = 4 + sub
                            P.deps("pe", reads=[Bwd, Bhid], writes=[PS[bank]] if kg == 0 else [])
                            for kk in range(11):
                                fcn = kg * 11 + kk
                                fn = (lambda e, bank=bank, fcn=fcn, kk=kk, sub=sub, wd=wd:
                                      e.matmul(psf[bank], hid[:, fcn, sub * 128:(sub + 1) * 128], wd[:, kk, :], start=(fcn == 0), stop=(fcn == 43)))
                                if kk == 10:
                                    P.commit("pe", fn, reads=[Bwd, Bhid], writes=[PS[bank]] if kg == 3 else [], pwrites=[PS[bank]] if kg < 3 else [])
                                else:
                                    P.emit("pe", fn)
                        ws2.release()
                    for sub in range(4):
                        act_(lambda e, sub=sub, n=n: e.activation(out=mix[:, sub, n * 512:(n + 1) * 512], in_=psf[4 + sub], func=AF.Copy),
                             reads=[PS[4 + sub]], pwrites=[Bmix[sub]])
                        act_(lambda e, sub=sub, n=n: e.activation(out=junk3, in_=psf[4 + sub], func=AF.Square, accum_out=ssacc[:, sub, 8 + n:9 + n]),
                             reads=[PS[4 + sub], Bssacc], writes=[Bjunk3], pwrites=[Bssacc])
                for sub in range(4):
                    r0 = t * TT + sub * 128
                    M = mix[:, sub, :]
                    xs, Bxs, xn, Bxn, ssq, Bss = xs2[sub % 2], Bxs2[sub % 2], xn2[sub % 2], Bxn2[sub % 2], ssq2[sub % 2], Bss2_[sub % 2]
                    P.dma("sp", xs, h1_s[r0:r0 + 128, :], Bxs, reads=[Bh1[sub]], writes=[Bxs])
                    dve_(lambda e, ssq=ssq, sub=sub: e.reduce_sum(out=ssq, in_=ssacc[:, sub, 8:12], axis=AX.X), reads=[Bssacc], writes=[Bss])
                    rms_rstd(ssq, ssq, float(D), Bss, Bss)
                    dve_(lambda e, M=M, ssq=ssq: e.scalar_tensor_tensor(out=M, in0=M, scalar=ssq, in1=gpo, op0=ALU.mult, op1=ALU.mult),
                         reads=[Bmix[sub], Bss, Bg], writes=[Bmix[sub]])
                    dve_(lambda e, M=M, xs=xs: e.tensor_tensor(out=M, in0=M, in1=xs, op=ALU.add), reads=[Bmix[sub], Bxs], writes=[Bmix[sub]])
                    P.dma("sp", out[r0:r0 + 128, :], M, Bmix[sub], reads=[Bmix[sub]])
            P.barrier()
            sb.reset(m)

        if int(os.environ.get("KPH", "9")) >= 4:
            phase3()

        outs_done = []

        P.barrier()

        sems = [es.enter_context(nc.semaphore(f"s{i}")) for i in range(len(P.cnt))]
        with nc.Block() as block:
            @block.tensor
            def _(e):
                P.replay("pe", e, sems)

            @block.scalar
            def _(e):
                P.replay("act", e, sems)

            @block.vector
            def _(e):
                P.replay("dve", e, sems)

            @block.gpsimd
            def _(e):
                P.replay("pool", e, sems)

            @block.sync
            def _(e):
                P.replay("sp", e, sems)
    return nc


def _consts():
    bf = ml_dtypes.bfloat16
    k = np.arange(128)
    ident = np.eye(128, dtype=np.float32).astype(bf)
    tri = (k[:, None] <= k[None, :]).astype(np.float32)
    negm = np.where(k[:, None] <= k[None, :], 0.0, -1e30).astype(np.float32)
    q = np.arange(512)
    cm = np.zeros((128, 4, 512), np.float32)
    for d in range(4):
        cm[:, d, :] = (q[None, :] >= d * 128 + k[:, None])
    cm = cm.reshape(128, 2048).astype(bf)
    inv_freq = (np.float32(10000.0) ** (-np.arange(0, 64, 2, dtype=np.float32) / np.float32(64))).astype(np.float32)
    rope = np.zeros((64, 2), np.float32)
    rope[:, 0] = np.concatenate([inv_freq, inv_freq])
    rope[:32, 1] = -1.0
    rope[32:, 1] = 1.0
    return dict(c_ident=ident, c_tri=tri, c_negm=negm, c_cmask=cm, c_rope=rope)


def make_in_maps(inputs, S, batches):
    f = lambda a: np.ascontiguousarray(np.asarray(a, dtype=np.float32))
    rep = lambda v, n=128: np.ascontiguousarray(np.broadcast_to(np.asarray(v, np.float32)[None, :], (n, len(v))))
    pp = lambda v: np.ascontiguousarray(np.asarray(v, np.float32).reshape(-1, 128).T)
    shared = dict(_consts())
    shared.update(
        w_in=f(inputs["w_in"][0]), w_uq=f(inputs["w_uq"][0]), w_ukv=f(inputs["w_ukv"][0]),
        w_out=f(inputs["w_out"][0]), w_gate=f(inputs["w_gate"][0]), w_up=f(inputs["w_up"][0]),
        w_down=f(inputs["w_down"][0]),
        g_pre=rep(inputs["pre_mix_norm_w"][0]), g_preffn=rep(inputs["pre_ffn_norm_w"][0]),
        g_postmix=rep(inputs["post_mix_norm_w"][0]), g_postffn=rep(inputs["post_ffn_norm_w"][0]),
        g_q=pp(inputs["q_norm_w"][0]), g_kv=pp(inputs["kv_norm_w"][0]), g_attn=pp(inputs["attn_out_norm_w"][0]),
        g_ssd=rep(inputs["ssd_norm_w"][0]),
    )
    cw = np.asarray(inputs["conv_w"][0], np.float32)
    cb = np.asarray(inputs["conv_b"][0], np.float32)
    cwb = np.concatenate([cw, cb[None, :]], axis=0)
    shared["conv_wb"] = np.ascontiguousarray(cwb.reshape(5, 12, 128).transpose(2, 1, 0).reshape(128, 60))
    small = np.concatenate([np.asarray(inputs["dt_bias"][0], np.float32), np.asarray(inputs["a_log"][0], np.float32),
                            np.asarray(inputs["d_skip"][0], np.float32)])
    shared["ssd_small"] = rep(small)
    maps = []
    for b in batches:
        m = dict(shared)
        m["x"] = f(inputs["x"][b][:S])
        pos = np.asarray(inputs["positions"][b][:S], np.int32)
        m["posb"] = np.ascontiguousarray(np.broadcast_to(pos[None, :], (64, S)))
        maps.append(m)
    return maps


def kernel(**inputs):
    S = inputs["x"].shape[1]
    B = inputs["x"].shape[0]
    nc = build_program(S)
    maps = make_in_maps(inputs, S, list(range(B)))
    res = run_bass_kernel_spmd(nc, maps, core_ids=list(range(B)))
    return np.stack([np.asarray(r["out"], dtype=np.float32) for r in res.results], axis=0)
```
